# Optimizing a Trainium2 kernel written in Bass

```python
import jax, jax.numpy as jnp
from jax import lax
import numpy as np

D_MODEL = 2048
BATCH = 2
SEQ = 4096
DEPTH = 1

RW_HEADS = 16
RW_HEAD = 64
RW_DIM = RW_HEADS * RW_HEAD
DECAY_LORA = 96
ICLR_LORA = 96
GATE_LORA = 256
GN_EPS = 64e-5
ATT_HEADS = 16
ATT_KV_HEADS = 4
ATT_GROUP = ATT_HEADS // ATT_KV_HEADS
ATT_HEAD = 64
ATT_Q_DIM = ATT_HEADS * ATT_HEAD
ATT_KV_DIM = ATT_KV_HEADS * ATT_HEAD
WINDOW = 128
BLOCK = 128
D_FF = 5632
CONV_W = 3
RMS_EPS = 1e-6
NEG_BIG = -1e30

RW_SHIFT_COLS = 3 * RW_DIM + DECAY_LORA + ICLR_LORA + GATE_LORA
ATT_COLS = ATT_Q_DIM + 2 * ATT_KV_DIM
GATE_COLS = 2 * D_MODEL
IN_COLS = RW_SHIFT_COLS + ATT_COLS + GATE_COLS

kernel_name = "hybrid_rwkv7_swa_gated_convffn"


def rms_norm(x, g, eps=RMS_EPS):
    xf = x.astype(jnp.float32)
    y = xf * lax.rsqrt(jnp.mean(xf * xf, axis=-1, keepdims=True) + eps)
    return (y * g.astype(jnp.float32)).astype(x.dtype)


def token_shift(z, mu):
    z_prev = jnp.pad(z, ((0, 0), (1, 0), (0, 0)))[:, :-1]
    return z + (z_prev - z) * mu


def rwkv7_mix(zs, w0, w2, a0, a2, g2, k_k, k_a, r_k, gn_w, gn_b):
    B, S, _ = zs.shape
    f32 = jnp.float32
    o1 = RW_DIM
    o2 = 2 * RW_DIM
    o3 = 3 * RW_DIM
    o4 = o3 + DECAY_LORA
    o5 = o4 + ICLR_LORA
    r, k, v, wd, ad, gd = jnp.split(zs, [o1, o2, o3, o4, o5], axis=-1)
    w = -jax.nn.softplus(-(w0 + jnp.tanh(wd) @ w2)) - 0.5
    decay = jnp.exp(-jnp.exp(w.astype(f32)))
    a = jax.nn.sigmoid(a0 + ad @ a2)
    g = jax.nn.sigmoid(gd) @ g2
    hd = (B, S, RW_HEADS, RW_HEAD)
    kk = (k * k_k).reshape(hd).astype(f32)
    kk = kk / jnp.maximum(jnp.sqrt(jnp.sum(kk * kk, axis=-1, keepdims=True)), 1e-12)
    k = k * (1.0 + (a - 1.0) * k_a)
    rh = r.reshape(hd).astype(f32)
    kh = k.reshape(hd).astype(f32)
    vh = v.reshape(hd).astype(f32)
    wh = decay.reshape(hd)
    ah = a.reshape(hd).astype(f32)
    tm = lambda t: jnp.moveaxis(t, 1, 0)
    xs = (tm(rh), tm(kh), tm(vh), tm(wh), tm(-kk), tm(kk * ah))

    def step(state, inp):
        r_t, k_t, v_t, w_t, ka_t, kb_t = inp
        sa = jnp.einsum('bhvk,bhk->bhv', state, ka_t)
        state = (state * w_t[:, :, None, :] + sa[..., None] * kb_t[:, :, None, :]
                 + v_t[..., None] * k_t[:, :, None, :])
        y = jnp.einsum('bhvk,bhk->bhv', state, r_t)
        return state, y

    state0 = jnp.zeros((B, RW_HEADS, RW_HEAD, RW_HEAD), f32)
    _, y = lax.scan(step, state0, xs)
    y = jnp.moveaxis(y, 0, 1)
    mu = jnp.mean(y, axis=-1, keepdims=True)
    var = jnp.mean(jnp.square(y - mu), axis=-1, keepdims=True)
    y = ((y - mu) * lax.rsqrt(var + GN_EPS)).reshape(B, S, RW_DIM)
    y = y * gn_w.astype(f32) + gn_b.astype(f32)
    bonus = jnp.sum(rh * kh * r_k.astype(f32), axis=-1, keepdims=True) * vh
    out = (y + bonus.reshape(B, S, RW_DIM)) * g.astype(f32)
    return out.astype(zs.dtype)


def swa_attention(q, k, v, q_gain, k_gain, sinks):
    B, S, _ = q.shape
    nb = S // BLOCK
    q = rms_norm(q.reshape(B, S, ATT_HEADS, ATT_HEAD), q_gain)
    k = rms_norm(k.reshape(B, S, ATT_KV_HEADS, ATT_HEAD), k_gain)
    v = v.reshape(B, S, ATT_KV_HEADS, ATT_HEAD)
    qb = q.reshape(B, nb, BLOCK, ATT_KV_HEADS, ATT_GROUP, ATT_HEAD)

    def band(t):
        tp = jnp.pad(t, ((0, 0), (BLOCK, 0), (0, 0), (0, 0)))[:, :S]
        prev = tp.reshape(B, nb, BLOCK, ATT_KV_HEADS, ATT_HEAD)
        cur = t.reshape(B, nb, BLOCK, ATT_KV_HEADS, ATT_HEAD)
        return jnp.concatenate([prev, cur], axis=2)

    kb, vb = band(k), band(v)
    scores = jnp.einsum('bnqhgd,bnkhd->bnhgqk', qb, kb).astype(jnp.float32) * (ATT_HEAD ** -0.5)
    qi = jnp.arange(BLOCK)[:, None]
    kj = jnp.arange(2 * BLOCK)[None, :]
    dist = (BLOCK + qi - kj).astype(jnp.float32)
    key_pos = jnp.arange(nb)[:, None] * BLOCK - BLOCK + jnp.arange(2 * BLOCK)[None, :]
    valid = ((dist >= 0) & (dist < WINDOW))[None] & (key_pos[:, None, :] >= 0)
    slopes = jnp.exp2(-8.0 * jnp.arange(1, ATT_HEADS + 1, dtype=jnp.float32) / ATT_HEADS)
    alibi = -slopes.reshape(ATT_KV_HEADS, ATT_GROUP)[:, :, None, None] * dist
    scores = jnp.where(valid[None, :, None, None], scores + alibi[None, None], NEG_BIG)
    sink = jnp.broadcast_to(sinks.astype(jnp.float32).reshape(1, 1, ATT_KV_HEADS, ATT_GROUP, 1, 1),
                            scores.shape[:-1] + (1,))
    p = jax.nn.softmax(jnp.concatenate([scores, sink], axis=-1), axis=-1)[..., :-1]
    o = jnp.einsum('bnhgqk,bnkhd->bnqhgd', p.astype(vb.dtype), vb)
    return o.reshape(B, S, ATT_Q_DIM)


def conv_ffn(h, w_up, conv_w, conv_b, w_down):
    u = h @ w_up
    S = u.shape[1]
    up = jnp.pad(u, ((0, 0), (CONV_W - 1, 0), (0, 0)))
    c = up[:, 0:S] * conv_w[0] + up[:, 1:S + 1] * conv_w[1] + up[:, 2:S + 2] * conv_w[2] + conv_b
    val, gate = jnp.split(c, 2, axis=-1)
    return (jax.nn.silu(gate) * val) @ w_down


def setup_inputs(seed: int = 0) -> dict:
    key = jax.random.key(seed)
    ks = jax.random.split(key, 26)
    f32 = jnp.float32
    L = DEPTH
    nrm = lambda k, shape, s: jax.random.normal(k, shape, f32) * s
    return {
        "x": nrm(ks[0], (BATCH, SEQ, D_MODEL), 1.0),
        "norm1_g": 1.0 + nrm(ks[1], (L, D_MODEL), 0.02),
        "w_in": nrm(ks[2], (L, D_MODEL, IN_COLS), D_MODEL ** -0.5),
        "rw_mu": jax.random.uniform(ks[3], (L, RW_SHIFT_COLS), f32),
        "rw_w0": jax.random.uniform(ks[4], (L, RW_DIM), f32, -6.0, -1.0),
        "rw_w2": nrm(ks[5], (L, DECAY_LORA, RW_DIM), 0.1 * DECAY_LORA ** -0.5),
        "rw_a0": nrm(ks[6], (L, RW_DIM), 0.5),
        "rw_a2": nrm(ks[7], (L, ICLR_LORA, RW_DIM), ICLR_LORA ** -0.5),
        "rw_g2": nrm(ks[8], (L, GATE_LORA, RW_DIM), GATE_LORA ** -0.5),
        "rw_k_k": 0.85 + nrm(ks[9], (L, RW_DIM), 0.02),
        "rw_k_a": 1.0 + nrm(ks[10], (L, RW_DIM), 0.02),
        "rw_r_k": nrm(ks[11], (L, RW_HEADS, RW_HEAD), 0.1),
        "rw_gn_w": 1.0 + nrm(ks[12], (L, RW_DIM), 0.02),
        "rw_gn_b": nrm(ks[13], (L, RW_DIM), 0.02),
        "q_norm_g": 1.0 + nrm(ks[14], (L, ATT_HEAD), 0.02),
        "k_norm_g": 1.0 + nrm(ks[15], (L, ATT_HEAD), 0.02),
        "attn_sinks": nrm(ks[16], (L, ATT_HEADS), 0.5),
        "w_branch": nrm(ks[17], (L, RW_DIM + ATT_Q_DIM, D_MODEL), RW_DIM ** -0.5),
        "w_out": nrm(ks[18], (L, D_MODEL, D_MODEL), D_MODEL ** -0.5),
        "norm2_g": 1.0 + nrm(ks[19], (L, D_MODEL), 0.02),
        "w_up": nrm(ks[20], (L, D_MODEL, 2 * D_FF), D_MODEL ** -0.5),
        "conv_w": nrm(ks[21], (L, CONV_W, 2 * D_FF), CONV_W ** -0.5),
        "conv_b": nrm(ks[22], (L, 2 * D_FF), 0.02),
        "w_down": nrm(ks[23], (L, D_FF, D_MODEL), D_FF ** -0.5),
    }


def reference(x, norm1_g, w_in, rw_mu, rw_w0, rw_w2, rw_a0, rw_a2, rw_g2, rw_k_k, rw_k_a,
              rw_r_k, rw_gn_w, rw_gn_b, q_norm_g, k_norm_g, attn_sinks, w_branch, w_out,
              norm2_g, w_up, conv_w, conv_b, w_down):
    for l in range(DEPTH):
        h = rms_norm(x, norm1_g[l])
        z = h @ w_in[l]
        zs = token_shift(z[..., :RW_SHIFT_COLS], rw_mu[l])
        za = z[..., RW_SHIFT_COLS:RW_SHIFT_COLS + ATT_COLS]
        zg = z[..., RW_SHIFT_COLS + ATT_COLS:]
        o_rw = rwkv7_mix(zs, rw_w0[l], rw_w2[l], rw_a0[l], rw_a2[l], rw_g2[l], rw_k_k[l],
                         rw_k_a[l], rw_r_k[l], rw_gn_w[l], rw_gn_b[l])
        q = za[..., :ATT_Q_DIM]
        k = za[..., ATT_Q_DIM:ATT_Q_DIM + ATT_KV_DIM]
        v = za[..., ATT_Q_DIM + ATT_KV_DIM:]
        o_att = swa_attention(q, k, v, q_norm_g[l], k_norm_g[l], attn_sinks[l])
        p_rw = o_rw @ w_branch[l, :RW_DIM]
        p_att = o_att @ w_branch[l, RW_DIM:]
        g_rw = jax.nn.sigmoid(zg[..., :D_MODEL])
        g_att = jax.nn.sigmoid(zg[..., D_MODEL:])
        x = x + (g_rw * p_rw + g_att * p_att) @ w_out[l]
        x = x + conv_ffn(rms_norm(x, norm2_g[l]), w_up[l], conv_w[l], conv_b[l], w_down[l])
    return x
```

```python
import os
import numpy as np
from contextlib import ExitStack
import concourse.bass as bass
import concourse.mybir as mybir
from concourse.bass_utils import run_bass_kernel_spmd

F32 = mybir.dt.float32
BF16 = mybir.dt.bfloat16
ALU = mybir.AluOpType
AF = mybir.ActivationFunctionType

ENGS = ["sp", "act", "pool", "pe", "dve"]


class Tile:
    def __init__(self, handle, name):
        self.h = handle
        self.name = name
        self.w = None
        self.rd = {}

    def __getitem__(self, idx):
        return View(self, self.h[idx])


class View:
    def __init__(self, tile, ap):
        self.tile = tile
        self.ap = ap


class DSem:
    def __init__(self, handle, idx):
        self.h = handle
        self.key = ("d", idx)
        self.count = 0


class Bank:
    def __init__(self, tile):
        self.t = tile
        self.fresh = [True, True]


class Prog:
    def __init__(self, nc, stack):
        self.nc = nc
        self.stack = stack
        self.rec = {e: [] for e in ENGS}
        self.clock = {e: {} for e in ENGS}
        self.nseq = {e: 0 for e in ENGS}
        self.esem = {e: stack.enter_context(nc.semaphore("es_" + e)) for e in ENGS}
        self.sembase = {e: 0 for e in ENGS}
        self.dsems = []
        self.snap = {}
        self.ntile = 0
        self.banks = []
        self.bank_rr = 0
        self.nblocks = 0
        self.tot = {e: [0, 0] for e in ENGS}

    def sb(self, shape, dt, name=None, stack=None, side=None):
        self.ntile += 1
        name = (name or "t") + f"_{self.ntile}"
        h = (stack or self.stack).enter_context(self.nc.sbuf_tensor(name, list(shape), dt, side=side))
        return Tile(h, name)

    def dram(self, handle, name):
        return Tile(handle, name)

    def dsem(self):
        h = self.stack.enter_context(self.nc.semaphore(f"ds{len(self.dsems)}"))
        d = DSem(h, len(self.dsems))
        self.dsems.append(d)
        return d

    def init_banks(self):
        for i in range(8):
            h = self.stack.enter_context(self.nc.psum_tensor(f"bank{i}", [128, 512], F32))
            self.banks.append(Bank(Tile(h, f"bank{i}")))

    def bank(self):
        b = self.banks[self.bank_rr % 8]
        self.bank_rr += 1
        b.fresh = [True, True]
        return b

    def _collect(self, eng, reads, writes, same_engine_ok=False):
        need = {}

        def add(tok):
            if tok is None:
                return
            k, v = tok
            if same_engine_ok and k == eng:
                return
            if need.get(k, 0) < v:
                need[k] = v

        for r in reads:
            add(r.tile.w)
        for w in writes:
            add(w.tile.w)
            for k, v in w.tile.rd.items():
                add((k, v))
        ck = self.clock[eng]
        waits = []
        for k, v in need.items():
            if ck.get(k, 0) >= v:
                continue
            waits.append((k, v))
            ck[k] = v
            sn = self.snap.get((k, v))
            if sn:
                for k2, v2 in sn.items():
                    if ck.get(k2, 0) < v2:
                        ck[k2] = v2
        return waits

    def op(self, eng, fn, reads=(), writes=()):
        waits = self._collect(eng, reads, writes, same_engine_ok=(eng == "pe"))
        self.nseq[eng] += 1
        s = self.nseq[eng]
        tok = (eng, s)
        self.snap[tok] = dict(self.clock[eng])
        for r in reads:
            t = r.tile
            if t.rd.get(eng, 0) < s:
                t.rd[eng] = s
        for w in writes:
            w.tile.w = tok
            w.tile.rd = {}
        self.rec[eng].append(("op", fn, waits, s))
        return tok

    def dma(self, eng, out, in_, dsem=None, **kw):
        if dsem is None:
            if not hasattr(out.tile, "dsem"):
                out.tile.dsem = self.dsem()
            dsem = out.tile.dsem
        waits = self._collect(eng, [in_], [out])
        dsem.count += 16
        tok = (dsem.key, dsem.count)
        self.snap[tok] = dict(self.clock[eng])
        t = in_.tile
        if t.rd.get(dsem.key, 0) < dsem.count:
            t.rd[dsem.key] = dsem.count
        out.tile.w = tok
        out.tile.rd = {}
        self.rec[eng].append(("dma", (out.ap, in_.ap, dsem, kw), waits, None))
        return tok

    def wait_all(self, eng, views):
        waits = self._collect(eng, views, [])
        self.rec[eng].append(("wait", None, waits, None))

    def emit(self):
        waited = {e: set() for e in ENGS}
        for e in ENGS:
            for kind, payload, waits, s in self.rec[e]:
                for k, v in waits:
                    if isinstance(k, str):
                        waited[k].add(v)
        semval = {}
        for e in ENGS:
            for i, s in enumerate(sorted(waited[e])):
                semval[(e, s)] = self.sembase[e] + i + 1
            self.sembase[e] += len(waited[e])
        dsem_by_key = {d.key: d for d in self.dsems}
        tot = self.tot

        def run(e, eobj):
            for kind, payload, waits, s in self.rec[e]:
                for k, v in waits:
                    if isinstance(k, str):
                        eobj.wait_ge(self.esem[k], semval[(k, v)])
                    else:
                        eobj.wait_ge(dsem_by_key[k].h, v)
                    tot[e][1] += 1
                if kind == "op":
                    ins = payload(eobj)
                    if (e, s) in semval:
                        ins.then_inc(self.esem[e], 1)
                    tot[e][0] += 1
                elif kind == "dma":
                    oap, iap, dsem, kw = payload
                    eobj.dma_start(out=oap, in_=iap, **kw).then_inc(dsem.h, 16)
                    tot[e][0] += 1

        self.nblocks += 1
        with self.nc.Block() as block:
            @block.sync
            def _(eng):
                run("sp", eng)

            @block.scalar
            def _(eng):
                run("act", eng)

            @block.gpsimd
            def _(eng):
                run("pool", eng)

            @block.tensor
            def _(eng):
                run("pe", eng)

            @block.vector
            def _(eng):
                run("dve", eng)
        self.rec = {e: [] for e in ENGS}
        for e in ENGS:
            for k in ENGS:
                self.clock[e][k] = self.nseq[k]

    def mm(self, bank, out, lhsT, rhs, half=None):
        if half is None:
            start = bank.fresh[0] or bank.fresh[1]
            bank.fresh = [False, False]
        else:
            start = bank.fresh[half]
            bank.fresh[half] = False
        o, l, r = out.ap, lhsT.ap, rhs.ap
        self.op("pe", lambda e: e.matmul(o, lhsT=l, rhs=r, start=start, stop=True, skip_group_check=True),
                reads=[lhsT, rhs], writes=[out])

    def act(self, out, in_, func, bias=None, scale=None):
        reads = [in_]
        kw = {}
        if bias is not None:
            if isinstance(bias, View):
                reads.append(bias)
                kw["bias"] = bias.ap
            else:
                kw["bias"] = float(bias)
        if scale is not None:
            if isinstance(scale, View):
                reads.append(scale)
                kw["scale"] = scale.ap
            else:
                kw["scale"] = float(scale)
        o, i = out.ap, in_.ap
        self.op("act", lambda e: e.activation(out=o, in_=i, func=func, **kw), reads=reads, writes=[out])

    def tt(self, eng, out, a, b, op):
        o, x, y = out.ap, a.ap, b.ap
        self.op(eng, lambda e: e.tensor_tensor(out=o, in0=x, in1=y, op=op), reads=[a, b], writes=[out])

    def ts(self, eng, out, a, s1, op0, s2=None, op1=None):
        reads = [a]
        v1 = s1
        if isinstance(s1, View):
            reads.append(s1)
            v1 = s1.ap
        v2 = s2
        if isinstance(s2, View):
            reads.append(s2)
            v2 = s2.ap
        o, x = out.ap, a.ap
        if op1 is None:
            self.op(eng, lambda e: e.tensor_scalar(out=o, in0=x, scalar1=v1, scalar2=None, op0=op0),
                    reads=reads, writes=[out])
        else:
            self.op(eng, lambda e: e.tensor_scalar(out=o, in0=x, scalar1=v1, scalar2=v2, op0=op0, op1=op1),
                    reads=reads, writes=[out])

    def stt(self, out, a, s, b, op0, op1):
        reads = [a, b]
        sv = s
        if isinstance(s, View):
            reads.append(s)
            sv = s.ap
        o, x, y = out.ap, a.ap, b.ap
        self.op("dve", lambda e: e.scalar_tensor_tensor(out=o, in0=x, scalar=sv, in1=y, op0=op0, op1=op1),
                reads=reads, writes=[out])

    def cp(self, eng, out, a):
        o, x = out.ap, a.ap
        if eng == "act":
            self.op("act", lambda e: e.activation(out=o, in_=x, func=AF.Copy), reads=[a], writes=[out])
        else:
            self.op(eng, lambda e: e.tensor_copy(out=o, in_=x), reads=[a], writes=[out])

    def memset(self, eng, out, val):
        o = out.ap
        self.op(eng, lambda e: e.memset(o, val), reads=[], writes=[out])


D = 2048
SEQ = 4096
NK = 16
RW = 1024
DFF = 5632
NFC = DFF // 128
CDEC = float(np.exp(-0.5))
E0 = 2944
NE = 1152
E1 = 2816
NE1 = 1280
HC0 = 126
NT3 = 342
C_R, C_K, C_V, C_WD, C_AD, C_GD = 0, 1024, 2048, 3072, 3168, 3264
C_AQ, C_AK, C_AV = 3520, 4544, 4800
C_GRW, C_GAT = 5056, 7104

PPI = {}
_o = 0
for _n, _c in [("n1g", 16), ("n2g", 16), ("mu_r", 8), ("mu_k", 8), ("mu_v", 8), ("mu_wd", 1), ("mu_ad", 1),
               ("mu_gd", 2), ("w0", 8), ("a0", 8), ("kk", 8), ("ka", 8), ("rk", 8), ("gnw", 8), ("gnb", 8),
               ("qg", 1), ("kg", 1), ("cw0", 88), ("cw1", 88), ("cw2", 88), ("cb", 88), ("sk", 8),
               ("prevmask", 1), ("convmask", 1)]:
    PPI[_n] = _o
    _o += _c
NPP = _o

CSI = {}
_o = 0
for _n, _c in [("ident", 128), ("i2", 64), ("m1", 256), ("bones", 128), ("rmask", 256), ("m13", 512), ("m22", 512),
               ("ones", 128)]:
    CSI[_n] = _o
    _o += _c
NCS = _o


def _chunked(v, ncol):
    return np.ascontiguousarray(np.asarray(v, np.float32).reshape(ncol, 128).T)


def host_consts():
    cs = np.zeros((128, NCS), np.float32)
    p = np.arange(128)[:, None]
    f = np.arange(128)[None, :]
    cs[:, CSI["ident"]:CSI["ident"] + 128] = (p == f)
    cs[:, CSI["i2"]:CSI["i2"] + 64] = ((p % 64) == np.arange(64)[None, :])
    m1 = (p < f).astype(np.float32)
    m2 = (p <= f).astype(np.float32)
    m3 = (f < p).astype(np.float32)
    cs[:, CSI["m1"]:CSI["m1"] + 256] = np.tile(m1, (1, 2))
    cs[:, CSI["m13"]:CSI["m13"] + 512] = np.concatenate([np.tile(m1, (1, 2)), np.tile(m3, (1, 2))], axis=1)
    cs[:, CSI["m22"]:CSI["m22"] + 512] = np.tile(m2, (1, 4))
    cs[:, CSI["bones"]:CSI["bones"] + 128] = ((p // 64) == (f // 64))
    rm = np.ones((128, 256), np.float32)
    rm[:, 0::128] = 0.0
    cs[:, CSI["rmask"]:CSI["rmask"] + 256] = rm
    cs[:, CSI["ones"]:CSI["ones"] + 128] = 1.0
    slopes = np.exp2(-8.0 * np.arange(1, 17, dtype=np.float32) / 16.0).astype(np.float32)
    ab = np.zeros((128, 4, 2, 4, 128), np.float32)
    k = np.arange(128)[:, None]
    q = np.arange(128)[None, :]
    for h in range(4):
        for g in range(4):
            sl = slopes[4 * h + g]
            dprev = (128 + q - k).astype(np.float32)
            dcur = (q - k).astype(np.float32)
            ab[:, h, 0, g, :] = np.where(dprev < 128, -sl * dprev, -30000.0)
            ab[:, h, 1, g, :] = np.where(dcur >= 0, -sl * dcur, -30000.0)
    return cs, ab.reshape(128, 4096)


def host_params(inp, j):
    pp = np.zeros((128, NPP), np.float32)

    def put(name, arr):
        arr = np.asarray(arr, np.float32)
        pp[:arr.shape[0], PPI[name]:PPI[name] + arr.shape[1]] = arr

    put("n1g", _chunked(inp["norm1_g"][0], 16))
    put("n2g", _chunked(inp["norm2_g"][0], 16))
    mu = np.asarray(inp["rw_mu"][0], np.float32)
    put("mu_r", _chunked(mu[0:1024], 8))
    put("mu_k", _chunked(mu[1024:2048], 8))
    put("mu_v", _chunked(mu[2048:3072], 8))
    put("mu_wd", mu[3072:3168].reshape(96, 1))
    put("mu_ad", mu[3168:3264].reshape(96, 1))
    put("mu_gd", _chunked(mu[3264:3520], 2))
    put("w0", _chunked(inp["rw_w0"][0], 8))
    put("a0", _chunked(inp["rw_a0"][0], 8))
    put("kk", _chunked(inp["rw_k_k"][0], 8))
    put("ka", _chunked(inp["rw_k_a"][0], 8))
    put("rk", _chunked(np.asarray(inp["rw_r_k"][0]).reshape(-1), 8))
    put("gnw", _chunked(inp["rw_gn_w"][0], 8))
    put("gnb", _chunked(inp["rw_gn_b"][0], 8))
    put("qg", np.tile(np.asarray(inp["q_norm_g"][0], np.float32), 2).reshape(128, 1))
    put("kg", np.tile(np.asarray(inp["k_norm_g"][0], np.float32), 2).reshape(128, 1))
    cw = np.asarray(inp["conv_w"][0], np.float32)
    put("cw0", _chunked(cw[0], 88))
    put("cw1", _chunked(cw[1], 88))
    put("cw2", _chunked(cw[2], 88))
    put("cb", _chunked(inp["conv_b"][0], 88))
    sk = np.asarray(inp["attn_sinks"][0], np.float32)
    put("sk", np.stack([sk[2 * c + (np.arange(128) // 64)] for c in range(8)], axis=1))
    put("prevmask", np.full((128, 1), 0.0 if j == 0 else 1.0, np.float32))
    put("convmask", np.full((128, 1), 0.0 if j == 0 else 1.0, np.float32))
    return pp


DBG = os.environ.get("KDBG", "")


class K:
    pass


def build_program(dbg=""):
    nc = bass.Bass("TRN2", target_bir_lowering=False)
    k = K()
    k.nc = nc
    dt_in = lambda name, shape: nc.dram_tensor(name, list(shape), F32, kind="ExternalInput")
    k.xT = dt_in("xT", [D, SEQ])
    k.w_in = dt_in("w_in", [D, 9152])
    small = dbg in ("A0", "A", "IO")
    k.w_branch = dt_in("w_branch", [D, D] if not small else [128, 128])
    k.w_out = dt_in("w_out", [D, D] if not small else [128, 128])
    k.w_up = dt_in("w_up", [D, 2 * DFF] if not small else [128, 128])
    k.w_down = dt_in("w_down", [DFF, D] if not small else [128, 128])
    k.w2 = dt_in("rw_w2", [96, RW])
    k.a2 = dt_in("rw_a2", [96, RW])
    k.g2 = dt_in("rw_g2", [256, RW])
    k.pp = dt_in("pp", [128, NPP])
    k.cst = dt_in("cst", [128, NCS])
    k.abias = dt_in("abias", [128, 4096])
    k.outT = nc.dram_tensor("outT", [D, 1024], F32, kind="ExternalOutput")
    k.dbg = dbg
    k.dbg_outs = {}
    with ExitStack() as st:
        P = Prog(nc, st)
        k.P = P
        P.init_banks()
        k.d = {n: P.dram(getattr(k, n), n) for n in
               ["xT", "w_in", "w_branch", "w_out", "w_up", "w_down", "w2", "a2", "g2", "pp", "cst", "abias", "outT"]}
        setup_persistent(k, st)
        R = lambda stack, shape, dt, name: P.sb(shape, dt, name, stack=stack, side="right")
        r_orw = ExitStack()
        k.o_rw = R(r_orw, [128, 8, NE], BF16, "o_rw")
        if dbg == "IO":
            dbg_out(k, "pp", k.ppt[:, :], [128, NPP])
        if dbg not in ("IO", "B"):
            for hp in range(2):
                with ExitStack() as ph:
                    phase_a(k, ph, hp)
                    P.emit()
                if dbg == "A0":
                    break
        if dbg not in ("A0", "A", "IO"):
            r_b = ExitStack()
            k.hE = R(r_b, [128, NK, NE1], BF16, "hE")
            k.o_att = R(r_b, [128, 8, NE], BF16, "o_att")
            r_b2 = ExitStack()
            k.qhat = R(r_b2, [128, 8, NE], BF16, "qhat")
            k.khat = R(r_b2, [128, 4, NE1], BF16, "khat")
            k.vtok = R(r_b2, [128, 10, 256], BF16, "vtok")
            with ExitStack() as ph:
                phase_b1(k, ph)
                P.emit()
            with ExitStack() as ph:
                phase_b2(k, ph)
                P.emit()
            r_b2.close()
            if dbg != "B":
                l_m = ExitStack()
                k.mT = P.sb([128, NK, 3 * NT3], BF16, "mT", stack=l_m)
                with ExitStack() as ph:
                    phase_c1(k, ph)
                    P.emit()
                r_b.close()
                r_orw.close()
                r_orw = None
                r_x1 = ExitStack()
                k.x1T = R(r_x1, [128, NK, 3 * NT3], F32, "x1T")
                with ExitStack() as ph:
                    phase_c2(k, ph)
                    P.emit()
                l_m.close()
                with ExitStack() as ph:
                    phase_d(k, ph)
                    P.emit()
                r_x1.close()
            else:
                r_b.close()
        if r_orw is not None:
            r_orw.close()
        finish(k)
        k.stats = P.tot
    return nc, k


def dbg_out(k, name, view, shape, dt=F32):
    P = k.P
    t = k.nc.dram_tensor("dbg_" + name, list(shape), dt, kind="ExternalOutput")
    k.dbg_outs[name] = t
    td = P.dram(t, "dbg_" + name)
    P.dma("sp", View(td, t.ap()), view)
    k.final_views.append(View(td, t.ap()))


def setup_persistent(k, st):
    P = k.P
    k.final_views = []
    k.ppt = P.sb([128, NPP], F32, "pp")
    k.cs = P.sb([128, NCS], F32, "cs")
    P.dma("sp", k.ppt[:, :], k.d["pp"][:, :])
    P.dma("sp", k.cs[:, :], k.d["cst"][:, :])
    k.identb = P.sb([128, 128], BF16, "identb")
    k.onesb = P.sb([128, 128], BF16, "onesb")
    k.bonesb = P.sb([128, 128], BF16, "bonesb")
    P.cp("dve", k.identb[:, :], k.cs[:, CSI["ident"]:CSI["ident"] + 128])
    P.cp("dve", k.onesb[:, :], k.cs[:, CSI["ones"]:CSI["ones"] + 128])
    P.cp("dve", k.bonesb[:, :], k.cs[:, CSI["bones"]:CSI["bones"] + 128])
    k.negb = P.sb([128, 16], F32, "negb")
    P.ts("dve", k.negb[:, :], k.ppt[:, PPI["w0"]:PPI["w0"] + 16], -1.0, ALU.mult)
    k.omu = P.sb([128, 28], F32, "omu")
    P.ts("dve", k.omu[:, :], k.ppt[:, PPI["mu_r"]:PPI["mu_r"] + 28], -1.0, ALU.mult, 1.0, ALU.add)


def pcol(k, name, c=0, rows=128):
    o = PPI[name] + c
    return k.ppt[0:rows, o:o + 1]


def omucol(k, name, c=0, rows=128):
    o = PPI[name] - PPI["mu_r"] + c
    return k.omu[0:rows, o:o + 1]


def cview(k, name, n):
    return k.cs[:, CSI[name]:CSI[name] + n]


def rsqrt_act(P, out, in_, scale, bias, tmp):
    P.act(tmp, in_, AF.Ln, bias=bias, scale=scale)
    P.act(out, tmp, AF.Exp, scale=-0.5)


def rmsnorm_block(k, ph_tiles, xb, hT, nt, gname):
    P = k.P
    sqb, rstd, tmp = ph_tiles
    P.act(sqb[:, :, 0:nt], xb[:, :, 0:nt], AF.Square)
    bk = P.bank()
    for kc in range(NK):
        P.mm(bk, bk.t[:, 0:nt], k.onesb[:, :], sqb[:, kc, 0:nt])
    rsqrt_act(P, rstd[:, 0:nt], bk.t[:, 0:nt], 1.0 / D, 1e-6, tmp[:, 0:nt])
    for kc in range(NK):
        P.stt(hT[:, kc, 0:nt], xb[:, kc, 0:nt], pcol(k, gname, kc), rstd[:, 0:nt], ALU.mult, ALU.mult)


def load_w(k, dst_view, src_handle, r0, nk, c0, ncol):
    P = k.P
    src = src_handle.ap().rearrange("(k p) c -> p k c", p=128)[:, r0:r0 + nk, c0:c0 + ncol]
    name = [n for n, t in k.d.items() if t.h is src_handle][0]
    P.dma("pool", dst_view, View(k.d[name], src))


class PA:
    pass


NTA = 256
NBA = SEQ // NTA
BI_R = 10
BI_O = 11


def proj(k, bank, W, col0, ncols, hT, nt):
    P = k.P
    for kc in range(NK):
        P.mm(bank, bank.t[0:ncols, 0:nt], W[:, kc, col0:col0 + ncols], hT[:, kc, 0:nt])


def proj_a(k, bank, W, col0, ncols, hTk, nt):
    P = k.P
    for kc in range(NK):
        P.mm(bank, bank.t[0:ncols, 0:nt], W[:, kc, col0:col0 + ncols], hTk[kc][:, 0:nt])


def shift(k, A, zps, rows, nt, mu, omu, carcol, out):
    P = k.P
    tmp = A.shtmp[A.shi % 3]
    A.shi += 1
    P.act(tmp[0:rows, 1:nt + 1], zps[0:rows, 0:nt], AF.Identity, scale=mu)
    P.cp("pool", tmp[0:rows, 0:1], A.car[0:rows, carcol:carcol + 1])
    P.cp("pool", A.car[0:rows, carcol:carcol + 1], tmp[0:rows, nt:nt + 1])
    P.stt(out, zps[0:rows, 0:nt], omu, tmp[0:rows, 0:nt], ALU.mult, ALU.add)


def phase_a(k, ph, hp):
    P = k.P
    A = PA()
    A.hp = hp
    nt = NTA
    S = lambda shape, dt, name: P.sb(shape, dt, name, stack=ph)
    A.W = S([128, NK, 1984], BF16, "WA")
    ch0 = 512 * hp
    load_w(k, A.W[:, :, 0:512], k.w_in, 0, NK, C_K + ch0, 512)
    load_w(k, A.W[:, :, 512:1024], k.w_in, 0, NK, C_V + ch0, 512)
    load_w(k, A.W[:, :, 1536:1984], k.w_in, 0, NK, C_WD, 448)
    load_w(k, A.W[:, :, 1024:1536], k.w_in, 0, NK, C_R + ch0, 512)
    A.w2b = S([128, 512], BF16, "w2b")
    A.a2b = S([128, 512], BF16, "a2b")
    A.g2b = S([128, 2, 512], BF16, "g2b")
    P.dma("pool", A.w2b[0:96, :], k.d["w2"][:, ch0:ch0 + 512])
    P.dma("pool", A.a2b[0:96, :], k.d["a2"][:, ch0:ch0 + 512])
    P.dma("pool", A.g2b[:, :, :], View(k.d["g2"], k.g2.ap().rearrange("(k p) c -> p k c", p=128)[:, :, ch0:ch0 + 512]))
    A.car = S([128, 16], F32, "car")
    P.memset("pool", A.car[:, :], 0.0)
    A.H = [S([128, 64], F32, f"H{i}") for i in range(8)]
    A.Hbf = [S([128, 64], BF16, f"Hbf{i}") for i in range(8)]
    for i in range(8):
        P.memset("pool", A.H[i][:, :], 0.0)
        P.memset("pool", A.Hbf[i][:, :], 0.0)
    A.xb = S([128, NK, nt], F32, "xb")
    A.sqr = [S([128, nt], BF16, f"sqr{i}") for i in range(4)]
    A.hTk = [S([128, nt], BF16, f"hT{i}") for i in range(NK)]
    A.xg = [S([128, nt], F32, f"xg{i}") for i in range(2)]
    A.rstd = S([128, nt], F32, "rstd")
    A.tmpn = S([128, nt], F32, "tmpn")
    A.shtmp = [S([128, nt + 1], F32, f"shtmp{i}") for i in range(3)]
    A.shi = 0
    A.scr = S([128, nt], F32, "scr")
    A.th_wd = S([128, nt], BF16, "th_wd")
    A.ad_s = S([128, nt], BF16, "ad_s")
    A.sg_gd = [S([128, 2, nt], BF16, f"sg_gd{i}") for i in range(2)]
    alias32 = {"ks": 0, "vs": 1, "rs": 2, "sg": 3, "cm": 3, "Em": 3, "aa": 4, "f": 4, "kp": 4, "cs": 5, "En": 6,
               "Ep": 7, "ssc": 8, "ln": 8, "rn": 8, "kkn": 9, "kba": 10, "bonus": 11, "yT": 12,
               "dd": 0, "sqd": 1, "rs2": 2, "yn": 3, "t1": 5, "t2": 6}
    alias16 = {"sq": 0, "rkb": 0, "At": 1, "Bt": 2, "Kt": 3, "Bh": 4, "Kh": 5, "Vb": 6, "Rt": 7}
    A.T = []
    for par in range(3):
        T = PA()
        b32 = [S([128, nt], F32, f"b32_{par}_{i}") for i in range(13)]
        b16 = [S([128, nt], BF16, f"b16_{par}_{i}") for i in range(8)]
        for n, i in alias32.items():
            setattr(T, n, b32[i])
        for n, i in alias16.items():
            setattr(T, n, b16[i])
        T.TK = [S([128, 384], BF16, f"TK{par}_{i}") for i in range(2)]
        T.EpL = S([128, 2], F32, f"EpL{par}")
        T.dgW = S([128, 128], F32, f"dgW{par}")
        A.T.append(T)
    A.Q = []
    for slot in range(2):
        Q = PA()
        Q.XX = [S([128, 512], BF16, f"XX{slot}_{i}") for i in range(2)]
        Q.Z = [S([128, 256], BF16, f"Z{slot}_{i}") for i in range(2)]
        Q.AAK = S([128, 256], BF16, f"AAK{slot}")
        Q.ARBK = S([128, 512], BF16, f"ARBK{slot}")
        Q.PG = S([128, 256], F32, f"PG{slot}")
        Q.QT = S([128, 256], BF16, f"QT{slot}")
        P.memset("pool", Q.PG[:, :], 0.0)
        P.memset("pool", Q.QT[:, :], 0.0)
        Q.acc = P.banks[slot]
        A.Q.append(Q)
    P.bank_rr = 0
    _orig_bank = P.bank

    def ring_bank():
        b = P.banks[[2, 3, 4, 5, 7][P.bank_rr % 5]]
        P.bank_rr += 1
        b.fresh = [True, True]
        return b
    P.bank = ring_bank

    evi = [0]

    def evac(out, in_):
        evi[0] += 1
        P.cp("act" if evi[0] % 3 else "dve", out, in_)

    def recip(out, in_):
        o, i = out.ap, in_.ap
        P.op("dve", lambda e: e.reciprocal(out=o, in_=i), reads=[in_], writes=[out])

    def block_head(bi):
        t0 = nt * bi
        src = k.xT.ap().rearrange("(k p) t -> p k t", p=128)[:, :, t0:t0 + nt]
        P.dma("sp", A.xb[:, :, :], View(k.d["xT"], src))
        bk = P.banks[6]
        bk.fresh = [True, True]
        for kc in range(NK):
            sq = A.sqr[kc % 4]
            if kc % 2 == 0:
                P.act(sq[:, :], A.xb[:, kc, :], AF.Square)
            else:
                P.tt("dve", sq[:, :], A.xb[:, kc, :], A.xb[:, kc, :], ALU.mult)
            P.mm(bk, bk.t[:, 0:nt], k.onesb[:, :], sq[:, :])
            if kc % 4 == 3:
                yield
        rsqrt_act(P, A.rstd[:, :], bk.t[:, 0:nt], 1.0 / D, 1e-6, A.tmpn[:, :])
        for kc in range(NK):
            if kc % 2 == 0:
                P.stt(A.hTk[kc][:, :], A.xb[:, kc, :], pcol(k, "n1g", kc), A.rstd[:, :], ALU.mult, ALU.mult)
            else:
                xg = A.xg[(kc // 2) % 2]
                P.act(xg[:, :], A.xb[:, kc, :], AF.Identity, scale=pcol(k, "n1g", kc))
                P.tt("pool", A.hTk[kc][:, :], xg[:, :], A.rstd[:, :], ALU.mult)
            if kc % 4 == 3:
                yield
        bk = P.bank()
        proj_a(k, bk, A.W, 1536, 96, A.hTk, nt)
        shift(k, A, bk.t, 96, nt, pcol(k, "mu_wd", 0, 96), omucol(k, "mu_wd", 0, 96), 12, A.scr[0:96, :])
        P.act(A.th_wd[0:96, :], A.scr[0:96, :], AF.Tanh)
        yield
        bk = P.bank()
        proj_a(k, bk, A.W, 1632, 96, A.hTk, nt)
        shift(k, A, bk.t, 96, nt, pcol(k, "mu_ad", 0, 96), omucol(k, "mu_ad", 0, 96), 13, A.ad_s[0:96, :])
        yield
        if bi >= BI_R:
            for j in range(2):
                bk = P.bank()
                proj_a(k, bk, A.W, 1728 + 128 * j, 128, A.hTk, nt)
                shift(k, A, bk.t, 128, nt, pcol(k, "mu_gd", j), omucol(k, "mu_gd", j), 14 + j, A.scr[:, :])
                P.act(A.sg_gd[bi % 2][:, j, :], A.scr[:, :], AF.Sigmoid)
                yield

    def prep(bi, lc):
        cc = 4 * hp + lc
        needR = bi >= BI_R
        outp = bi >= BI_O
        T = A.T[(bi * 4 + lc) % 3]
        bk = P.bank()
        proj_a(k, bk, A.W, lc * 128, 128, A.hTk, nt)
        shift(k, A, bk.t, 128, nt, pcol(k, "mu_k", cc), omucol(k, "mu_k", cc), lc, T.ks[:, :])
        yield
        bk = P.bank()
        proj_a(k, bk, A.W, 512 + lc * 128, 128, A.hTk, nt)
        shift(k, A, bk.t, 128, nt, pcol(k, "mu_v", cc), omucol(k, "mu_v", cc), 4 + lc, T.vs[:, :])
        yield
        if needR:
            bk = P.bank()
            proj_a(k, bk, A.W, 1024 + lc * 128, 128, A.hTk, nt)
            shift(k, A, bk.t, 128, nt, pcol(k, "mu_r", cc), omucol(k, "mu_r", cc), 8 + lc, T.rs[:, :])
            yield
        bw = P.bank()
        P.mm(bw, bw.t[:, 0:nt], A.w2b[0:96, lc * 128:(lc + 1) * 128], A.th_wd[0:96, :])
        P.act(T.sg[:, :], bw.t[:, 0:nt], AF.Sigmoid, bias=pcol(k, "w0", cc))
        ba = P.bank()
        P.mm(ba, ba.t[:, 0:nt], A.a2b[0:96, lc * 128:(lc + 1) * 128], A.ad_s[0:96, :])
        P.act(T.aa[:, :], ba.t[:, 0:nt], AF.Sigmoid, bias=pcol(k, "a0", cc))
        yield
        rm = cview(k, "rmask", nt)
        cs_ap, rm_ap, sg_ap = T.cs[:, :].ap, rm.ap, T.sg[:, :].ap
        P.op("dve", lambda e: e.tensor_tensor_scan(out=cs_ap, data0=rm_ap, data1=sg_ap, initial=0.0,
                                                   op0=ALU.mult, op1=ALU.add),
             reads=[rm, T.sg[:, :]], writes=[T.cs[:, :]])
        P.tt("pool", T.cm[:, :], T.cs[:, :], T.sg[:, :], ALU.subtract)
        P.act(T.En[:, :], T.cs[:, :], AF.Exp, scale=CDEC)
        P.act(T.Em[:, :], T.cm[:, :], AF.Exp, scale=-CDEC)
        for c in range(2):
            P.act(T.EpL[:, c:c + 1], T.cs[:, c * 128 + 127:c * 128 + 128], AF.Exp, scale=-CDEC)
        if outp:
            P.act(T.Ep[:, :], T.cs[:, :], AF.Exp, scale=-CDEC)
        yield
        P.act(T.sq[:, :], T.ks[:, :], AF.Square, scale=pcol(k, "kk", cc))
        bs = P.bank()
        P.mm(bs, bs.t[:, 0:nt], k.bonesb[:, :], T.sq[:, :])
        P.ts("dve", T.ssc[:, :], bs.t[:, 0:nt], 1e-24, ALU.max)
        P.act(T.ln[:, :], T.ssc[:, :], AF.Ln, scale=float(2.0 ** 40))
        P.act(T.rn[:, :], T.ln[:, :], AF.Exp, scale=-0.5)
        yield
        P.stt(T.kkn[:, :], T.ks[:, :], pcol(k, "kk", cc), T.rn[:, :], ALU.mult, ALU.mult)
        P.stt(T.At[:, :], T.kkn[:, :], -float(2.0 ** 20), T.Em[:, :], ALU.mult, ALU.mult)
        P.stt(T.kba[:, :], T.kkn[:, :], float(2.0 ** 20), T.aa[:, :], ALU.mult, ALU.mult)
        yield
        P.ts("dve", T.f[:, :], T.aa[:, :], -1.0, ALU.add, pcol(k, "ka", cc), ALU.mult)
        P.stt(T.kp[:, :], T.f[:, :], 1.0, T.ks[:, :], ALU.add, ALU.mult)
        P.tt("pool", T.Bt[:, :], T.kba[:, :], T.En[:, :], ALU.mult)
        P.tt("pool", T.Kt[:, :], T.kp[:, :], T.En[:, :], ALU.mult)
        P.cp("pool", T.Vb[:, :], T.vs[:, :])
        yield
        for c in range(2):
            cs_ = slice(c * 128, (c + 1) * 128)
            P.stt(T.Bh[:, cs_], T.kba[:, cs_], T.EpL[:, c:c + 1], T.En[:, cs_], ALU.mult, ALU.mult)
            P.stt(T.Kh[:, cs_], T.kp[:, cs_], T.EpL[:, c:c + 1], T.En[:, cs_], ALU.mult, ALU.mult)
            P.ts("dve", T.dgW[:, c * 64:(c + 1) * 64], cview(k, "i2", 64), T.EpL[:, c:c + 1], ALU.mult)
        yield
        if outp:
            P.tt("pool", T.Rt[:, :], T.rs[:, :], T.Ep[:, :], ALU.mult)
            P.stt(T.rkb[:, :], T.rs[:, :], pcol(k, "rk", cc), T.kp[:, :], ALU.mult, ALU.mult)
            bb = P.bank()
            P.mm(bb, bb.t[:, 0:nt], k.bonesb[:, :], T.rkb[:, :])
            P.tt("dve", T.bonus[:, :], bb.t[:, 0:nt], T.vs[:, :], ALU.mult)
            yield

    def hquad(bi, lc, h, slot):
        T = A.T[(bi * 4 + lc) % 3]
        Q = A.Q[slot]
        outp = bi >= BI_O
        m1 = cview(k, "m1", 256)
        hs = slice(64 * h, 64 * h + 64)
        hd = 2 * lc + h

        def cs_(ci):
            return slice(ci * 128, (ci + 1) * 128)
        m13 = cview(k, "m13", 512)
        m22 = cview(k, "m22", 512)
        g = P.bank()
        for ci in range(2):
            P.mm(g, g.t[:, cs_(ci)], T.Bt[hs, cs_(ci)], T.At[hs, cs_(ci)])
        for ci in range(2):
            P.mm(g, g.t[:, 256 + ci * 128:256 + (ci + 1) * 128], T.At[hs, cs_(ci)], T.Bt[hs, cs_(ci)])
        P.tt("dve", Q.XX[0][:, :], g.t[:, :], m13, ALU.mult)
        g = P.bank()
        for ci in range(2):
            P.mm(g, g.t[:, cs_(ci)], T.Kt[hs, cs_(ci)], T.At[hs, cs_(ci)])
        P.tt("dve", Q.AAK[:, :], g.t[:, 0:256], m1, ALU.mult)
        if outp:
            g = P.bank()
            for ci in range(2):
                P.mm(g, g.t[:, cs_(ci)], T.Bt[hs, cs_(ci)], T.Rt[hs, cs_(ci)])
            for ci in range(2):
                P.mm(g, g.t[:, 256 + ci * 128:256 + (ci + 1) * 128], T.Kt[hs, cs_(ci)], T.Rt[hs, cs_(ci)])
            P.tt("dve", Q.ARBK[:, :], g.t[:, :], m22, ALU.mult)
        if h == 0:
            for ci in range(2):
                g = P.bank()
                P.mm(g, g.t[:, 0:128], T.Vb[:, cs_(ci)], k.identb[:, :])
                P.mm(g, g.t[:, 128:256], T.Bh[:, cs_(ci)], k.identb[:, :])
                P.mm(g, g.t[:, 256:384], T.Kh[:, cs_(ci)], k.identb[:, :])
                evac(T.TK[ci][:, :], g.t[:, 0:384])
        yield
        acc = Q.acc
        acc.fresh = [True, True]
        for ci in range(2):
            P.mm(acc, acc.t[:, ci * 128:ci * 128 + 64], T.At[:, cs_(ci)], k.identb[:, hs])
            P.mm(acc, acc.t[:, ci * 128 + 64:ci * 128 + 128], Q.AAK[:, cs_(ci)], T.TK[ci][:, 64 * h:64 * h + 64])
        evac(Q.Z[0][:, :], acc.t[:, 0:256])
        yield
        zi = 0
        for i in range(7):
            XX = Q.XX[i % 2]
            for ci in range(2):
                P.mm(acc, acc.t[:, cs_(ci)], XX[:, cs_(ci)], Q.Z[zi][:, cs_(ci)])
            zi ^= 1
            evac(Q.Z[zi][:, :], acc.t[:, 0:256])
            if i < 6:
                g = P.bank()
                for ci in range(2):
                    P.mm(g, g.t[:, cs_(ci)], XX[:, 256 + ci * 128:256 + (ci + 1) * 128], XX[:, cs_(ci)])
                if i < 5:
                    for ci in range(2):
                        P.mm(g, g.t[:, 256 + ci * 128:256 + (ci + 1) * 128], XX[:, cs_(ci)],
                             XX[:, 256 + ci * 128:256 + (ci + 1) * 128])
                    evac(Q.XX[(i + 1) % 2][:, :], g.t[:, :])
                else:
                    evac(Q.XX[(i + 1) % 2][:, 0:256], g.t[:, 0:256])
            yield
        Z = Q.Z[zi]
        g = P.bank()
        ic = CSI["ident"]
        for ci in range(2):
            P.mm(g, g.t[0:64, ci * 128:ci * 128 + 64], Z[:, ci * 128:ci * 128 + 64],
                 T.TK[ci][:, 128 + 64 * h:128 + 64 * h + 64], half=0)
            P.mm(g, g.t[0:64, ci * 128:ci * 128 + 64], k.cs[:, ic + 64 * h:ic + 64 * h + 64],
                 T.dgW[:, ci * 64:(ci + 1) * 64], half=0)
            P.mm(g, g.t[0:64, ci * 128 + 64:ci * 128 + 128], T.TK[ci][:, 128 + 64 * h:128 + 64 * h + 64],
                 Z[:, ci * 128 + 64:ci * 128 + 128], half=0)
            P.mm(g, g.t[0:64, ci * 128 + 64:ci * 128 + 128], T.TK[ci][:, 256 + 64 * h:256 + 64 * h + 64],
                 T.TK[ci][:, 64 * h:64 * h + 64], half=0)
        evac(Q.PG[0:64, :], g.t[0:64, 0:256])
        if outp:
            gq = P.bank()
            for ci in range(2):
                P.mm(gq, gq.t[0:64, cs_(ci)], Z[:, ci * 128:ci * 128 + 64], Q.ARBK[:, cs_(ci)], half=0)
                P.mm(gq, gq.t[0:64, cs_(ci)], k.identb[:, hs], T.Rt[:, cs_(ci)], half=0)
            evac(Q.QT[0:64, :], gq.t[0:64, 0:256])
            yield
            gy = P.bank()
        for ci in range(2):
            if outp:
                P.cp("pool", A.Hbf[hd][0:64, :], A.H[hd][0:64, :])
                P.mm(gy, gy.t[hs, cs_(ci)], Z[:, ci * 128 + 64:ci * 128 + 128], Q.ARBK[:, cs_(ci)], half=h)
                P.mm(gy, gy.t[hs, cs_(ci)], T.TK[ci][:, 64 * h:64 * h + 64], Q.ARBK[:, 256 + ci * 128:256 + (ci + 1) * 128], half=h)
                P.mm(gy, gy.t[hs, cs_(ci)], A.Hbf[hd][:, :], Q.QT[:, cs_(ci)], half=h)
            gs = P.bank()
            P.mm(gs, gs.t[0:64, 0:64], Q.PG[:, ci * 128:ci * 128 + 64], A.H[hd][:, :], half=0)
            P.tt("dve", A.H[hd][0:64, :], gs.t[0:64, 0:64], Q.PG[0:64, ci * 128 + 64:ci * 128 + 128], ALU.add)
        if outp:
            evac(T.yT[hs, :], gy.t[hs, 0:256])
        yield

    def post(bi, lc):
        cc = 4 * hp + lc
        T = A.T[(bi * 4 + lc) % 3]
        bonesf = cview(k, "bones", 128)
        bm = P.bank()
        P.mm(bm, bm.t[:, 0:nt], bonesf, T.yT[:, :])
        P.stt(T.dd[:, :], bm.t[:, 0:nt], -1.0 / 64, T.yT[:, :], ALU.mult, ALU.add)
        P.act(T.sqd[:, :], T.dd[:, :], AF.Square)
        yield
        bv = P.bank()
        P.mm(bv, bv.t[:, 0:nt], bonesf, T.sqd[:, :])
        rsqrt_act(P, T.rs2[:, :], bv.t[:, 0:nt], 1.0 / 64, 64e-5, T.ln[:, :])
        P.tt("dve", T.yn[:, :], T.dd[:, :], T.rs2[:, :], ALU.mult)
        P.ts("dve", T.t1[:, :], T.yn[:, :], pcol(k, "gnw", cc), ALU.mult, pcol(k, "gnb", cc), ALU.add)
        P.tt("pool", T.t2[:, :], T.t1[:, :], T.bonus[:, :], ALU.add)
        yield
        bg = P.bank()
        for j in range(2):
            P.mm(bg, bg.t[:, 0:nt], A.g2b[:, j, lc * 128:(lc + 1) * 128], A.sg_gd[bi % 2][:, j, :])
        if bi == BI_O:
            P.tt("dve", k.o_rw[:, cc, 0:128], T.t2[:, 128:256], bg.t[:, 128:256], ALU.mult)
        else:
            e0 = nt * bi - E0
            P.tt("dve", k.o_rw[:, cc, e0:e0 + nt], T.t2[:, :], bg.t[:, 0:nt], ALU.mult)
        yield

    ksteps = int(os.environ.get("KSTEPS", "1000000000"))
    stepc = [0]

    warm_bank = P.banks[7]
    nwarm = int(os.environ.get("KWARM", "0"))

    def keep_warm():
        o, l, r = warm_bank.t[:, 0:512].ap, k.identb[:, :].ap, A.W[:, 0, 0:512].ap
        for _ in range(nwarm):
            P.op("pe", lambda e: e.matmul(o, lhsT=l, rhs=r, start=True, stop=True, skip_group_check=True),
                 reads=[], writes=[])

    def run_all(gens):
        gens = list(gens)
        while gens:
            if len(gens) > 1:
                keep_warm()
            nxt = []
            for g_ in gens:
                if stepc[0] >= ksteps:
                    return
                stepc[0] += 1
                try:
                    next(g_)
                    nxt.append(g_)
                except StopIteration:
                    pass
            gens = nxt

    nblk = int(os.environ.get("KNBLK", str(NBA))) if k.dbg else NBA
    work = [(bi, lc) for bi in range(nblk) for lc in range(4)]

    def prep_full(w):
        bi, lc = w
        if lc == 0:
            yield from block_head(bi)
        yield from prep(bi, lc)

    run_all([prep_full(work[0])])
    if len(work) > 1:
        run_all([prep_full(work[1])])
    for wi, (bi, lc) in enumerate(work):
        gens = [hquad(bi, lc, 0, 0), hquad(bi, lc, 1, 1)]
        if wi + 2 < len(work):
            gens.append(prep_full(work[wi + 2]))
        run_all(gens)
        if bi >= BI_O:
            run_all([post(bi, lc)])
    P.bank = _orig_bank
    if k.dbg in ("A0", "A"):
        for i in range(8):
            dbg_out(k, f"H{hp}_{i}", A.H[i][0:64, :], [64, 64])
        if nblk > BI_O + 1:
            dbg_out(k, f"orw{hp}", k.o_rw[:, 4 * hp:4 * hp + 4, :], [128, 4, NE], BF16)
        dbg_out(k, f"ks{hp}", A.T[1].ks[:, :], [128, nt])


def rms_block_256(k, xb, sqr, rstd, tmpn, hT_out, t0, gname):
    P = k.P
    nt = 256
    src = k.xT.ap().rearrange("(k p) t -> p k t", p=128)[:, :, t0:t0 + nt]
    P.dma("sp", xb[:, :, :], View(k.d["xT"], src))
    bk = P.bank()
    for kc in range(NK):
        sq = sqr[kc % 2]
        P.act(sq[:, :], xb[:, kc, :], AF.Square)
        P.mm(bk, bk.t[:, 0:nt], k.onesb[:, :], sq[:, :])
    rsqrt_act(P, rstd[:, :], bk.t[:, 0:nt], 1.0 / D, 1e-6, tmpn[:, :])
    for kc in range(NK):
        P.stt(hT_out(kc), xb[:, kc, :], pcol(k, gname, kc), rstd[:, :], ALU.mult, ALU.mult)


def phase_b1(k, ph):
    P = k.P
    S = lambda shape, dt, name: P.sb(shape, dt, name, stack=ph)
    nt = 256
    bufA = S([128, NK, 512], BF16, "wbufA")
    bufB = S([128, NK, 512], BF16, "wbufB")
    xb = S([128, NK, nt], F32, "xbB")
    sqr = [S([128, nt], BF16, f"sqrB{i}") for i in range(2)]
    rstd = S([128, nt], F32, "rstdB")
    tmpn = S([128, nt], F32, "tmpnB")
    sqq = [S([128, nt], BF16, f"sqq{i}") for i in range(2)]
    rq = [S([128, nt], F32, f"rq{i}") for i in range(2)]
    lq = [S([128, nt], F32, f"lq{i}") for i in range(2)]
    load_w(k, bufA[:, :, 0:256], k.w_in, 0, NK, C_AK, 256)
    for bi in range(5):
        rms_block_256(k, xb, sqr, rstd, tmpn, lambda kc: k.hE[:, kc, bi * nt:(bi + 1) * nt], E1 + nt * bi, "n1g")
    for h in range(4):
        for dup in range(2):
            P.cp("pool" if dup else "act", bufB[:, :, h * 128 + 64 * dup:h * 128 + 64 * dup + 64],
                 bufA[:, :, h * 64:(h + 1) * 64])
    ii = [0]

    def qknorm(bank, out_view, gname, c0, c1):
        i2 = ii[0] % 2
        ii[0] += 1
        P.act(sqq[i2][:, :], bank.t[:, 0:nt], AF.Square)
        bs = P.bank()
        P.mm(bs, bs.t[:, 0:nt], k.bonesb[:, :], sqq[i2][:, :])
        rsqrt_act(P, rq[i2][:, :], bs.t[:, 0:nt], 1.0 / 64, 1e-6, lq[i2][:, :])
        P.stt(out_view, bank.t[:, c0:c1], pcol(k, gname), rq[i2][:, c0:c1], ALU.mult, ALU.mult)

    for bi in range(5):
        for h in range(4):
            bk = P.bank()
            for kc in range(NK):
                P.mm(bk, bk.t[:, 0:nt], bufB[:, kc, h * 128:(h + 1) * 128], k.hE[:, kc, bi * nt:(bi + 1) * nt])
            qknorm(bk, k.khat[:, h, bi * nt:(bi + 1) * nt], "kg", 0, nt)
    load_w(k, bufA[:, :, 0:256], k.w_in, 0, NK, C_AV, 256)
    for blk in range(10):
        bv = P.bank()
        for kc in range(NK):
            P.mm(bv, bv.t[:, 0:256], k.hE[:, kc, blk * 128:(blk + 1) * 128], bufA[:, kc, 0:256])
        P.cp("act" if blk % 2 else "dve", k.vtok[:, blk, :], bv.t[:, 0:256])
    for half in range(2):
        buf = bufB if half == 0 else bufA
        load_w(k, buf[:, :, :], k.w_in, 0, NK, C_AQ + 512 * half, 512)
        for bi in range(5):
            for q4 in range(4):
                qc = 4 * half + q4
                bq = P.bank()
                for kc in range(NK):
                    P.mm(bq, bq.t[:, 0:nt], buf[:, kc, q4 * 128:(q4 + 1) * 128], k.hE[:, kc, bi * nt:(bi + 1) * nt])
                if bi == 0:
                    qknorm(bq, k.qhat[:, qc, 0:128], "qg", 128, 256)
                else:
                    e0 = nt * bi - 128
                    qknorm(bq, k.qhat[:, qc, e0:e0 + nt], "qg", 0, nt)


def phase_b2(k, ph):
    P = k.P
    S = lambda shape, dt, name: P.sb(shape, dt, name, stack=ph)
    ab = S([128, 4096], F32, "abias")
    P.dma("sp", ab[:, :], k.d["abias"][:, :])
    sk = S([128, 8], F32, "sk")
    P.act(sk[:, :], k.ppt[:, PPI["sk"]:PPI["sk"] + 8], AF.Exp)
    oneslo = S([128, 128], BF16, "oneslo")
    oneshi = S([128, 128], BF16, "oneshi")
    rowlo = S([128, 1], F32, "rowlo")
    rowhi = S([128, 1], F32, "rowhi")
    for t_, lo in ((oneslo, True), (oneshi, False)):
        P.memset("pool", t_[:, :], 0.0)
        P.memset("pool", t_[:, 0:64] if lo else t_[:, 64:128], 1.0)
    P.memset("pool", rowlo[:, :], 0.0)
    P.memset("pool", rowhi[:, :], 0.0)
    P.memset("pool", rowlo[0:64, :], 1.0)
    P.memset("pool", rowhi[64:128, :], 1.0)
    NR = 3
    klo = [S([128, 128], BF16, f"klo{i}") for i in range(NR)]
    khi = [S([128, 128], BF16, f"khi{i}") for i in range(NR)]
    vlo = [S([128, 128], BF16, f"vlo{i}") for i in range(NR)]
    vhi = [S([128, 128], BF16, f"vhi{i}") for i in range(NR)]
    for i in range(NR):
        P.memset("pool", vlo[i][:, :], 0.0)
        P.memset("pool", vhi[i][:, :], 0.0)
    sbuf = [S([128, 512], F32, f"sb{i}") for i in range(2)]
    pb = [S([128, 512], BF16, f"pb{i}") for i in range(4)]
    dn = [S([128, 256], F32, f"dn{i}") for i in range(2)]
    it = 0
    vi = 0

    def variants(h, kb, slot):
        kcols = slice(kb * 128, (kb + 1) * 128)
        P.ts("pool", klo[slot][:, :], k.khat[:, h, kcols], rowlo[:, 0:1], ALU.mult)
        P.ts("pool", khi[slot][:, :], k.khat[:, h, kcols], rowhi[:, 0:1], ALU.mult)
        P.cp("pool", vlo[slot][:, 0:64], k.vtok[:, kb, h * 64:(h + 1) * 64])
        P.cp("pool", vhi[slot][:, 64:128], k.vtok[:, kb, h * 64:(h + 1) * 64])

    for h in range(4):
        variants(h, 0, vi % NR)
        prev_slot = vi % NR
        vi += 1
        for n in range(9):
            qcols = slice(n * 128, (n + 1) * 128)
            cur_slot = vi % NR
            vi += 1
            variants(h, n + 1, cur_slot)
            ps = []
            for which, slot in ((0, prev_slot), (1, cur_slot)):
                bs = P.bank()
                for g in range(4):
                    P.mm(bs, bs.t[:, g * 128:(g + 1) * 128], (klo if g % 2 == 0 else khi)[slot][:, :],
                         k.qhat[:, 2 * h + g // 2, qcols])
                s_ = sbuf[it % 2]
                p_ = pb[it % 4]
                it += 1
                o = (h * 2 + which) * 512
                P.stt(s_[:, :], bs.t[:, :], 0.125, ab[:, o:o + 512], ALU.mult, ALU.add)
                P.act(p_[:, :], s_[:, :], AF.Exp)
                if n == 1 and which == 0:
                    P.ts("pool", p_[:, :], p_[:, :], pcol(k, "prevmask"), ALU.mult)
                ps.append((p_, slot))
            bo = P.bank()
            bd = P.bank()
            for g2 in range(2):
                for (p_, slot) in ps:
                    for par in range(2):
                        g = 2 * g2 + par
                        P.mm(bo, bo.t[:, g2 * 128:(g2 + 1) * 128], (vlo if par == 0 else vhi)[slot][:, :],
                             p_[:, g * 128:(g + 1) * 128])
                        P.mm(bd, bd.t[:, g2 * 128:(g2 + 1) * 128], (oneslo if par == 0 else oneshi)[:, :],
                             p_[:, g * 128:(g + 1) * 128])
            d_ = dn[(n * 4 + h) % 2]
            for g2 in range(2):
                P.ts("dve", d_[:, g2 * 128:(g2 + 1) * 128], bd.t[:, g2 * 128:(g2 + 1) * 128],
                     sk[:, 2 * h + g2:2 * h + g2 + 1], ALU.add)
            d_ap = d_[:, :].ap
            P.op("dve", lambda e, d_ap=d_ap: e.reciprocal(out=d_ap, in_=d_ap), reads=[d_[:, :]], writes=[d_[:, :]])
            for g2 in range(2):
                P.tt("dve", k.o_att[:, 2 * h + g2, qcols], bo.t[:, g2 * 128:(g2 + 1) * 128],
                     d_[:, g2 * 128:(g2 + 1) * 128], ALU.mult)
            prev_slot = cur_slot
    if k.dbg == "B":
        dbg_out(k, "oatt", k.o_att[:, :, :], [128, 8, NE], BF16)
        dbg_out(k, "qhat", k.qhat[:, :, :], [128, 8, NE], BF16)
        dbg_out(k, "hE", k.hE[:, :, :], [128, NK, NE1], BF16)


def phase_c1(k, ph):
    P = k.P
    S = lambda shape, dt, name: P.sb(shape, dt, name, stack=ph)
    wb = [S([128, NK, 256], BF16, f"wb{i}") for i in range(2)]
    wgr = [S([128, NK, 256], BF16, f"wgr{i}") for i in range(2)]
    wga = [S([128, NK, 256], BF16, f"wga{i}") for i in range(2)]
    grw = [S([128, NT3], F32, f"grw{i}") for i in range(2)]
    gat = [S([128, NT3], F32, f"gat{i}") for i in range(2)]
    t1 = [S([128, NT3], F32, f"t1c{i}") for i in range(2)]
    t2 = [S([128, NT3], F32, f"t2c{i}") for i in range(2)]

    def load(g):
        i = g % 2
        load_w(k, wb[i][:, :, :], k.w_branch, 0, NK, g * 256, 256)
        load_w(k, wgr[i][:, :, :], k.w_in, 0, NK, C_GRW + g * 256, 256)
        load_w(k, wga[i][:, :, :], k.w_in, 0, NK, C_GAT + g * 256, 256)

    load(0)
    it = 0
    for g in range(8):
        if g + 1 < 8:
            load(g + 1)
        i = g % 2
        for dl in range(2):
            dm = 2 * g + dl
            wc = slice(dl * 128, (dl + 1) * 128)
            for tt in range(3):
                c0 = HC0 + NT3 * tt
                ec = slice(c0, c0 + NT3)
                hc = slice(c0 + 128, c0 + 128 + NT3)
                mc = slice(NT3 * tt, NT3 * (tt + 1))
                b1 = P.bank()
                for kc in range(NK):
                    P.mm(b1, b1.t[:, 0:NT3], wgr[i][:, kc, wc], k.hE[:, kc, hc])
                b2 = P.bank()
                for kc in range(NK):
                    P.mm(b2, b2.t[:, 0:NT3], wga[i][:, kc, wc], k.hE[:, kc, hc])
                b3 = P.bank()
                for cc in range(8):
                    P.mm(b3, b3.t[:, 0:NT3], wb[i][:, cc, wc], k.o_rw[:, cc, ec])
                b4 = P.bank()
                for cc in range(8):
                    P.mm(b4, b4.t[:, 0:NT3], wb[i][:, 8 + cc, wc], k.o_att[:, cc, ec])
                j = it % 2
                it += 1
                P.act(grw[j][:, :], b1.t[:, 0:NT3], AF.Sigmoid)
                P.act(gat[j][:, :], b2.t[:, 0:NT3], AF.Sigmoid)
                P.tt("dve", t1[j][:, :], b3.t[:, 0:NT3], grw[j][:, :], ALU.mult)
                P.tt("dve", t2[j][:, :], b4.t[:, 0:NT3], gat[j][:, :], ALU.mult)
                P.tt("pool", k.mT[:, dm, mc], t1[j][:, :], t2[j][:, :], ALU.add)
    if k.dbg == "C1":
        dbg_out(k, "mT", k.mT[:, :, :], [128, NK, 3 * NT3], BF16)


def phase_c2(k, ph):
    P = k.P
    S = lambda shape, dt, name: P.sb(shape, dt, name, stack=ph)
    wo = [S([128, NK, 256], BF16, f"wo{i}") for i in range(2)]
    xr = [S([128, 3 * NT3], F32, f"xr{i}") for i in range(2)]
    load_w(k, wo[0][:, :, :], k.w_out, 0, NK, 0, 256)
    xsrc = k.xT.ap().rearrange("(k p) t -> p k t", p=128)
    for g in range(8):
        if g + 1 < 8:
            load_w(k, wo[(g + 1) % 2][:, :, :], k.w_out, 0, NK, (g + 1) * 256, 256)
        for dl in range(2):
            dm2 = 2 * g + dl
            xx = xr[dm2 % 2]
            P.dma("sp", xx[:, :], View(k.d["xT"], xsrc[:, dm2, E0 + HC0:SEQ]))
            for tt in range(3):
                mc = slice(NT3 * tt, NT3 * (tt + 1))
                bk = P.bank()
                for dm in range(NK):
                    P.mm(bk, bk.t[:, 0:NT3], wo[g % 2][:, dm, dl * 128:(dl + 1) * 128], k.mT[:, dm, mc])
                P.tt("dve", k.x1T[:, dm2, mc], bk.t[:, 0:NT3], xx[:, mc], ALU.add)
    if k.dbg == "C2":
        dbg_out(k, "x1T", k.x1T[:, :, :], [128, NK, 3 * NT3])


def phase_d(k, ph):
    P = k.P
    S = lambda shape, dt, name: P.sb(shape, dt, name, stack=ph)
    NC = 3 * NT3
    h2T = S([128, NK, NC], BF16, "h2T")
    sqr = [S([128, NT3], BF16, f"sqrD{i}") for i in range(2)]
    rstd = S([128, NC], F32, "rstdD")
    tmpn = S([128, NT3], F32, "tmpnD")
    for tt in range(3):
        mc = slice(NT3 * tt, NT3 * (tt + 1))
        bk = P.bank()
        for kc in range(NK):
            sq = sqr[kc % 2]
            P.act(sq[:, :], k.x1T[:, kc, mc], AF.Square)
            P.mm(bk, bk.t[:, 0:NT3], k.onesb[:, :], sq[:, :])
        rsqrt_act(P, rstd[:, mc], bk.t[:, 0:NT3], 1.0 / D, 1e-6, tmpn[:, :])
    for kc in range(NK):
        P.stt(h2T[:, kc, :], k.x1T[:, kc, :], pcol(k, "n2g", kc), rstd[:, :], ALU.mult, ALU.mult)
    wv = [S([128, NK, 256], BF16, f"wv{i}") for i in range(2)]
    wg = [S([128, NK, 256], BF16, f"wg{i}") for i in range(2)]
    wd = [S([128, 2, D], BF16, f"wd{i}") for i in range(2)]
    aT = [S([128, 2, 1024], BF16, f"aT{i}") for i in range(2)]
    u = [S([128, NC], F32, f"u{i}") for i in range(2)]
    cv = S([128, 1024], F32, "cv")
    cg = S([128, 1024], F32, "cg")
    sg = S([128, 1024], F32, "sgD")
    NFG = NFC // 2
    wdsrc = k.w_down.ap().rearrange("(f p) d -> p f d", p=128)

    def load(fg):
        i = fg % 2
        load_w(k, wv[i][:, :, :], k.w_up, 0, NK, fg * 256, 256)
        load_w(k, wg[i][:, :, :], k.w_up, 0, NK, DFF + fg * 256, 256)
        P.dma("pool", wd[i][:, :, :], View(k.d["w_down"], wdsrc[:, 2 * fg:2 * fg + 2, :]))

    nfg = int(os.environ.get("KNFG", str(NFG))) if k.dbg else NFG
    ev = [0]

    def up(fg):
        i = fg % 2
        for fc in range(2):
            wc = slice(fc * 128, (fc + 1) * 128)
            for kind, W, ctile in ((0, wv[i], cv), (1, wg[i], cg)):
                ch = (0 if kind == 0 else NFC) + 2 * fg + fc
                uu = u[kind]
                for tt in range(3):
                    mc = slice(NT3 * tt, NT3 * (tt + 1))
                    bk = P.bank()
                    for kc in range(NK):
                        P.mm(bk, bk.t[:, 0:NT3], W[:, kc, wc], h2T[:, kc, mc])
                    ev[0] += 1
                    P.cp("act" if ev[0] % 3 else "dve", uu[:, mc], bk.t[:, 0:NT3])
                P.ts("pool", uu[:, 0:2], uu[:, 0:2], pcol(k, "convmask"), ALU.mult)
                P.act(ctile[:, :], uu[:, 2:NC], AF.Identity, bias=pcol(k, "cb", ch), scale=pcol(k, "cw2", ch))
                P.stt(ctile[:, :], uu[:, 1:NC - 1], pcol(k, "cw1", ch), ctile[:, :], ALU.mult, ALU.add)
                P.stt(ctile[:, :], uu[:, 0:NC - 2], pcol(k, "cw0", ch), ctile[:, :], ALU.mult, ALU.add)
            P.act(sg[:, :], cg[:, :], AF.Silu)
            P.tt("pool", aT[i][:, fc, :], sg[:, :], cv[:, :], ALU.mult)

    def down(fg):
        i = fg % 2
        for dm in range(NK):
            for t2 in range(2):
                bk = P.bank()
                for fc in range(2):
                    P.mm(bk, bk.t[:, 0:512], wd[i][:, fc, dm * 128:(dm + 1) * 128], aT[i][:, fc, t2 * 512:(t2 + 1) * 512])
                oc = slice(2 + t2 * 512, 2 + (t2 + 1) * 512)
                P.tt("dve", k.x1T[:, dm, oc], bk.t[:, 0:512], k.x1T[:, dm, oc], ALU.add)

    load(0)
    if nfg > 1:
        load(1)
    up(0)
    for fg in range(nfg):
        if fg + 1 < nfg:
            up(fg + 1)
        down(fg)
        if fg + 2 < nfg:
            load(fg + 2)
    dst = k.outT.ap().rearrange("(k p) t -> p k t", p=128)
    for half in range(2):
        P.dma("sp", View(k.d["outT"], dst[:, 8 * half:8 * half + 8, :]), k.x1T[:, 8 * half:8 * half + 8, 2:NC])
    k.final_views.append(k.d["outT"][:, :])


def finish(k):
    P = k.P
    if k.final_views:
        P.wait_all("sp", k.final_views)
    P.emit()


def make_inputs(inp):
    cs, ab = host_consts()
    x = np.asarray(inp["x"], np.float32)
    shared = {
        "w_in": np.ascontiguousarray(np.asarray(inp["w_in"][0], np.float32)),
        "w_branch": np.ascontiguousarray(np.asarray(inp["w_branch"][0], np.float32)),
        "w_out": np.ascontiguousarray(np.asarray(inp["w_out"][0], np.float32)),
        "w_up": np.ascontiguousarray(np.asarray(inp["w_up"][0], np.float32)),
        "w_down": np.ascontiguousarray(np.asarray(inp["w_down"][0], np.float32)),
        "rw_w2": np.ascontiguousarray(np.asarray(inp["rw_w2"][0], np.float32)),
        "rw_a2": np.ascontiguousarray(np.asarray(inp["rw_a2"][0], np.float32)),
        "rw_g2": np.ascontiguousarray(np.asarray(inp["rw_g2"][0], np.float32)),
        "cst": cs, "abias": ab,
    }
    if DBG in ("A0", "A", "IO"):
        for n in ("w_branch", "w_out", "w_up", "w_down"):
            shared[n] = np.zeros((128, 128), np.float32)
    maps = []
    for c in range(8):
        b, j = c // 4, c % 4
        n = 1024 * (j + 1)
        xT = np.zeros((D, SEQ), np.float32)
        xT[:, SEQ - n:] = x[b, :n].T
        m = dict(shared)
        m["xT"] = xT
        m["pp"] = host_params(inp, j)
        maps.append(m)
    return maps


def kernel(**inp):
    nc, k = build_program(DBG)
    maps = make_inputs(inp)
    res = run_bass_kernel_spmd(nc, maps, core_ids=list(range(8)))
    out = np.zeros((2, SEQ, D), np.float32)
    for c in range(8):
        b, j = c // 4, c % 4
        out[b, 1024 * j:1024 * (j + 1), :] = np.asarray(res.results[c]["outT"]).T
    kernel.last = (res, k)
    return out
```

```python
import os
import numpy as np
from contextlib import ExitStack
import concourse.bass as bass
import concourse.mybir as mybir
from concourse.bass_utils import run_bass_kernel_spmd

F32 = mybir.dt.float32
BF16 = mybir.dt.bfloat16
ALU = mybir.AluOpType
AF = mybir.ActivationFunctionType

ENGS = ["sp", "act", "pool", "pe", "dve"]


class Tile:
    def __init__(self, handle, name):
        self.h = handle
        self.name = name
        self.w = None
        self.rd = {}

    def __getitem__(self, idx):
        return View(self, self.h[idx])


class View:
    def __init__(self, tile, ap):
        self.tile = tile
        self.ap = ap


class DSem:
    def __init__(self, handle, idx):
        self.h = handle
        self.key = ("d", idx)
        self.count = 0


class Bank:
    def __init__(self, tile):
        self.t = tile
        self.fresh = [True, True]


class Prog:
    def __init__(self, nc, stack):
        self.nc = nc
        self.stack = stack
        self.rec = {e: [] for e in ENGS}
        self.clock = {e: {} for e in ENGS}
        self.nseq = {e: 0 for e in ENGS}
        self.esem = {e: stack.enter_context(nc.semaphore("es_" + e)) for e in ENGS}
        self.sembase = {e: 0 for e in ENGS}
        self.dsems = []
        self.snap = {}
        self.ntile = 0
        self.banks = []
        self.bank_rr = 0
        self.nblocks = 0
        self.tot = {e: [0, 0] for e in ENGS}

    def sb(self, shape, dt, name=None, stack=None, side=None):
        self.ntile += 1
        name = (name or "t") + f"_{self.ntile}"
        h = (stack or self.stack).enter_context(self.nc.sbuf_tensor(name, list(shape), dt, side=side))
        return Tile(h, name)

    def dram(self, handle, name):
        return Tile(handle, name)

    def dsem(self):
        h = self.stack.enter_context(self.nc.semaphore(f"ds{len(self.dsems)}"))
        d = DSem(h, len(self.dsems))
        self.dsems.append(d)
        return d

    def init_banks(self):
        for i in range(8):
            h = self.stack.enter_context(self.nc.psum_tensor(f"bank{i}", [128, 512], F32))
            self.banks.append(Bank(Tile(h, f"bank{i}")))

    def bank(self):
        b = self.banks[self.bank_rr % 8]
        self.bank_rr += 1
        b.fresh = [True, True]
        return b

    def _collect(self, eng, reads, writes, same_engine_ok=False):
        need = {}

        def add(tok):
            if tok is None:
                return
            k, v = tok
            if same_engine_ok and k == eng:
                return
            if need.get(k, 0) < v:
                need[k] = v

        for r in reads:
            add(r.tile.w)
        for w in writes:
            add(w.tile.w)
            for k, v in w.tile.rd.items():
                add((k, v))
        ck = self.clock[eng]
        waits = []
        for k, v in need.items():
            if ck.get(k, 0) >= v:
                continue
            waits.append((k, v))
            ck[k] = v
            sn = self.snap.get((k, v))
            if sn:
                for k2, v2 in sn.items():
                    if ck.get(k2, 0) < v2:
                        ck[k2] = v2
        return waits

    def op(self, eng, fn, reads=(), writes=()):
        waits = self._collect(eng, reads, writes, same_engine_ok=(eng == "pe"))
        self.nseq[eng] += 1
        s = self.nseq[eng]
        tok = (eng, s)
        self.snap[tok] = dict(self.clock[eng])
        for r in reads:
            t = r.tile
            if t.rd.get(eng, 0) < s:
                t.rd[eng] = s
        for w in writes:
            w.tile.w = tok
            w.tile.rd = {}
        self.rec[eng].append(("op", fn, waits, s))
        return tok

    def dma(self, eng, out, in_, dsem=None, **kw):
        if dsem is None:
            if not hasattr(out.tile, "dsem"):
                out.tile.dsem = self.dsem()
            dsem = out.tile.dsem
        waits = self._collect(eng, [in_], [out])
        dsem.count += 16
        tok = (dsem.key, dsem.count)
        self.snap[tok] = dict(self.clock[eng])
        t = in_.tile
        if t.rd.get(dsem.key, 0) < dsem.count:
            t.rd[dsem.key] = dsem.count
        out.tile.w = tok
        out.tile.rd = {}
        self.rec[eng].append(("dma", (out.ap, in_.ap, dsem, kw), waits, None))
        return tok

    def wait_all(self, eng, views):
        waits = self._collect(eng, views, [])
        self.rec[eng].append(("wait", None, waits, None))

    def emit(self):
        waited = {e: set() for e in ENGS}
        for e in ENGS:
            for kind, payload, waits, s in self.rec[e]:
                for k, v in waits:
                    if isinstance(k, str):
                        waited[k].add(v)
        semval = {}
        for e in ENGS:
            for i, s in enumerate(sorted(waited[e])):
                semval[(e, s)] = self.sembase[e] + i + 1
            self.sembase[e] += len(waited[e])
        dsem_by_key = {d.key: d for d in self.dsems}
        tot = self.tot

        def run(e, eobj):
            for kind, payload, waits, s in self.rec[e]:
                for k, v in waits:
                    if isinstance(k, str):
                        eobj.wait_ge(self.esem[k], semval[(k, v)])
                    else:
                        eobj.wait_ge(dsem_by_key[k].h, v)
                    tot[e][1] += 1
                if kind == "op":
                    ins = payload(eobj)
                    if (e, s) in semval:
                        ins.then_inc(self.esem[e], 1)
                    tot[e][0] += 1
                elif kind == "dma":
                    oap, iap, dsem, kw = payload
                    eobj.dma_start(out=oap, in_=iap, **kw).then_inc(dsem.h, 16)
                    tot[e][0] += 1

        self.nblocks += 1
        with self.nc.Block() as block:
            @block.sync
            def _(eng):
                run("sp", eng)

            @block.scalar
            def _(eng):
                run("act", eng)

            @block.gpsimd
            def _(eng):
                run("pool", eng)

            @block.tensor
            def _(eng):
                run("pe", eng)

            @block.vector
            def _(eng):
                run("dve", eng)
        self.rec = {e: [] for e in ENGS}
        for e in ENGS:
            for k in ENGS:
                self.clock[e][k] = self.nseq[k]

    def mm(self, bank, out, lhsT, rhs, half=None):
        if half is None:
            start = bank.fresh[0] or bank.fresh[1]
            bank.fresh = [False, False]
        else:
            start = bank.fresh[half]
            bank.fresh[half] = False
        o, l, r = out.ap, lhsT.ap, rhs.ap
        self.op("pe", lambda e: e.matmul(o, lhsT=l, rhs=r, start=start, stop=True, skip_group_check=True),
                reads=[lhsT, rhs], writes=[out])

    def act(self, out, in_, func, bias=None, scale=None):
        reads = [in_]
        kw = {}
        if bias is not None:
            if isinstance(bias, View):
                reads.append(bias)
                kw["bias"] = bias.ap
            else:
                kw["bias"] = float(bias)
        if scale is not None:
            if isinstance(scale, View):
                reads.append(scale)
                kw["scale"] = scale.ap
            else:
                kw["scale"] = float(scale)
        o, i = out.ap, in_.ap
        self.op("act", lambda e: e.activation(out=o, in_=i, func=func, **kw), reads=reads, writes=[out])

    def tt(self, eng, out, a, b, op):
        o, x, y = out.ap, a.ap, b.ap
        self.op(eng, lambda e: e.tensor_tensor(out=o, in0=x, in1=y, op=op), reads=[a, b], writes=[out])

    def ts(self, eng, out, a, s1, op0, s2=None, op1=None):
        reads = [a]
        v1 = s1
        if isinstance(s1, View):
            reads.append(s1)
            v1 = s1.ap
        v2 = s2
        if isinstance(s2, View):
            reads.append(s2)
            v2 = s2.ap
        o, x = out.ap, a.ap
        if op1 is None:
            self.op(eng, lambda e: e.tensor_scalar(out=o, in0=x, scalar1=v1, scalar2=None, op0=op0),
                    reads=reads, writes=[out])
        else:
            self.op(eng, lambda e: e.tensor_scalar(out=o, in0=x, scalar1=v1, scalar2=v2, op0=op0, op1=op1),
                    reads=reads, writes=[out])

    def stt(self, out, a, s, b, op0, op1):
        reads = [a, b]
        sv = s
        if isinstance(s, View):
            reads.append(s)
            sv = s.ap
        o, x, y = out.ap, a.ap, b.ap
        self.op("dve", lambda e: e.scalar_tensor_tensor(out=o, in0=x, scalar=sv, in1=y, op0=op0, op1=op1),
                reads=reads, writes=[out])

    def cp(self, eng, out, a):
        o, x = out.ap, a.ap
        if eng == "act":
            self.op("act", lambda e: e.activation(out=o, in_=x, func=AF.Copy), reads=[a], writes=[out])
        else:
            self.op(eng, lambda e: e.tensor_copy(out=o, in_=x), reads=[a], writes=[out])

    def memset(self, eng, out, val):
        o = out.ap
        self.op(eng, lambda e: e.memset(o, val), reads=[], writes=[out])


D = 2048
SEQ = 4096
NK = 16
RW = 1024
DFF = 5632
NFC = DFF // 128
CDEC = float(np.exp(-0.5))
E0 = 2944
NE = 1152
E1 = 2816
NE1 = 1280
HC0 = 126
NT3 = 342
C_R, C_K, C_V, C_WD, C_AD, C_GD = 0, 1024, 2048, 3072, 3168, 3264
C_AQ, C_AK, C_AV = 3520, 4544, 4800
C_GRW, C_GAT = 5056, 7104

PPI = {}
_o = 0
for _n, _c in [("n1g", 16), ("n2g", 16), ("mu_r", 8), ("mu_k", 8), ("mu_v", 8), ("mu_wd", 1), ("mu_ad", 1),
               ("mu_gd", 2), ("w0", 8), ("a0", 8), ("kk", 8), ("ka", 8), ("rk", 8), ("gnw", 8), ("gnb", 8),
               ("qg", 1), ("kg", 1), ("cw0", 88), ("cw1", 88), ("cw2", 88), ("cb", 88), ("sk", 8),
               ("prevmask", 1), ("convmask", 1)]:
    PPI[_n] = _o
    _o += _c
NPP = _o

CSI = {}
_o = 0
for _n, _c in [("ident", 128), ("i2", 64), ("m1", 256), ("bones", 128), ("rmask", 256), ("m13", 512), ("m22", 512),
               ("ones", 128)]:
    CSI[_n] = _o
    _o += _c
NCS = _o


def _chunked(v, ncol):
    return np.ascontiguousarray(np.asarray(v, np.float32).reshape(ncol, 128).T)


def host_consts():
    cs = np.zeros((128, NCS), np.float32)
    p = np.arange(128)[:, None]
    f = np.arange(128)[None, :]
    cs[:, CSI["ident"]:CSI["ident"] + 128] = (p == f)
    cs[:, CSI["i2"]:CSI["i2"] + 64] = ((p % 64) == np.arange(64)[None, :])
    m1 = (p < f).astype(np.float32)
    m2 = (p <= f).astype(np.float32)
    m3 = (f < p).astype(np.float32)
    cs[:, CSI["m1"]:CSI["m1"] + 256] = np.tile(m1, (1, 2))
    cs[:, CSI["m13"]:CSI["m13"] + 512] = np.concatenate([np.tile(m1, (1, 2)), np.tile(m3, (1, 2))], axis=1)
    cs[:, CSI["m22"]:CSI["m22"] + 512] = np.tile(m2, (1, 4))
    cs[:, CSI["bones"]:CSI["bones"] + 128] = ((p // 64) == (f // 64))
    rm = np.ones((128, 256), np.float32)
    rm[:, 0::128] = 0.0
    cs[:, CSI["rmask"]:CSI["rmask"] + 256] = rm
    cs[:, CSI["ones"]:CSI["ones"] + 128] = 1.0
    slopes = np.exp2(-8.0 * np.arange(1, 17, dtype=np.float32) / 16.0).astype(np.float32)
    ab = np.zeros((128, 4, 2, 4, 128), np.float32)
    k = np.arange(128)[:, None]
    q = np.arange(128)[None, :]
    for h in range(4):
        for g in range(4):
            sl = slopes[4 * h + g]
            dprev = (128 + q - k).astype(np.float32)
            dcur = (q - k).astype(np.float32)
            ab[:, h, 0, g, :] = np.where(dprev < 128, -sl * dprev, -30000.0)
            ab[:, h, 1, g, :] = np.where(dcur >= 0, -sl * dcur, -30000.0)
    return cs, ab.reshape(128, 4096)


def host_params(inp, j):
    pp = np.zeros((128, NPP), np.float32)

    def put(name, arr):
        arr = np.asarray(arr, np.float32)
        pp[:arr.shape[0], PPI[name]:PPI[name] + arr.shape[1]] = arr

    put("n1g", _chunked(inp["norm1_g"][0], 16))
    put("n2g", _chunked(inp["norm2_g"][0], 16))
    mu = np.asarray(inp["rw_mu"][0], np.float32)
    put("mu_r", _chunked(mu[0:1024], 8))
    put("mu_k", _chunked(mu[1024:2048], 8))
    put("mu_v", _chunked(mu[2048:3072], 8))
    put("mu_wd", mu[3072:3168].reshape(96, 1))
    put("mu_ad", mu[3168:3264].reshape(96, 1))
    put("mu_gd", _chunked(mu[3264:3520], 2))
    put("w0", _chunked(inp["rw_w0"][0], 8))
    put("a0", _chunked(inp["rw_a0"][0], 8))
    put("kk", _chunked(inp["rw_k_k"][0], 8))
    put("ka", _chunked(inp["rw_k_a"][0], 8))
    put("rk", _chunked(np.asarray(inp["rw_r_k"][0]).reshape(-1), 8))
    put("gnw", _chunked(inp["rw_gn_w"][0], 8))
    put("gnb", _chunked(inp["rw_gn_b"][0], 8))
    put("qg", np.tile(np.asarray(inp["q_norm_g"][0], np.float32), 2).reshape(128, 1))
    put("kg", np.tile(np.asarray(inp["k_norm_g"][0], np.float32), 2).reshape(128, 1))
    cw = np.asarray(inp["conv_w"][0], np.float32)
    put("cw0", _chunked(cw[0], 88))
    put("cw1", _chunked(cw[1], 88))
    put("cw2", _chunked(cw[2], 88))
    put("cb", _chunked(inp["conv_b"][0], 88))
    sk = np.asarray(inp["attn_sinks"][0], np.float32)
    put("sk", np.stack([sk[2 * c + (np.arange(128) // 64)] for c in range(8)], axis=1))
    put("prevmask", np.full((128, 1), 0.0 if j == 0 else 1.0, np.float32))
    put("convmask", np.full((128, 1), 0.0 if j == 0 else 1.0, np.float32))
    return pp


DBG = os.environ.get("KDBG", "")


class K:
    pass


def build_program(dbg=""):
    nc = bass.Bass("TRN2", target_bir_lowering=False)
    k = K()
    k.nc = nc
    dt_in = lambda name, shape: nc.dram_tensor(name, list(shape), F32, kind="ExternalInput")
    k.xT = dt_in("xT", [D, SEQ])
    k.w_in = dt_in("w_in", [D, 9152])
    small = dbg in ("A0", "A", "IO")
    k.w_branch = dt_in("w_branch", [D, D] if not small else [128, 128])
    k.w_out = dt_in("w_out", [D, D] if not small else [128, 128])
    k.w_up = dt_in("w_up", [D, 2 * DFF] if not small else [128, 128])
    k.w_down = dt_in("w_down", [DFF, D] if not small else [128, 128])
    k.w2 = dt_in("rw_w2", [96, RW])
    k.a2 = dt_in("rw_a2", [96, RW])
    k.g2 = dt_in("rw_g2", [256, RW])
    k.pp = dt_in("pp", [128, NPP])
    k.cst = dt_in("cst", [128, NCS])
    k.abias = dt_in("abias", [128, 4096])
    k.outT = nc.dram_tensor("outT", [D, 1024], F32, kind="ExternalOutput")
    k.dbg = dbg
    k.dbg_outs = {}
    with ExitStack() as st:
        P = Prog(nc, st)
        k.P = P
        P.init_banks()
        k.d = {n: P.dram(getattr(k, n), n) for n in
               ["xT", "w_in", "w_branch", "w_out", "w_up", "w_down", "w2", "a2", "g2", "pp", "cst", "abias", "outT"]}
        setup_persistent(k, st)
        R = lambda stack, shape, dt, name: P.sb(shape, dt, name, stack=stack, side="right")
        r_orw = ExitStack()
        k.o_rw = R(r_orw, [128, 8, NE], BF16, "o_rw")
        if dbg == "IO":
            dbg_out(k, "pp", k.ppt[:, :], [128, NPP])
        if dbg not in ("IO", "B"):
            for hp in range(2):
                with ExitStack() as ph:
                    phase_a(k, ph, hp)
                    P.emit()
                if dbg == "A0":
                    break
        if dbg not in ("A0", "A", "IO"):
            r_b = ExitStack()
            k.hE = R(r_b, [128, NK, NE1], BF16, "hE")
            k.o_att = R(r_b, [128, 8, NE], BF16, "o_att")
            r_b2 = ExitStack()
            k.qhat = R(r_b2, [128, 8, NE], BF16, "qhat")
            k.khat = R(r_b2, [128, 4, NE1], BF16, "khat")
            k.vtok = R(r_b2, [128, 10, 256], BF16, "vtok")
            with ExitStack() as ph:
                phase_b1(k, ph)
                P.emit()
            with ExitStack() as ph:
                phase_b2(k, ph)
                P.emit()
            r_b2.close()
            if dbg != "B":
                l_m = ExitStack()
                k.mT = P.sb([128, NK, 3 * NT3], BF16, "mT", stack=l_m)
                with ExitStack() as ph:
                    phase_c1(k, ph)
                    P.emit()
                r_b.close()
                r_orw.close()
                r_orw = None
                r_x1 = ExitStack()
                k.x1T = R(r_x1, [128, NK, 3 * NT3], F32, "x1T")
                with ExitStack() as ph:
                    phase_c2(k, ph)
                    P.emit()
                l_m.close()
                with ExitStack() as ph:
                    phase_d(k, ph)
                    P.emit()
                r_x1.close()
            else:
                r_b.close()
        if r_orw is not None:
            r_orw.close()
        finish(k)
        k.stats = P.tot
    return nc, k


def dbg_out(k, name, view, shape, dt=F32):
    P = k.P
    t = k.nc.dram_tensor("dbg_" + name, list(shape), dt, kind="ExternalOutput")
    k.dbg_outs[name] = t
    td = P.dram(t, "dbg_" + name)
    P.dma("sp", View(td, t.ap()), view)
    k.final_views.append(View(td, t.ap()))


def setup_persistent(k, st):
    P = k.P
    k.final_views = []
    k.ppt = P.sb([128, NPP], F32, "pp")
    k.cs = P.sb([128, NCS], F32, "cs")
    P.dma("sp", k.ppt[:, :], k.d["pp"][:, :])
    P.dma("sp", k.cs[:, :], k.d["cst"][:, :])
    k.identb = P.sb([128, 128], BF16, "identb")
    k.onesb = P.sb([128, 128], BF16, "onesb")
    k.bonesb = P.sb([128, 128], BF16, "bonesb")
    P.cp("dve", k.identb[:, :], k.cs[:, CSI["ident"]:CSI["ident"] + 128])
    P.cp("dve", k.onesb[:, :], k.cs[:, CSI["ones"]:CSI["ones"] + 128])
    P.cp("dve", k.bonesb[:, :], k.cs[:, CSI["bones"]:CSI["bones"] + 128])
    k.negb = P.sb([128, 16], F32, "negb")
    P.ts("dve", k.negb[:, :], k.ppt[:, PPI["w0"]:PPI["w0"] + 16], -1.0, ALU.mult)
    k.omu = P.sb([128, 28], F32, "omu")
    P.ts("dve", k.omu[:, :], k.ppt[:, PPI["mu_r"]:PPI["mu_r"] + 28], -1.0, ALU.mult, 1.0, ALU.add)


def pcol(k, name, c=0, rows=128):
    o = PPI[name] + c
    return k.ppt[0:rows, o:o + 1]


def omucol(k, name, c=0, rows=128):
    o = PPI[name] - PPI["mu_r"] + c
    return k.omu[0:rows, o:o + 1]


def cview(k, name, n):
    return k.cs[:, CSI[name]:CSI[name] + n]


def rsqrt_act(P, out, in_, scale, bias, tmp):
    P.act(tmp, in_, AF.Ln, bias=bias, scale=scale)
    P.act(out, tmp, AF.Exp, scale=-0.5)


def rmsnorm_block(k, ph_tiles, xb, hT, nt, gname):
    P = k.P
    sqb, rstd, tmp = ph_tiles
    P.act(sqb[:, :, 0:nt], xb[:, :, 0:nt], AF.Square)
    bk = P.bank()
    for kc in range(NK):
        P.mm(bk, bk.t[:, 0:nt], k.onesb[:, :], sqb[:, kc, 0:nt])
    rsqrt_act(P, rstd[:, 0:nt], bk.t[:, 0:nt], 1.0 / D, 1e-6, tmp[:, 0:nt])
    for kc in range(NK):
        P.stt(hT[:, kc, 0:nt], xb[:, kc, 0:nt], pcol(k, gname, kc), rstd[:, 0:nt], ALU.mult, ALU.mult)


def load_w(k, dst_view, src_handle, r0, nk, c0, ncol):
    P = k.P
    src = src_handle.ap().rearrange("(k p) c -> p k c", p=128)[:, r0:r0 + nk, c0:c0 + ncol]
    name = [n for n, t in k.d.items() if t.h is src_handle][0]
    P.dma("pool", dst_view, View(k.d[name], src))


class PA:
    pass


NTA = 256
NBA = SEQ // NTA
BI_R = 10
BI_O = 11


def proj(k, bank, W, col0, ncols, hT, nt):
    P = k.P
    for kc in range(NK):
        P.mm(bank, bank.t[0:ncols, 0:nt], W[:, kc, col0:col0 + ncols], hT[:, kc, 0:nt])


def proj_a(k, bank, W, col0, ncols, hTk, nt):
    P = k.P
    for kc in range(NK):
        P.mm(bank, bank.t[0:ncols, 0:nt], W[:, kc, col0:col0 + ncols], hTk[kc][:, 0:nt])


def shift(k, A, zps, rows, nt, mu, omu, carcol, out):
    P = k.P
    tmp = A.shtmp[A.shi % 3]
    A.shi += 1
    P.act(tmp[0:rows, 1:nt + 1], zps[0:rows, 0:nt], AF.Identity, scale=mu)
    P.cp("pool", tmp[0:rows, 0:1], A.car[0:rows, carcol:carcol + 1])
    P.cp("pool", A.car[0:rows, carcol:carcol + 1], tmp[0:rows, nt:nt + 1])
    P.stt(out, zps[0:rows, 0:nt], omu, tmp[0:rows, 0:nt], ALU.mult, ALU.add)


def phase_a(k, ph, hp):
    P = k.P
    A = PA()
    A.hp = hp
    nt = NTA
    S = lambda shape, dt, name: P.sb(shape, dt, name, stack=ph)
    A.W = S([128, NK, 1984], BF16, "WA")
    ch0 = 512 * hp
    load_w(k, A.W[:, :, 0:512], k.w_in, 0, NK, C_K + ch0, 512)
    load_w(k, A.W[:, :, 512:1024], k.w_in, 0, NK, C_V + ch0, 512)
    load_w(k, A.W[:, :, 1536:1984], k.w_in, 0, NK, C_WD, 448)
    load_w(k, A.W[:, :, 1024:1536], k.w_in, 0, NK, C_R + ch0, 512)
    A.w2b = S([128, 512], BF16, "w2b")
    A.a2b = S([128, 512], BF16, "a2b")
    A.g2b = S([128, 2, 512], BF16, "g2b")
    P.dma("pool", A.w2b[0:96, :], k.d["w2"][:, ch0:ch0 + 512])
    P.dma("pool", A.a2b[0:96, :], k.d["a2"][:, ch0:ch0 + 512])
    P.dma("pool", A.g2b[:, :, :], View(k.d["g2"], k.g2.ap().rearrange("(k p) c -> p k c", p=128)[:, :, ch0:ch0 + 512]))
    A.car = S([128, 16], F32, "car")
    P.memset("pool", A.car[:, :], 0.0)
    A.H = [S([128, 64], F32, f"H{i}") for i in range(8)]
    A.Hbf = [S([128, 64], BF16, f"Hbf{i}") for i in range(8)]
    for i in range(8):
        P.memset("pool", A.H[i][:, :], 0.0)
        P.memset("pool", A.Hbf[i][:, :], 0.0)
    A.xb = S([128, NK // 2, nt], F32, "xb")
    A.ssp = S([128, nt], F32, "ssp")
    A.sqr = [S([128, nt], BF16, f"sqr{i}") for i in range(2)]
    A.hTk = [S([128, nt], BF16, f"hT{i}") for i in range(NK)]
    A.rstd = S([128, nt], F32, "rstd")
    A.tmpn = S([128, nt], F32, "tmpn")
    A.shtmp = [S([128, nt + 1], F32, f"shtmp{i}") for i in range(3)]
    A.shi = 0
    A.scr = S([128, nt], F32, "scr")
    A.th_wd = S([128, nt], BF16, "th_wd")
    A.ad_s = S([128, nt], BF16, "ad_s")
    A.sg_gd = [S([128, 2, nt], BF16, f"sg_gd{i}") for i in range(2)]
    alias32 = {"ks": 0, "vs": 1, "rs": 2, "sg": 3, "cm": 3, "Em": 3, "aa": 4, "f": 4, "kp": 4, "cs": 5, "En": 6,
               "Ep": 7, "ssc": 8, "ln": 8, "rn": 8, "kkn": 9, "kba": 10, "bonus": 11, "yT": 12,
               "dd": 0, "sqd": 1, "rs2": 2, "yn": 3, "t1": 5, "t2": 6}
    alias16 = {"sq": 0, "rkb": 0, "At": 1, "Bt": 2, "Kt": 3, "Bh": 4, "Kh": 5, "Vb": 6, "Rt": 7}
    A.T = []
    for par in range(3):
        T = PA()
        b32 = [S([128, nt], F32, f"b32_{par}_{i}") for i in range(13)]
        b16 = [S([128, nt], BF16, f"b16_{par}_{i}") for i in range(8)]
        for n, i in alias32.items():
            setattr(T, n, b32[i])
        for n, i in alias16.items():
            setattr(T, n, b16[i])
        T.TK = [S([128, 384], BF16, f"TK{par}_{i}") for i in range(2)]
        T.EpL = S([128, 2], F32, f"EpL{par}")
        T.dgW = S([128, 128], F32, f"dgW{par}")
        A.T.append(T)
    A.Q = []
    for slot in range(4):
        Q = PA()
        Q.XX = [S([128, 512], BF16, f"XX{slot}_{i}") for i in range(2)]
        Q.Z = [S([128, 256], BF16, f"Z{slot}_{i}") for i in range(2)]
        Q.AAK = S([128, 256], BF16, f"AAK{slot}")
        Q.ARBK = S([128, 512], BF16, f"ARBK{slot}")
        Q.PG = S([128, 256], F32, f"PG{slot}")
        Q.QT = S([128, 256], BF16, f"QT{slot}")
        P.memset("pool", Q.PG[:, :], 0.0)
        P.memset("pool", Q.QT[:, :], 0.0)
        Q.acc = P.banks[slot]
        A.Q.append(Q)
    P.bank_rr = 0
    _orig_bank = P.bank

    def ring_bank():
        b = P.banks[4 + (P.bank_rr % 4)]
        P.bank_rr += 1
        b.fresh = [True, True]
        return b
    P.bank = ring_bank

    evi = [0]

    def evac(out, in_):
        evi[0] += 1
        P.cp("act" if evi[0] % 3 else "dve", out, in_)

    def recip(out, in_):
        o, i = out.ap, in_.ap
        P.op("dve", lambda e: e.reciprocal(out=o, in_=i), reads=[in_], writes=[out])

    def block_head(bi):
        t0 = nt * bi
        xsrc = k.xT.ap().rearrange("(k p) t -> p k t", p=128)

        def loadhalf(hf):
            P.dma("sp", A.xb[:, :, :], View(k.d["xT"], xsrc[:, 8 * hf:8 * hf + 8, t0:t0 + nt]))

        for hf in range(2):
            loadhalf(hf)
            yield
            bk = P.bank()
            for kl in range(8):
                sq = A.sqr[kl % 2]
                if kl % 2 == 0:
                    P.act(sq[:, :], A.xb[:, kl, :], AF.Square)
                else:
                    P.tt("dve", sq[:, :], A.xb[:, kl, :], A.xb[:, kl, :], ALU.mult)
                P.mm(bk, bk.t[:, 0:nt], k.onesb[:, :], sq[:, :])
            if hf == 0:
                P.cp("act", A.ssp[:, :], bk.t[:, 0:nt])
            else:
                P.tt("dve", A.ssp[:, :], bk.t[:, 0:nt], A.ssp[:, :], ALU.add)
            yield
        rsqrt_act(P, A.rstd[:, :], A.ssp[:, :], 1.0 / D, 1e-6, A.tmpn[:, :])
        for hf in range(2):
            loadhalf(hf)
            yield
            for kl in range(8):
                kc = 8 * hf + kl
                P.stt(A.hTk[kc][:, :], A.xb[:, kl, :], pcol(k, "n1g", kc), A.rstd[:, :], ALU.mult, ALU.mult)
                if kl % 4 == 3:
                    yield
        bk = P.bank()
        proj_a(k, bk, A.W, 1536, 96, A.hTk, nt)
        shift(k, A, bk.t, 96, nt, pcol(k, "mu_wd", 0, 96), omucol(k, "mu_wd", 0, 96), 12, A.scr[0:96, :])
        P.act(A.th_wd[0:96, :], A.scr[0:96, :], AF.Tanh)
        yield
        bk = P.bank()
        proj_a(k, bk, A.W, 1632, 96, A.hTk, nt)
        shift(k, A, bk.t, 96, nt, pcol(k, "mu_ad", 0, 96), omucol(k, "mu_ad", 0, 96), 13, A.ad_s[0:96, :])
        yield
        if bi >= BI_R:
            for j in range(2):
                bk = P.bank()
                proj_a(k, bk, A.W, 1728 + 128 * j, 128, A.hTk, nt)
                shift(k, A, bk.t, 128, nt, pcol(k, "mu_gd", j), omucol(k, "mu_gd", j), 14 + j, A.scr[:, :])
                P.act(A.sg_gd[bi % 2][:, j, :], A.scr[:, :], AF.Sigmoid)
                yield

    def prep(bi, lc):
        cc = 4 * hp + lc
        needR = bi >= BI_R
        outp = bi >= BI_O
        T = A.T[(bi * 4 + lc) % 3]
        bk = P.bank()
        proj_a(k, bk, A.W, lc * 128, 128, A.hTk, nt)
        shift(k, A, bk.t, 128, nt, pcol(k, "mu_k", cc), omucol(k, "mu_k", cc), lc, T.ks[:, :])
        yield
        bk = P.bank()
        proj_a(k, bk, A.W, 512 + lc * 128, 128, A.hTk, nt)
        shift(k, A, bk.t, 128, nt, pcol(k, "mu_v", cc), omucol(k, "mu_v", cc), 4 + lc, T.vs[:, :])
        yield
        if needR:
            bk = P.bank()
            proj_a(k, bk, A.W, 1024 + lc * 128, 128, A.hTk, nt)
            shift(k, A, bk.t, 128, nt, pcol(k, "mu_r", cc), omucol(k, "mu_r", cc), 8 + lc, T.rs[:, :])
            yield
        bw = P.bank()
        P.mm(bw, bw.t[:, 0:nt], A.w2b[0:96, lc * 128:(lc + 1) * 128], A.th_wd[0:96, :])
        P.act(T.sg[:, :], bw.t[:, 0:nt], AF.Sigmoid, bias=pcol(k, "w0", cc))
        ba = P.bank()
        P.mm(ba, ba.t[:, 0:nt], A.a2b[0:96, lc * 128:(lc + 1) * 128], A.ad_s[0:96, :])
        P.act(T.aa[:, :], ba.t[:, 0:nt], AF.Sigmoid, bias=pcol(k, "a0", cc))
        yield
        rm = cview(k, "rmask", nt)
        cs_ap, rm_ap, sg_ap = T.cs[:, :].ap, rm.ap, T.sg[:, :].ap
        P.op("dve", lambda e: e.tensor_tensor_scan(out=cs_ap, data0=rm_ap, data1=sg_ap, initial=0.0,
                                                   op0=ALU.mult, op1=ALU.add),
             reads=[rm, T.sg[:, :]], writes=[T.cs[:, :]])
        P.tt("pool", T.cm[:, :], T.cs[:, :], T.sg[:, :], ALU.subtract)
        P.act(T.En[:, :], T.cs[:, :], AF.Exp, scale=CDEC)
        P.act(T.Em[:, :], T.cm[:, :], AF.Exp, scale=-CDEC)
        for c in range(2):
            P.act(T.EpL[:, c:c + 1], T.cs[:, c * 128 + 127:c * 128 + 128], AF.Exp, scale=-CDEC)
        if outp:
            P.act(T.Ep[:, :], T.cs[:, :], AF.Exp, scale=-CDEC)
        yield
        P.act(T.sq[:, :], T.ks[:, :], AF.Square, scale=pcol(k, "kk", cc))
        bs = P.bank()
        P.mm(bs, bs.t[:, 0:nt], k.bonesb[:, :], T.sq[:, :])
        P.ts("dve", T.ssc[:, :], bs.t[:, 0:nt], 1e-24, ALU.max)
        P.act(T.ln[:, :], T.ssc[:, :], AF.Ln, scale=float(2.0 ** 40))
        P.act(T.rn[:, :], T.ln[:, :], AF.Exp, scale=-0.5)
        yield
        P.stt(T.kkn[:, :], T.ks[:, :], pcol(k, "kk", cc), T.rn[:, :], ALU.mult, ALU.mult)
        P.stt(T.At[:, :], T.kkn[:, :], -float(2.0 ** 20), T.Em[:, :], ALU.mult, ALU.mult)
        P.stt(T.kba[:, :], T.kkn[:, :], float(2.0 ** 20), T.aa[:, :], ALU.mult, ALU.mult)
        yield
        P.ts("dve", T.f[:, :], T.aa[:, :], -1.0, ALU.add, pcol(k, "ka", cc), ALU.mult)
        P.stt(T.kp[:, :], T.f[:, :], 1.0, T.ks[:, :], ALU.add, ALU.mult)
        P.tt("pool", T.Bt[:, :], T.kba[:, :], T.En[:, :], ALU.mult)
        P.tt("pool", T.Kt[:, :], T.kp[:, :], T.En[:, :], ALU.mult)
        P.cp("pool", T.Vb[:, :], T.vs[:, :])
        yield
        for c in range(2):
            cs_ = slice(c * 128, (c + 1) * 128)
            P.stt(T.Bh[:, cs_], T.kba[:, cs_], T.EpL[:, c:c + 1], T.En[:, cs_], ALU.mult, ALU.mult)
            P.stt(T.Kh[:, cs_], T.kp[:, cs_], T.EpL[:, c:c + 1], T.En[:, cs_], ALU.mult, ALU.mult)
            P.ts("dve", T.dgW[:, c * 64:(c + 1) * 64], cview(k, "i2", 64), T.EpL[:, c:c + 1], ALU.mult)
        yield
        if outp:
            P.tt("pool", T.Rt[:, :], T.rs[:, :], T.Ep[:, :], ALU.mult)
            P.stt(T.rkb[:, :], T.rs[:, :], pcol(k, "rk", cc), T.kp[:, :], ALU.mult, ALU.mult)
            bb = P.bank()
            P.mm(bb, bb.t[:, 0:nt], k.bonesb[:, :], T.rkb[:, :])
            P.tt("dve", T.bonus[:, :], bb.t[:, 0:nt], T.vs[:, :], ALU.mult)
            yield

    def hquad(bi, lc, h, slot):
        T = A.T[(bi * 4 + lc) % 3]
        Q = A.Q[slot]
        outp = bi >= BI_O
        m1 = cview(k, "m1", 256)
        hs = slice(64 * h, 64 * h + 64)
        hd = 2 * lc + h

        def cs_(ci):
            return slice(ci * 128, (ci + 1) * 128)
        m13 = cview(k, "m13", 512)
        m22 = cview(k, "m22", 512)
        g = P.bank()
        for ci in range(2):
            P.mm(g, g.t[:, cs_(ci)], T.Bt[hs, cs_(ci)], T.At[hs, cs_(ci)])
        for ci in range(2):
            P.mm(g, g.t[:, 256 + ci * 128:256 + (ci + 1) * 128], T.At[hs, cs_(ci)], T.Bt[hs, cs_(ci)])
        P.tt("dve", Q.XX[0][:, :], g.t[:, :], m13, ALU.mult)
        yield
        g = P.bank()
        for ci in range(2):
            P.mm(g, g.t[:, cs_(ci)], T.Kt[hs, cs_(ci)], T.At[hs, cs_(ci)])
        P.tt("dve", Q.AAK[:, :], g.t[:, 0:256], m1, ALU.mult)
        if outp:
            g = P.bank()
            for ci in range(2):
                P.mm(g, g.t[:, cs_(ci)], T.Bt[hs, cs_(ci)], T.Rt[hs, cs_(ci)])
            for ci in range(2):
                P.mm(g, g.t[:, 256 + ci * 128:256 + (ci + 1) * 128], T.Kt[hs, cs_(ci)], T.Rt[hs, cs_(ci)])
            P.tt("dve", Q.ARBK[:, :], g.t[:, :], m22, ALU.mult)
        yield
        if h == 0:
            for ci in range(2):
                g = P.bank()
                P.mm(g, g.t[:, 0:128], T.Vb[:, cs_(ci)], k.identb[:, :])
                P.mm(g, g.t[:, 128:256], T.Bh[:, cs_(ci)], k.identb[:, :])
                P.mm(g, g.t[:, 256:384], T.Kh[:, cs_(ci)], k.identb[:, :])
                evac(T.TK[ci][:, :], g.t[:, 0:384])
        yield
        acc = Q.acc
        acc.fresh = [True, True]
        for ci in range(2):
            P.mm(acc, acc.t[:, ci * 128:ci * 128 + 64], T.At[:, cs_(ci)], k.identb[:, hs])
            P.mm(acc, acc.t[:, ci * 128 + 64:ci * 128 + 128], Q.AAK[:, cs_(ci)], T.TK[ci][:, 64 * h:64 * h + 64])
        evac(Q.Z[0][:, :], acc.t[:, 0:256])
        yield
        zi = 0
        for i in range(7):
            XX = Q.XX[i % 2]
            for ci in range(2):
                P.mm(acc, acc.t[:, cs_(ci)], XX[:, cs_(ci)], Q.Z[zi][:, cs_(ci)])
            zi ^= 1
            evac(Q.Z[zi][:, :], acc.t[:, 0:256])
            if i < 6:
                g = P.bank()
                for ci in range(2):
                    P.mm(g, g.t[:, cs_(ci)], XX[:, 256 + ci * 128:256 + (ci + 1) * 128], XX[:, cs_(ci)])
                if i < 5:
                    for ci in range(2):
                        P.mm(g, g.t[:, 256 + ci * 128:256 + (ci + 1) * 128], XX[:, cs_(ci)],
                             XX[:, 256 + ci * 128:256 + (ci + 1) * 128])
                    evac(Q.XX[(i + 1) % 2][:, :], g.t[:, :])
                else:
                    evac(Q.XX[(i + 1) % 2][:, 0:256], g.t[:, 0:256])
            yield
        Z = Q.Z[zi]
        g = P.bank()
        ic = CSI["ident"]
        for ci in range(2):
            P.mm(g, g.t[0:64, ci * 128:ci * 128 + 64], Z[:, ci * 128:ci * 128 + 64],
                 T.TK[ci][:, 128 + 64 * h:128 + 64 * h + 64], half=0)
            P.mm(g, g.t[0:64, ci * 128:ci * 128 + 64], k.cs[:, ic + 64 * h:ic + 64 * h + 64],
                 T.dgW[:, ci * 64:(ci + 1) * 64], half=0)
            P.mm(g, g.t[0:64, ci * 128 + 64:ci * 128 + 128], T.TK[ci][:, 128 + 64 * h:128 + 64 * h + 64],
                 Z[:, ci * 128 + 64:ci * 128 + 128], half=0)
            P.mm(g, g.t[0:64, ci * 128 + 64:ci * 128 + 128], T.TK[ci][:, 256 + 64 * h:256 + 64 * h + 64],
                 T.TK[ci][:, 64 * h:64 * h + 64], half=0)
        evac(Q.PG[0:64, :], g.t[0:64, 0:256])
        yield
        if outp:
            gq = P.bank()
            for ci in range(2):
                P.mm(gq, gq.t[0:64, cs_(ci)], Z[:, ci * 128:ci * 128 + 64], Q.ARBK[:, cs_(ci)], half=0)
                P.mm(gq, gq.t[0:64, cs_(ci)], k.identb[:, hs], T.Rt[:, cs_(ci)], half=0)
            evac(Q.QT[0:64, :], gq.t[0:64, 0:256])
            yield
            gy = P.bank()
        for ci in range(2):
            if outp:
                P.cp("pool", A.Hbf[hd][0:64, :], A.H[hd][0:64, :])
                P.mm(gy, gy.t[hs, cs_(ci)], Z[:, ci * 128 + 64:ci * 128 + 128], Q.ARBK[:, cs_(ci)], half=h)
                P.mm(gy, gy.t[hs, cs_(ci)], T.TK[ci][:, 64 * h:64 * h + 64], Q.ARBK[:, 256 + ci * 128:256 + (ci + 1) * 128], half=h)
                P.mm(gy, gy.t[hs, cs_(ci)], A.Hbf[hd][:, :], Q.QT[:, cs_(ci)], half=h)
            gs = P.bank()
            P.mm(gs, gs.t[0:64, 0:64], Q.PG[:, ci * 128:ci * 128 + 64], A.H[hd][:, :], half=0)
            P.tt("dve", A.H[hd][0:64, :], gs.t[0:64, 0:64], Q.PG[0:64, ci * 128 + 64:ci * 128 + 128], ALU.add)
        if outp:
            evac(T.yT[hs, :], gy.t[hs, 0:256])
        yield

    def post(bi, lc):
        cc = 4 * hp + lc
        T = A.T[(bi * 4 + lc) % 3]
        bonesf = cview(k, "bones", 128)
        bm = P.bank()
        P.mm(bm, bm.t[:, 0:nt], bonesf, T.yT[:, :])
        P.stt(T.dd[:, :], bm.t[:, 0:nt], -1.0 / 64, T.yT[:, :], ALU.mult, ALU.add)
        P.act(T.sqd[:, :], T.dd[:, :], AF.Square)
        yield
        bv = P.bank()
        P.mm(bv, bv.t[:, 0:nt], bonesf, T.sqd[:, :])
        rsqrt_act(P, T.rs2[:, :], bv.t[:, 0:nt], 1.0 / 64, 64e-5, T.ln[:, :])
        P.tt("dve", T.yn[:, :], T.dd[:, :], T.rs2[:, :], ALU.mult)
        P.ts("dve", T.t1[:, :], T.yn[:, :], pcol(k, "gnw", cc), ALU.mult, pcol(k, "gnb", cc), ALU.add)
        P.tt("pool", T.t2[:, :], T.t1[:, :], T.bonus[:, :], ALU.add)
        yield
        bg = P.bank()
        for j in range(2):
            P.mm(bg, bg.t[:, 0:nt], A.g2b[:, j, lc * 128:(lc + 1) * 128], A.sg_gd[bi % 2][:, j, :])
        if bi == BI_O:
            P.tt("dve", k.o_rw[:, cc, 0:128], T.t2[:, 128:256], bg.t[:, 128:256], ALU.mult)
        else:
            e0 = nt * bi - E0
            P.tt("dve", k.o_rw[:, cc, e0:e0 + nt], T.t2[:, :], bg.t[:, 0:nt], ALU.mult)
        yield

    ksteps = int(os.environ.get("KSTEPS", "1000000000"))
    stepc = [0]

    warm_bank = P.banks[7]
    nwarm = int(os.environ.get("KWARM", "0"))

    def keep_warm():
        o, l, r = warm_bank.t[:, 0:512].ap, k.identb[:, :].ap, A.W[:, 0, 0:512].ap
        for _ in range(nwarm):
            P.op("pe", lambda e: e.matmul(o, lhsT=l, rhs=r, start=True, stop=True, skip_group_check=True),
                 reads=[], writes=[])

    def run_all(gens):
        gens = list(gens)
        while gens:
            if len(gens) > 1:
                keep_warm()
            nxt = []
            for g_ in gens:
                if stepc[0] >= ksteps:
                    return
                stepc[0] += 1
                try:
                    next(g_)
                    nxt.append(g_)
                except StopIteration:
                    pass
            gens = nxt

    nblk = int(os.environ.get("KNBLK", str(NBA))) if k.dbg else NBA
    work = [(bi, lc) for bi in range(nblk) for lc in range(4)]

    def prep_full(w):
        bi, lc = w
        if lc == 0:
            yield from block_head(bi)
        yield from prep(bi, lc)

    def advance(g_):
        if stepc[0] >= ksteps:
            return False
        stepc[0] += 1
        try:
            next(g_)
            return True
        except StopIteration:
            return False

    def item_gen(wi):
        bi, lc = work[wi]
        sl = 2 * (wi % 2)
        hq = [hquad(bi, lc, 0, sl), hquad(bi, lc, 1, sl + 1)]
        while hq:
            hq = [g_ for g_ in hq if advance(g_)]
            yield
        if bi >= BI_O:
            yield from post(bi, lc)

    nw = len(work)
    prep_done = 0

    def run_prep(wi):
        for _ in prep_full(work[wi]):
            yield

    for wi in range(min(2, nw)):
        run_all([prep_full(work[wi])])
    prep_next = min(2, nw)
    active = []
    next_item = 0
    done = 0
    prep_gen = None
    prep_wi = None
    while done < nw:
        while len(active) < 2 and next_item < nw and next_item < prep_next and (prep_wi is None or prep_wi != next_item):
            active.append((next_item, item_gen(next_item)))
            next_item += 1
        if prep_gen is None and prep_next < nw and (prep_next - 3 < 0 or done > prep_next - 3):
            prep_gen = prep_full(work[prep_next])
            prep_wi = prep_next
        still = []
        for wi, g_ in active:
            if advance(g_):
                still.append((wi, g_))
            else:
                done += 1
        active = still
        if prep_gen is not None:
            if not advance(prep_gen):
                prep_gen = None
                prep_wi = None
                prep_next += 1
        if not active and prep_gen is None and next_item >= nw:
            break
        if stepc[0] >= ksteps:
            break
    P.bank = _orig_bank
    if k.dbg in ("A0", "A"):
        for i in range(8):
            dbg_out(k, f"H{hp}_{i}", A.H[i][0:64, :], [64, 64])
        if nblk > BI_O + 1:
            dbg_out(k, f"orw{hp}", k.o_rw[:, 4 * hp:4 * hp + 4, :], [128, 4, NE], BF16)
        dbg_out(k, f"ks{hp}", A.T[1].ks[:, :], [128, nt])


def rms_block_256(k, xb, sqr, rstd, tmpn, hT_out, t0, gname):
    P = k.P
    nt = 256
    src = k.xT.ap().rearrange("(k p) t -> p k t", p=128)[:, :, t0:t0 + nt]
    P.dma("sp", xb[:, :, :], View(k.d["xT"], src))
    bk = P.bank()
    for kc in range(NK):
        sq = sqr[kc % 2]
        P.act(sq[:, :], xb[:, kc, :], AF.Square)
        P.mm(bk, bk.t[:, 0:nt], k.onesb[:, :], sq[:, :])
    rsqrt_act(P, rstd[:, :], bk.t[:, 0:nt], 1.0 / D, 1e-6, tmpn[:, :])
    for kc in range(NK):
        P.stt(hT_out(kc), xb[:, kc, :], pcol(k, gname, kc), rstd[:, :], ALU.mult, ALU.mult)


def phase_b1(k, ph):
    P = k.P
    S = lambda shape, dt, name: P.sb(shape, dt, name, stack=ph)
    nt = 256
    bufA = S([128, NK, 512], BF16, "wbufA")
    bufB = S([128, NK, 512], BF16, "wbufB")
    xb = S([128, NK, nt], F32, "xbB")
    sqr = [S([128, nt], BF16, f"sqrB{i}") for i in range(2)]
    rstd = S([128, nt], F32, "rstdB")
    tmpn = S([128, nt], F32, "tmpnB")
    sqq = [S([128, nt], BF16, f"sqq{i}") for i in range(2)]
    rq = [S([128, nt], F32, f"rq{i}") for i in range(2)]
    lq = [S([128, nt], F32, f"lq{i}") for i in range(2)]
    load_w(k, bufA[:, :, 0:256], k.w_in, 0, NK, C_AK, 256)
    for bi in range(5):
        rms_block_256(k, xb, sqr, rstd, tmpn, lambda kc: k.hE[:, kc, bi * nt:(bi + 1) * nt], E1 + nt * bi, "n1g")
    for h in range(4):
        for dup in range(2):
            P.cp("pool" if dup else "act", bufB[:, :, h * 128 + 64 * dup:h * 128 + 64 * dup + 64],
                 bufA[:, :, h * 64:(h + 1) * 64])
    ii = [0]

    def qknorm(bank, out_view, gname, c0, c1):
        i2 = ii[0] % 2
        ii[0] += 1
        P.act(sqq[i2][:, :], bank.t[:, 0:nt], AF.Square)
        bs = P.bank()
        P.mm(bs, bs.t[:, 0:nt], k.bonesb[:, :], sqq[i2][:, :])
        rsqrt_act(P, rq[i2][:, :], bs.t[:, 0:nt], 1.0 / 64, 1e-6, lq[i2][:, :])
        P.stt(out_view, bank.t[:, c0:c1], pcol(k, gname), rq[i2][:, c0:c1], ALU.mult, ALU.mult)

    for bi in range(5):
        for h in range(4):
            bk = P.bank()
            for kc in range(NK):
                P.mm(bk, bk.t[:, 0:nt], bufB[:, kc, h * 128:(h + 1) * 128], k.hE[:, kc, bi * nt:(bi + 1) * nt])
            qknorm(bk, k.khat[:, h, bi * nt:(bi + 1) * nt], "kg", 0, nt)
    load_w(k, bufA[:, :, 0:256], k.w_in, 0, NK, C_AV, 256)
    for blk in range(10):
        bv = P.bank()
        for kc in range(NK):
            P.mm(bv, bv.t[:, 0:256], k.hE[:, kc, blk * 128:(blk + 1) * 128], bufA[:, kc, 0:256])
        P.cp("act" if blk % 2 else "dve", k.vtok[:, blk, :], bv.t[:, 0:256])
    for half in range(2):
        buf = bufB if half == 0 else bufA
        load_w(k, buf[:, :, :], k.w_in, 0, NK, C_AQ + 512 * half, 512)
        for bi in range(5):
            for q4 in range(4):
                qc = 4 * half + q4
                bq = P.bank()
                for kc in range(NK):
                    P.mm(bq, bq.t[:, 0:nt], buf[:, kc, q4 * 128:(q4 + 1) * 128], k.hE[:, kc, bi * nt:(bi + 1) * nt])
                if bi == 0:
                    qknorm(bq, k.qhat[:, qc, 0:128], "qg", 128, 256)
                else:
                    e0 = nt * bi - 128
                    qknorm(bq, k.qhat[:, qc, e0:e0 + nt], "qg", 0, nt)


def phase_b2(k, ph):
    P = k.P
    S = lambda shape, dt, name: P.sb(shape, dt, name, stack=ph)
    ab = S([128, 4096], F32, "abias")
    P.dma("sp", ab[:, :], k.d["abias"][:, :])
    sk = S([128, 8], F32, "sk")
    P.act(sk[:, :], k.ppt[:, PPI["sk"]:PPI["sk"] + 8], AF.Exp)
    oneslo = S([128, 128], BF16, "oneslo")
    oneshi = S([128, 128], BF16, "oneshi")
    rowlo = S([128, 1], F32, "rowlo")
    rowhi = S([128, 1], F32, "rowhi")
    for t_, lo in ((oneslo, True), (oneshi, False)):
        P.memset("pool", t_[:, :], 0.0)
        P.memset("pool", t_[:, 0:64] if lo else t_[:, 64:128], 1.0)
    P.memset("pool", rowlo[:, :], 0.0)
    P.memset("pool", rowhi[:, :], 0.0)
    P.memset("pool", rowlo[0:64, :], 1.0)
    P.memset("pool", rowhi[64:128, :], 1.0)
    NR = 3
    klo = [S([128, 128], BF16, f"klo{i}") for i in range(NR)]
    khi = [S([128, 128], BF16, f"khi{i}") for i in range(NR)]
    vlo = [S([128, 128], BF16, f"vlo{i}") for i in range(NR)]
    vhi = [S([128, 128], BF16, f"vhi{i}") for i in range(NR)]
    for i in range(NR):
        P.memset("pool", vlo[i][:, :], 0.0)
        P.memset("pool", vhi[i][:, :], 0.0)
    sbuf = [S([128, 512], F32, f"sb{i}") for i in range(2)]
    pb = [S([128, 512], BF16, f"pb{i}") for i in range(4)]
    dn = [S([128, 256], F32, f"dn{i}") for i in range(2)]
    it = 0
    vi = 0

    def variants(h, kb, slot):
        kcols = slice(kb * 128, (kb + 1) * 128)
        P.ts("pool", klo[slot][:, :], k.khat[:, h, kcols], rowlo[:, 0:1], ALU.mult)
        P.ts("pool", khi[slot][:, :], k.khat[:, h, kcols], rowhi[:, 0:1], ALU.mult)
        P.cp("pool", vlo[slot][:, 0:64], k.vtok[:, kb, h * 64:(h + 1) * 64])
        P.cp("pool", vhi[slot][:, 64:128], k.vtok[:, kb, h * 64:(h + 1) * 64])

    for h in range(4):
        variants(h, 0, vi % NR)
        prev_slot = vi % NR
        vi += 1
        for n in range(9):
            qcols = slice(n * 128, (n + 1) * 128)
            cur_slot = vi % NR
            vi += 1
            variants(h, n + 1, cur_slot)
            ps = []
            for which, slot in ((0, prev_slot), (1, cur_slot)):
                bs = P.bank()
                for g in range(4):
                    P.mm(bs, bs.t[:, g * 128:(g + 1) * 128], (klo if g % 2 == 0 else khi)[slot][:, :],
                         k.qhat[:, 2 * h + g // 2, qcols])
                s_ = sbuf[it % 2]
                p_ = pb[it % 4]
                it += 1
                o = (h * 2 + which) * 512
                P.stt(s_[:, :], bs.t[:, :], 0.125, ab[:, o:o + 512], ALU.mult, ALU.add)
                P.act(p_[:, :], s_[:, :], AF.Exp)
                if n == 1 and which == 0:
                    P.ts("pool", p_[:, :], p_[:, :], pcol(k, "prevmask"), ALU.mult)
                ps.append((p_, slot))
            bo = P.bank()
            bd = P.bank()
            for g2 in range(2):
                for (p_, slot) in ps:
                    for par in range(2):
                        g = 2 * g2 + par
                        P.mm(bo, bo.t[:, g2 * 128:(g2 + 1) * 128], (vlo if par == 0 else vhi)[slot][:, :],
                             p_[:, g * 128:(g + 1) * 128])
                        P.mm(bd, bd.t[:, g2 * 128:(g2 + 1) * 128], (oneslo if par == 0 else oneshi)[:, :],
                             p_[:, g * 128:(g + 1) * 128])
            d_ = dn[(n * 4 + h) % 2]
            for g2 in range(2):
                P.ts("dve", d_[:, g2 * 128:(g2 + 1) * 128], bd.t[:, g2 * 128:(g2 + 1) * 128],
                     sk[:, 2 * h + g2:2 * h + g2 + 1], ALU.add)
            d_ap = d_[:, :].ap
            P.op("dve", lambda e, d_ap=d_ap: e.reciprocal(out=d_ap, in_=d_ap), reads=[d_[:, :]], writes=[d_[:, :]])
            for g2 in range(2):
                P.tt("dve", k.o_att[:, 2 * h + g2, qcols], bo.t[:, g2 * 128:(g2 + 1) * 128],
                     d_[:, g2 * 128:(g2 + 1) * 128], ALU.mult)
            prev_slot = cur_slot
    if k.dbg == "B":
        dbg_out(k, "oatt", k.o_att[:, :, :], [128, 8, NE], BF16)
        dbg_out(k, "qhat", k.qhat[:, :, :], [128, 8, NE], BF16)
        dbg_out(k, "hE", k.hE[:, :, :], [128, NK, NE1], BF16)


def phase_c1(k, ph):
    P = k.P
    S = lambda shape, dt, name: P.sb(shape, dt, name, stack=ph)
    wb = [S([128, NK, 256], BF16, f"wb{i}") for i in range(2)]
    wgr = [S([128, NK, 256], BF16, f"wgr{i}") for i in range(2)]
    wga = [S([128, NK, 256], BF16, f"wga{i}") for i in range(2)]
    grw = [S([128, NT3], F32, f"grw{i}") for i in range(2)]
    gat = [S([128, NT3], F32, f"gat{i}") for i in range(2)]
    t1 = [S([128, NT3], F32, f"t1c{i}") for i in range(2)]
    t2 = [S([128, NT3], F32, f"t2c{i}") for i in range(2)]

    def load(g):
        i = g % 2
        load_w(k, wb[i][:, :, :], k.w_branch, 0, NK, g * 256, 256)
        load_w(k, wgr[i][:, :, :], k.w_in, 0, NK, C_GRW + g * 256, 256)
        load_w(k, wga[i][:, :, :], k.w_in, 0, NK, C_GAT + g * 256, 256)

    load(0)
    it = 0
    for g in range(8):
        if g + 1 < 8:
            load(g + 1)
        i = g % 2
        for dl in range(2):
            dm = 2 * g + dl
            wc = slice(dl * 128, (dl + 1) * 128)
            for tt in range(3):
                c0 = HC0 + NT3 * tt
                ec = slice(c0, c0 + NT3)
                hc = slice(c0 + 128, c0 + 128 + NT3)
                mc = slice(NT3 * tt, NT3 * (tt + 1))
                b1 = P.bank()
                for kc in range(NK):
                    P.mm(b1, b1.t[:, 0:NT3], wgr[i][:, kc, wc], k.hE[:, kc, hc])
                b2 = P.bank()
                for kc in range(NK):
                    P.mm(b2, b2.t[:, 0:NT3], wga[i][:, kc, wc], k.hE[:, kc, hc])
                b3 = P.bank()
                for cc in range(8):
                    P.mm(b3, b3.t[:, 0:NT3], wb[i][:, cc, wc], k.o_rw[:, cc, ec])
                b4 = P.bank()
                for cc in range(8):
                    P.mm(b4, b4.t[:, 0:NT3], wb[i][:, 8 + cc, wc], k.o_att[:, cc, ec])
                j = it % 2
                it += 1
                P.act(grw[j][:, :], b1.t[:, 0:NT3], AF.Sigmoid)
                P.act(gat[j][:, :], b2.t[:, 0:NT3], AF.Sigmoid)
                P.tt("dve", t1[j][:, :], b3.t[:, 0:NT3], grw[j][:, :], ALU.mult)
                P.tt("dve", t2[j][:, :], b4.t[:, 0:NT3], gat[j][:, :], ALU.mult)
                P.tt("pool", k.mT[:, dm, mc], t1[j][:, :], t2[j][:, :], ALU.add)
    if k.dbg == "C1":
        dbg_out(k, "mT", k.mT[:, :, :], [128, NK, 3 * NT3], BF16)


def phase_c2(k, ph):
    P = k.P
    S = lambda shape, dt, name: P.sb(shape, dt, name, stack=ph)
    wo = [S([128, NK, 256], BF16, f"wo{i}") for i in range(2)]
    xr = [S([128, 3 * NT3], F32, f"xr{i}") for i in range(2)]
    load_w(k, wo[0][:, :, :], k.w_out, 0, NK, 0, 256)
    xsrc = k.xT.ap().rearrange("(k p) t -> p k t", p=128)
    for g in range(8):
        if g + 1 < 8:
            load_w(k, wo[(g + 1) % 2][:, :, :], k.w_out, 0, NK, (g + 1) * 256, 256)
        for dl in range(2):
            dm2 = 2 * g + dl
            xx = xr[dm2 % 2]
            P.dma("sp", xx[:, :], View(k.d["xT"], xsrc[:, dm2, E0 + HC0:SEQ]))
            for tt in range(3):
                mc = slice(NT3 * tt, NT3 * (tt + 1))
                bk = P.bank()
                for dm in range(NK):
                    P.mm(bk, bk.t[:, 0:NT3], wo[g % 2][:, dm, dl * 128:(dl + 1) * 128], k.mT[:, dm, mc])
                P.tt("dve", k.x1T[:, dm2, mc], bk.t[:, 0:NT3], xx[:, mc], ALU.add)
    if k.dbg == "C2":
        dbg_out(k, "x1T", k.x1T[:, :, :], [128, NK, 3 * NT3])


def phase_d(k, ph):
    P = k.P
    S = lambda shape, dt, name: P.sb(shape, dt, name, stack=ph)
    NC = 3 * NT3
    h2T = S([128, NK, NC], BF16, "h2T")
    sqr = [S([128, NT3], BF16, f"sqrD{i}") for i in range(2)]
    rstd = S([128, NC], F32, "rstdD")
    tmpn = S([128, NT3], F32, "tmpnD")
    for tt in range(3):
        mc = slice(NT3 * tt, NT3 * (tt + 1))
        bk = P.bank()
        for kc in range(NK):
            sq = sqr[kc % 2]
            P.act(sq[:, :], k.x1T[:, kc, mc], AF.Square)
            P.mm(bk, bk.t[:, 0:NT3], k.onesb[:, :], sq[:, :])
        rsqrt_act(P, rstd[:, mc], bk.t[:, 0:NT3], 1.0 / D, 1e-6, tmpn[:, :])
    for kc in range(NK):
        P.stt(h2T[:, kc, :], k.x1T[:, kc, :], pcol(k, "n2g", kc), rstd[:, :], ALU.mult, ALU.mult)
    wv = [S([128, NK, 256], BF16, f"wv{i}") for i in range(2)]
    wg = [S([128, NK, 256], BF16, f"wg{i}") for i in range(2)]
    wd = [S([128, 2, D], BF16, f"wd{i}") for i in range(2)]
    aT = [S([128, 2, 1024], BF16, f"aT{i}") for i in range(2)]
    u = [S([128, NC], F32, f"u{i}") for i in range(2)]
    cv = S([128, 1024], F32, "cv")
    cg = S([128, 1024], F32, "cg")
    sg = S([128, 1024], F32, "sgD")
    NFG = NFC // 2
    wdsrc = k.w_down.ap().rearrange("(f p) d -> p f d", p=128)

    def load(fg):
        i = fg % 2
        load_w(k, wv[i][:, :, :], k.w_up, 0, NK, fg * 256, 256)
        load_w(k, wg[i][:, :, :], k.w_up, 0, NK, DFF + fg * 256, 256)
        P.dma("pool", wd[i][:, :, :], View(k.d["w_down"], wdsrc[:, 2 * fg:2 * fg + 2, :]))

    nfg = int(os.environ.get("KNFG", str(NFG))) if k.dbg else NFG
    ev = [0]

    def up(fg):
        i = fg % 2
        for fc in range(2):
            wc = slice(fc * 128, (fc + 1) * 128)
            for kind, W, ctile in ((0, wv[i], cv), (1, wg[i], cg)):
                ch = (0 if kind == 0 else NFC) + 2 * fg + fc
                uu = u[kind]
                for tt in range(3):
                    mc = slice(NT3 * tt, NT3 * (tt + 1))
                    bk = P.bank()
                    for kc in range(NK):
                        P.mm(bk, bk.t[:, 0:NT3], W[:, kc, wc], h2T[:, kc, mc])
                    ev[0] += 1
                    P.cp("act" if ev[0] % 3 else "dve", uu[:, mc], bk.t[:, 0:NT3])
                P.ts("pool", uu[:, 0:2], uu[:, 0:2], pcol(k, "convmask"), ALU.mult)
                P.act(ctile[:, :], uu[:, 2:NC], AF.Identity, bias=pcol(k, "cb", ch), scale=pcol(k, "cw2", ch))
                P.stt(ctile[:, :], uu[:, 1:NC - 1], pcol(k, "cw1", ch), ctile[:, :], ALU.mult, ALU.add)
                P.stt(ctile[:, :], uu[:, 0:NC - 2], pcol(k, "cw0", ch), ctile[:, :], ALU.mult, ALU.add)
            P.act(sg[:, :], cg[:, :], AF.Silu)
            P.tt("pool", aT[i][:, fc, :], sg[:, :], cv[:, :], ALU.mult)

    def down(fg):
        i = fg % 2
        for dm in range(NK):
            for t2 in range(2):
                bk = P.bank()
                for fc in range(2):
                    P.mm(bk, bk.t[:, 0:512], wd[i][:, fc, dm * 128:(dm + 1) * 128], aT[i][:, fc, t2 * 512:(t2 + 1) * 512])
                oc = slice(2 + t2 * 512, 2 + (t2 + 1) * 512)
                P.tt("dve", k.x1T[:, dm, oc], bk.t[:, 0:512], k.x1T[:, dm, oc], ALU.add)

    load(0)
    if nfg > 1:
        load(1)
    up(0)
    for fg in range(nfg):
        if fg + 1 < nfg:
            up(fg + 1)
        down(fg)
        if fg + 2 < nfg:
            load(fg + 2)
    dst = k.outT.ap().rearrange("(k p) t -> p k t", p=128)
    for half in range(2):
        P.dma("sp", View(k.d["outT"], dst[:, 8 * half:8 * half + 8, :]), k.x1T[:, 8 * half:8 * half + 8, 2:NC])
    k.final_views.append(k.d["outT"][:, :])


def finish(k):
    P = k.P
    if k.final_views:
        P.wait_all("sp", k.final_views)
    P.emit()


def make_inputs(inp):
    cs, ab = host_consts()
    x = np.asarray(inp["x"], np.float32)
    shared = {
        "w_in": np.ascontiguousarray(np.asarray(inp["w_in"][0], np.float32)),
        "w_branch": np.ascontiguousarray(np.asarray(inp["w_branch"][0], np.float32)),
        "w_out": np.ascontiguousarray(np.asarray(inp["w_out"][0], np.float32)),
        "w_up": np.ascontiguousarray(np.asarray(inp["w_up"][0], np.float32)),
        "w_down": np.ascontiguousarray(np.asarray(inp["w_down"][0], np.float32)),
        "rw_w2": np.ascontiguousarray(np.asarray(inp["rw_w2"][0], np.float32)),
        "rw_a2": np.ascontiguousarray(np.asarray(inp["rw_a2"][0], np.float32)),
        "rw_g2": np.ascontiguousarray(np.asarray(inp["rw_g2"][0], np.float32)),
        "cst": cs, "abias": ab,
    }
    if DBG in ("A0", "A", "IO"):
        for n in ("w_branch", "w_out", "w_up", "w_down"):
            shared[n] = np.zeros((128, 128), np.float32)
    maps = []
    for c in range(8):
        b, j = c // 4, c % 4
        n = 1024 * (j + 1)
        xT = np.zeros((D, SEQ), np.float32)
        xT[:, SEQ - n:] = x[b, :n].T
        m = dict(shared)
        m["xT"] = xT
        m["pp"] = host_params(inp, j)
        maps.append(m)
    return maps


def kernel(**inp):
    nc, k = build_program(DBG)
    maps = make_inputs(inp)
    res = run_bass_kernel_spmd(nc, maps, core_ids=list(range(8)))
    out = np.zeros((2, SEQ, D), np.float32)
    for c in range(8):
        b, j = c // 4, c % 4
        out[b, 1024 * j:1024 * (j + 1), :] = np.asarray(res.results[c]["outT"]).T
    kernel.last = (res, k)
    return out
```

```python
import os
import numpy as np
from contextlib import ExitStack
import concourse.bass as bass
import concourse.mybir as mybir
from concourse.bass_utils import run_bass_kernel_spmd

F32 = mybir.dt.float32
BF16 = mybir.dt.bfloat16
ALU = mybir.AluOpType
AF = mybir.ActivationFunctionType

ENGS = ["sp", "act", "pool", "pe", "dve"]


class Tile:
    def __init__(self, handle, name):
        self.h = handle
        self.name = name
        self.w = None
        self.rd = {}

    def __getitem__(self, idx):
        return View(self, self.h[idx])


class View:
    def __init__(self, tile, ap):
        self.tile = tile
        self.ap = ap


class DSem:
    def __init__(self, handle, idx):
        self.h = handle
        self.key = ("d", idx)
        self.count = 0


class Bank:
    def __init__(self, tile):
        self.t = tile
        self.fresh = [True, True]


class Prog:
    def __init__(self, nc, stack):
        self.nc = nc
        self.stack = stack
        self.rec = {e: [] for e in ENGS}
        self.clock = {e: {} for e in ENGS}
        self.nseq = {e: 0 for e in ENGS}
        self.esem = {e: stack.enter_context(nc.semaphore("es_" + e)) for e in ENGS}
        self.sembase = {e: 0 for e in ENGS}
        self.dsems = []
        self.snap = {}
        self.ntile = 0
        self.banks = []
        self.bank_rr = 0
        self.nblocks = 0
        self.tot = {e: [0, 0] for e in ENGS}

    def sb(self, shape, dt, name=None, stack=None, side=None):
        self.ntile += 1
        name = (name or "t") + f"_{self.ntile}"
        h = (stack or self.stack).enter_context(self.nc.sbuf_tensor(name, list(shape), dt, side=side))
        return Tile(h, name)

    def dram(self, handle, name):
        return Tile(handle, name)

    def dsem(self):
        h = self.stack.enter_context(self.nc.semaphore(f"ds{len(self.dsems)}"))
        d = DSem(h, len(self.dsems))
        self.dsems.append(d)
        return d

    def init_banks(self):
        for i in range(8):
            h = self.stack.enter_context(self.nc.psum_tensor(f"bank{i}", [128, 512], F32))
            self.banks.append(Bank(Tile(h, f"bank{i}")))

    def bank(self):
        b = self.banks[self.bank_rr % 8]
        self.bank_rr += 1
        b.fresh = [True, True]
        return b

    def _collect(self, eng, reads, writes, same_engine_ok=False):
        need = {}

        def add(tok):
            if tok is None:
                return
            k, v = tok
            if same_engine_ok and k == eng:
                return
            if need.get(k, 0) < v:
                need[k] = v

        for r in reads:
            add(r.tile.w)
        for w in writes:
            add(w.tile.w)
            for k, v in w.tile.rd.items():
                add((k, v))
        ck = self.clock[eng]
        waits = []
        for k, v in need.items():
            if ck.get(k, 0) >= v:
                continue
            waits.append((k, v))
            ck[k] = v
            sn = self.snap.get((k, v))
            if sn:
                for k2, v2 in sn.items():
                    if ck.get(k2, 0) < v2:
                        ck[k2] = v2
        return waits

    def op(self, eng, fn, reads=(), writes=()):
        waits = self._collect(eng, reads, writes, same_engine_ok=(eng in ("pe", "act", "dve")))
        self.nseq[eng] += 1
        s = self.nseq[eng]
        tok = (eng, s)
        self.snap[tok] = dict(self.clock[eng])
        for r in reads:
            t = r.tile
            if t.rd.get(eng, 0) < s:
                t.rd[eng] = s
        for w in writes:
            w.tile.w = tok
            w.tile.rd = {}
        self.rec[eng].append(("op", fn, waits, s))
        return tok

    def dma(self, eng, out, in_, dsem=None, **kw):
        if dsem is None:
            if not hasattr(out.tile, "dsem"):
                out.tile.dsem = self.dsem()
            dsem = out.tile.dsem
        waits = self._collect(eng, [in_], [out])
        dsem.count += 16
        tok = (dsem.key, dsem.count)
        self.snap[tok] = dict(self.clock[eng])
        t = in_.tile
        if t.rd.get(dsem.key, 0) < dsem.count:
            t.rd[dsem.key] = dsem.count
        out.tile.w = tok
        out.tile.rd = {}
        self.rec[eng].append(("dma", (out.ap, in_.ap, dsem, kw), waits, None))
        return tok

    def wait_all(self, eng, views):
        waits = self._collect(eng, views, [])
        self.rec[eng].append(("wait", None, waits, None))

    def emit(self):
        waited = {e: set() for e in ENGS}
        for e in ENGS:
            for kind, payload, waits, s in self.rec[e]:
                for k, v in waits:
                    if isinstance(k, str):
                        waited[k].add(v)
        semval = {}
        for e in ENGS:
            for i, s in enumerate(sorted(waited[e])):
                semval[(e, s)] = self.sembase[e] + i + 1
            self.sembase[e] += len(waited[e])
        dsem_by_key = {d.key: d for d in self.dsems}
        tot = self.tot

        def run(e, eobj):
            for kind, payload, waits, s in self.rec[e]:
                for k, v in waits:
                    if isinstance(k, str):
                        eobj.wait_ge(self.esem[k], semval[(k, v)])
                    else:
                        eobj.wait_ge(dsem_by_key[k].h, v)
                    tot[e][1] += 1
                if kind == "op":
                    ins = payload(eobj)
                    if (e, s) in semval:
                        ins.then_inc(self.esem[e], 1)
                    tot[e][0] += 1
                elif kind == "dma":
                    oap, iap, dsem, kw = payload
                    eobj.dma_start(out=oap, in_=iap, **kw).then_inc(dsem.h, 16)
                    tot[e][0] += 1

        self.nblocks += 1
        with self.nc.Block() as block:
            @block.sync
            def _(eng):
                run("sp", eng)

            @block.scalar
            def _(eng):
                run("act", eng)

            @block.gpsimd
            def _(eng):
                run("pool", eng)

            @block.tensor
            def _(eng):
                run("pe", eng)

            @block.vector
            def _(eng):
                run("dve", eng)
        self.rec = {e: [] for e in ENGS}
        for e in ENGS:
            for k in ENGS:
                self.clock[e][k] = self.nseq[k]

    def mm(self, bank, out, lhsT, rhs, half=None):
        if half is None:
            start = bank.fresh[0] or bank.fresh[1]
            bank.fresh = [False, False]
        else:
            start = bank.fresh[half]
            bank.fresh[half] = False
        o, l, r = out.ap, lhsT.ap, rhs.ap
        self.op("pe", lambda e: e.matmul(o, lhsT=l, rhs=r, start=start, stop=True, skip_group_check=True),
                reads=[lhsT, rhs], writes=[out])

    def act(self, out, in_, func, bias=None, scale=None):
        reads = [in_]
        kw = {}
        if bias is not None:
            if isinstance(bias, View):
                reads.append(bias)
                kw["bias"] = bias.ap
            else:
                kw["bias"] = float(bias)
        if scale is not None:
            if isinstance(scale, View):
                reads.append(scale)
                kw["scale"] = scale.ap
            else:
                kw["scale"] = float(scale)
        o, i = out.ap, in_.ap
        self.op("act", lambda e: e.activation(out=o, in_=i, func=func, **kw), reads=reads, writes=[out])

    def tt(self, eng, out, a, b, op):
        o, x, y = out.ap, a.ap, b.ap
        self.op(eng, lambda e: e.tensor_tensor(out=o, in0=x, in1=y, op=op), reads=[a, b], writes=[out])

    def ts(self, eng, out, a, s1, op0, s2=None, op1=None):
        reads = [a]
        v1 = s1
        if isinstance(s1, View):
            reads.append(s1)
            v1 = s1.ap
        v2 = s2
        if isinstance(s2, View):
            reads.append(s2)
            v2 = s2.ap
        o, x = out.ap, a.ap
        if op1 is None:
            self.op(eng, lambda e: e.tensor_scalar(out=o, in0=x, scalar1=v1, scalar2=None, op0=op0),
                    reads=reads, writes=[out])
        else:
            self.op(eng, lambda e: e.tensor_scalar(out=o, in0=x, scalar1=v1, scalar2=v2, op0=op0, op1=op1),
                    reads=reads, writes=[out])

    def stt(self, out, a, s, b, op0, op1):
        reads = [a, b]
        sv = s
        if isinstance(s, View):
            reads.append(s)
            sv = s.ap
        o, x, y = out.ap, a.ap, b.ap
        self.op("dve", lambda e: e.scalar_tensor_tensor(out=o, in0=x, scalar=sv, in1=y, op0=op0, op1=op1),
                reads=reads, writes=[out])

    def cp(self, eng, out, a):
        o, x = out.ap, a.ap
        if eng == "act":
            self.op("act", lambda e: e.activation(out=o, in_=x, func=AF.Copy), reads=[a], writes=[out])
        else:
            self.op(eng, lambda e: e.tensor_copy(out=o, in_=x), reads=[a], writes=[out])

    def memset(self, eng, out, val):
        o = out.ap
        self.op(eng, lambda e: e.memset(o, val), reads=[], writes=[out])


D = 2048
SEQ = 4096
NK = 16
RW = 1024
DFF = 5632
NFC = DFF // 128
CDEC = float(np.exp(-0.5))
E0 = 2944
NE = 1152
E1 = 2816
NE1 = 1280
HC0 = 126
NT3 = 342
C_R, C_K, C_V, C_WD, C_AD, C_GD = 0, 1024, 2048, 3072, 3168, 3264
C_AQ, C_AK, C_AV = 3520, 4544, 4800
C_GRW, C_GAT = 5056, 7104

PPI = {}
_o = 0
for _n, _c in [("n1g", 16), ("n2g", 16), ("mu_r", 8), ("mu_k", 8), ("mu_v", 8), ("mu_wd", 1), ("mu_ad", 1),
               ("mu_gd", 2), ("w0", 8), ("a0", 8), ("kk", 8), ("ka", 8), ("rk", 8), ("gnw", 8), ("gnb", 8),
               ("qg", 1), ("kg", 1), ("cw0", 88), ("cw1", 88), ("cw2", 88), ("cb", 88), ("sk", 8),
               ("prevmask", 1), ("convmask", 1)]:
    PPI[_n] = _o
    _o += _c
NPP = _o

CSI = {}
_o = 0
for _n, _c in [("ident", 128), ("i2", 64), ("m1", 256), ("bones", 128), ("rmask", 256), ("m13", 512), ("m22", 512),
               ("ones", 128)]:
    CSI[_n] = _o
    _o += _c
NCS = _o


def _chunked(v, ncol):
    return np.ascontiguousarray(np.asarray(v, np.float32).reshape(ncol, 128).T)


def host_consts():
    cs = np.zeros((128, NCS), np.float32)
    p = np.arange(128)[:, None]
    f = np.arange(128)[None, :]
    cs[:, CSI["ident"]:CSI["ident"] + 128] = (p == f)
    cs[:, CSI["i2"]:CSI["i2"] + 64] = ((p % 64) == np.arange(64)[None, :])
    m1 = (p < f).astype(np.float32)
    m2 = (p <= f).astype(np.float32)
    m3 = (f < p).astype(np.float32)
    cs[:, CSI["m1"]:CSI["m1"] + 256] = np.tile(m1, (1, 2))
    cs[:, CSI["m13"]:CSI["m13"] + 512] = np.concatenate([np.tile(m1, (1, 2)), np.tile(m3, (1, 2))], axis=1)
    cs[:, CSI["m22"]:CSI["m22"] + 512] = np.tile(m2, (1, 4))
    cs[:, CSI["bones"]:CSI["bones"] + 128] = ((p // 64) == (f // 64))
    rm = np.ones((128, 256), np.float32)
    rm[:, 0::128] = 0.0
    cs[:, CSI["rmask"]:CSI["rmask"] + 256] = rm
    cs[:, CSI["ones"]:CSI["ones"] + 128] = 1.0
    slopes = np.exp2(-8.0 * np.arange(1, 17, dtype=np.float32) / 16.0).astype(np.float32)
    ab = np.zeros((128, 4, 2, 4, 128), np.float32)
    k = np.arange(128)[:, None]
    q = np.arange(128)[None, :]
    for h in range(4):
        for g in range(4):
            sl = slopes[4 * h + g]
            dprev = (128 + q - k).astype(np.float32)
            dcur = (q - k).astype(np.float32)
            ab[:, h, 0, g, :] = np.where(dprev < 128, -sl * dprev, -30000.0)
            ab[:, h, 1, g, :] = np.where(dcur >= 0, -sl * dcur, -30000.0)
    return cs, ab.reshape(128, 4096)


def host_params(inp, j):
    pp = np.zeros((128, NPP), np.float32)

    def put(name, arr):
        arr = np.asarray(arr, np.float32)
        pp[:arr.shape[0], PPI[name]:PPI[name] + arr.shape[1]] = arr

    put("n1g", _chunked(inp["norm1_g"][0], 16))
    put("n2g", _chunked(inp["norm2_g"][0], 16))
    mu = np.asarray(inp["rw_mu"][0], np.float32)
    put("mu_r", _chunked(mu[0:1024], 8))
    put("mu_k", _chunked(mu[1024:2048], 8))
    put("mu_v", _chunked(mu[2048:3072], 8))
    put("mu_wd", mu[3072:3168].reshape(96, 1))
    put("mu_ad", mu[3168:3264].reshape(96, 1))
    put("mu_gd", _chunked(mu[3264:3520], 2))
    put("w0", _chunked(inp["rw_w0"][0], 8))
    put("a0", _chunked(inp["rw_a0"][0], 8))
    put("kk", _chunked(inp["rw_k_k"][0], 8))
    put("ka", _chunked(inp["rw_k_a"][0], 8))
    put("rk", _chunked(np.asarray(inp["rw_r_k"][0]).reshape(-1), 8))
    put("gnw", _chunked(inp["rw_gn_w"][0], 8))
    put("gnb", _chunked(inp["rw_gn_b"][0], 8))
    put("qg", np.tile(np.asarray(inp["q_norm_g"][0], np.float32), 2).reshape(128, 1))
    put("kg", np.tile(np.asarray(inp["k_norm_g"][0], np.float32), 2).reshape(128, 1))
    cw = np.asarray(inp["conv_w"][0], np.float32)
    put("cw0", _chunked(cw[0], 88))
    put("cw1", _chunked(cw[1], 88))
    put("cw2", _chunked(cw[2], 88))
    put("cb", _chunked(inp["conv_b"][0], 88))
    sk = np.asarray(inp["attn_sinks"][0], np.float32)
    put("sk", np.stack([sk[2 * c + (np.arange(128) // 64)] for c in range(8)], axis=1))
    put("prevmask", np.full((128, 1), 0.0 if j == 0 else 1.0, np.float32))
    put("convmask", np.full((128, 1), 0.0 if j == 0 else 1.0, np.float32))
    return pp


DBG = os.environ.get("KDBG", "")


class K:
    pass


def build_program(dbg=""):
    nc = bass.Bass("TRN2", target_bir_lowering=False)
    k = K()
    k.nc = nc
    dt_in = lambda name, shape: nc.dram_tensor(name, list(shape), F32, kind="ExternalInput")
    k.xT = dt_in("xT", [D, SEQ])
    k.w_in = dt_in("w_in", [D, 9152])
    small = dbg in ("A0", "A", "IO")
    k.w_branch = dt_in("w_branch", [D, D] if not small else [128, 128])
    k.w_out = dt_in("w_out", [D, D] if not small else [128, 128])
    k.w_up = dt_in("w_up", [D, 2 * DFF] if not small else [128, 128])
    k.w_down = dt_in("w_down", [DFF, D] if not small else [128, 128])
    k.w2 = dt_in("rw_w2", [96, RW])
    k.a2 = dt_in("rw_a2", [96, RW])
    k.g2 = dt_in("rw_g2", [256, RW])
    k.pp = dt_in("pp", [128, NPP])
    k.cst = dt_in("cst", [128, NCS])
    k.abias = dt_in("abias", [128, 4096])
    k.outT = nc.dram_tensor("outT", [D, 1024], F32, kind="ExternalOutput")
    k.dbg = dbg
    k.dbg_outs = {}
    with ExitStack() as st:
        P = Prog(nc, st)
        k.P = P
        P.init_banks()
        k.d = {n: P.dram(getattr(k, n), n) for n in
               ["xT", "w_in", "w_branch", "w_out", "w_up", "w_down", "w2", "a2", "g2", "pp", "cst", "abias", "outT"]}
        setup_persistent(k, st)
        R = lambda stack, shape, dt, name: P.sb(shape, dt, name, stack=stack, side="right")
        r_orw = ExitStack()
        k.o_rw = R(r_orw, [128, 8, NE], BF16, "o_rw")
        if dbg == "IO":
            dbg_out(k, "pp", k.ppt[:, :], [128, NPP])
        if dbg not in ("IO", "B"):
            for hp in range(2):
                with ExitStack() as ph:
                    phase_a(k, ph, hp)
                    P.emit()
                if dbg == "A0":
                    break
        if dbg not in ("A0", "A", "IO"):
            r_b = ExitStack()
            k.hE = R(r_b, [128, NK, NE1], BF16, "hE")
            k.o_att = R(r_b, [128, 8, NE], BF16, "o_att")
            r_b2 = ExitStack()
            k.qhat = R(r_b2, [128, 8, NE], BF16, "qhat")
            k.khat = R(r_b2, [128, 4, NE1], BF16, "khat")
            k.vtok = R(r_b2, [128, 10, 256], BF16, "vtok")
            with ExitStack() as ph:
                phase_b1(k, ph)
                P.emit()
            with ExitStack() as ph:
                phase_b2(k, ph)
                P.emit()
            r_b2.close()
            if dbg != "B":
                l_m = ExitStack()
                k.mT = P.sb([128, NK, 3 * NT3], BF16, "mT", stack=l_m)
                with ExitStack() as ph:
                    phase_c1(k, ph)
                    P.emit()
                r_b.close()
                r_orw.close()
                r_orw = None
                r_x1 = ExitStack()
                k.x1T = R(r_x1, [128, NK, 3 * NT3], F32, "x1T")
                with ExitStack() as ph:
                    phase_c2(k, ph)
                    P.emit()
                l_m.close()
                with ExitStack() as ph:
                    phase_d(k, ph)
                    P.emit()
                r_x1.close()
            else:
                r_b.close()
        if r_orw is not None:
            r_orw.close()
        finish(k)
        k.stats = P.tot
    return nc, k


def dbg_out(k, name, view, shape, dt=F32):
    P = k.P
    t = k.nc.dram_tensor("dbg_" + name, list(shape), dt, kind="ExternalOutput")
    k.dbg_outs[name] = t
    td = P.dram(t, "dbg_" + name)
    P.dma("sp", View(td, t.ap()), view)
    k.final_views.append(View(td, t.ap()))


def setup_persistent(k, st):
    P = k.P
    k.final_views = []
    k.ppt = P.sb([128, NPP], F32, "pp")
    k.cs = P.sb([128, NCS], F32, "cs")
    P.dma("sp", k.ppt[:, :], k.d["pp"][:, :])
    P.dma("sp", k.cs[:, :], k.d["cst"][:, :])
    k.identb = P.sb([128, 128], BF16, "identb")
    k.onesb = P.sb([128, 128], BF16, "onesb")
    k.bonesb = P.sb([128, 128], BF16, "bonesb")
    P.cp("dve", k.identb[:, :], k.cs[:, CSI["ident"]:CSI["ident"] + 128])
    P.cp("dve", k.onesb[:, :], k.cs[:, CSI["ones"]:CSI["ones"] + 128])
    P.cp("dve", k.bonesb[:, :], k.cs[:, CSI["bones"]:CSI["bones"] + 128])
    k.negb = P.sb([128, 16], F32, "negb")
    P.ts("dve", k.negb[:, :], k.ppt[:, PPI["w0"]:PPI["w0"] + 16], -1.0, ALU.mult)
    k.omu = P.sb([128, 28], F32, "omu")
    P.ts("dve", k.omu[:, :], k.ppt[:, PPI["mu_r"]:PPI["mu_r"] + 28], -1.0, ALU.mult, 1.0, ALU.add)


def pcol(k, name, c=0, rows=128):
    o = PPI[name] + c
    return k.ppt[0:rows, o:o + 1]


def omucol(k, name, c=0, rows=128):
    o = PPI[name] - PPI["mu_r"] + c
    return k.omu[0:rows, o:o + 1]


def cview(k, name, n):
    return k.cs[:, CSI[name]:CSI[name] + n]


def rsqrt_act(P, out, in_, scale, bias, tmp):
    P.act(tmp, in_, AF.Ln, bias=bias, scale=scale)
    P.act(out, tmp, AF.Exp, scale=-0.5)


def rmsnorm_block(k, ph_tiles, xb, hT, nt, gname):
    P = k.P
    sqb, rstd, tmp = ph_tiles
    P.act(sqb[:, :, 0:nt], xb[:, :, 0:nt], AF.Square)
    bk = P.bank()
    for kc in range(NK):
        P.mm(bk, bk.t[:, 0:nt], k.onesb[:, :], sqb[:, kc, 0:nt])
    rsqrt_act(P, rstd[:, 0:nt], bk.t[:, 0:nt], 1.0 / D, 1e-6, tmp[:, 0:nt])
    for kc in range(NK):
        P.stt(hT[:, kc, 0:nt], xb[:, kc, 0:nt], pcol(k, gname, kc), rstd[:, 0:nt], ALU.mult, ALU.mult)


def load_w(k, dst_view, src_handle, r0, nk, c0, ncol):
    P = k.P
    src = src_handle.ap().rearrange("(k p) c -> p k c", p=128)[:, r0:r0 + nk, c0:c0 + ncol]
    name = [n for n, t in k.d.items() if t.h is src_handle][0]
    P.dma("pool", dst_view, View(k.d[name], src))


class PA:
    pass


NTA = 256
NBA = SEQ // NTA
BI_R = 10
BI_O = 11


def proj(k, bank, W, col0, ncols, hT, nt):
    P = k.P
    for kc in range(NK):
        P.mm(bank, bank.t[0:ncols, 0:nt], W[:, kc, col0:col0 + ncols], hT[:, kc, 0:nt])


def proj_a(k, bank, W, col0, ncols, hTk, nt):
    P = k.P
    for kc in range(NK):
        P.mm(bank, bank.t[0:ncols, 0:nt], W[:, kc, col0:col0 + ncols], hTk[kc][:, 0:nt])


def shift(k, A, zps, rows, nt, mu, omu, carcol, out):
    P = k.P
    tmp = A.shtmp[A.shi % 3]
    A.shi += 1
    P.act(tmp[0:rows, 1:nt + 1], zps[0:rows, 0:nt], AF.Identity, scale=mu)
    P.cp("pool", tmp[0:rows, 0:1], A.car[0:rows, carcol:carcol + 1])
    P.cp("pool", A.car[0:rows, carcol:carcol + 1], tmp[0:rows, nt:nt + 1])
    P.stt(out, zps[0:rows, 0:nt], omu, tmp[0:rows, 0:nt], ALU.mult, ALU.add)


def phase_a(k, ph, hp):
    P = k.P
    A = PA()
    A.hp = hp
    nt = NTA
    S = lambda shape, dt, name: P.sb(shape, dt, name, stack=ph)
    A.W = S([128, NK, 1984], BF16, "WA")
    ch0 = 512 * hp
    load_w(k, A.W[:, :, 0:512], k.w_in, 0, NK, C_K + ch0, 512)
    load_w(k, A.W[:, :, 512:1024], k.w_in, 0, NK, C_V + ch0, 512)
    load_w(k, A.W[:, :, 1536:1984], k.w_in, 0, NK, C_WD, 448)
    load_w(k, A.W[:, :, 1024:1536], k.w_in, 0, NK, C_R + ch0, 512)
    A.w2b = S([128, 512], BF16, "w2b")
    A.a2b = S([128, 512], BF16, "a2b")
    A.g2b = S([128, 2, 512], BF16, "g2b")
    P.dma("pool", A.w2b[0:96, :], k.d["w2"][:, ch0:ch0 + 512])
    P.dma("pool", A.a2b[0:96, :], k.d["a2"][:, ch0:ch0 + 512])
    P.dma("pool", A.g2b[:, :, :], View(k.d["g2"], k.g2.ap().rearrange("(k p) c -> p k c", p=128)[:, :, ch0:ch0 + 512]))
    A.car = S([128, 16], F32, "car")
    P.memset("pool", A.car[:, :], 0.0)
    A.H = [S([128, 64], F32, f"H{i}") for i in range(8)]
    A.Hbf = [S([128, 64], BF16, f"Hbf{i}") for i in range(8)]
    for i in range(8):
        P.memset("pool", A.H[i][:, :], 0.0)
        P.memset("pool", A.Hbf[i][:, :], 0.0)
    A.xb = S([128, NK // 2, nt], F32, "xb")
    A.ssp = S([128, nt], F32, "ssp")
    A.sqr = [S([128, nt], BF16, f"sqr{i}") for i in range(2)]
    A.hTk = [S([128, nt], BF16, f"hT{i}") for i in range(NK)]
    A.rstd = S([128, nt], F32, "rstd")
    A.tmpn = S([128, nt], F32, "tmpn")
    A.shtmp = [S([128, nt + 1], F32, f"shtmp{i}") for i in range(3)]
    A.shi = 0
    A.scr = S([128, nt], F32, "scr")
    A.th_wd = S([128, nt], BF16, "th_wd")
    A.ad_s = S([128, nt], BF16, "ad_s")
    A.sg_gd = [S([128, 2, nt], BF16, f"sg_gd{i}") for i in range(2)]
    alias32 = {"ks": 0, "vs": 1, "rs": 2, "sg": 3, "cm": 3, "Em": 3, "aa": 4, "f": 4, "kp": 4, "cs": 5, "En": 6,
               "Ep": 7, "ssc": 8, "ln": 8, "rn": 8, "kkn": 9, "kba": 10, "bonus": 11, "yT": 12,
               "dd": 0, "sqd": 1, "rs2": 2, "yn": 3, "t1": 5, "t2": 6}
    alias16 = {"sq": 0, "rkb": 0, "At": 1, "Bt": 2, "Kt": 3, "Bh": 4, "Kh": 5, "Vb": 6, "Rt": 7}
    A.T = []
    for par in range(3):
        T = PA()
        b32 = [S([128, nt], F32, f"b32_{par}_{i}") for i in range(13)]
        b16 = [S([128, nt], BF16, f"b16_{par}_{i}") for i in range(8)]
        for n, i in alias32.items():
            setattr(T, n, b32[i])
        for n, i in alias16.items():
            setattr(T, n, b16[i])
        T.TK = [S([128, 384], BF16, f"TK{par}_{i}") for i in range(2)]
        T.EpL = S([128, 2], F32, f"EpL{par}")
        T.dgW = S([128, 128], F32, f"dgW{par}")
        A.T.append(T)
    A.Q = []
    for slot in range(4):
        Q = PA()
        Q.XX = [S([128, 512], BF16, f"XX{slot}_{i}") for i in range(2)]
        Q.Z = [S([128, 256], BF16, f"Z{slot}_{i}") for i in range(2)]
        Q.AAK = S([128, 256], BF16, f"AAK{slot}")
        Q.ARBK = S([128, 512], BF16, f"ARBK{slot}")
        Q.PG = S([128, 256], F32, f"PG{slot}")
        Q.QT = S([128, 256], BF16, f"QT{slot}")
        P.memset("pool", Q.PG[:, :], 0.0)
        P.memset("pool", Q.QT[:, :], 0.0)
        Q.acc = P.banks[slot]
        A.Q.append(Q)
    P.bank_rr = 0
    _orig_bank = P.bank

    def ring_bank():
        b = P.banks[4 + (P.bank_rr % 4)]
        P.bank_rr += 1
        b.fresh = [True, True]
        return b
    P.bank = ring_bank

    evi = [0]

    def evac(out, in_):
        evi[0] += 1
        P.cp("act" if evi[0] % 3 else "dve", out, in_)

    def recip(out, in_):
        o, i = out.ap, in_.ap
        P.op("dve", lambda e: e.reciprocal(out=o, in_=i), reads=[in_], writes=[out])

    def block_head(bi):
        t0 = nt * bi
        xsrc = k.xT.ap().rearrange("(k p) t -> p k t", p=128)

        def loadhalf(hf):
            P.dma("sp", A.xb[:, :, :], View(k.d["xT"], xsrc[:, 8 * hf:8 * hf + 8, t0:t0 + nt]))

        for hf in range(2):
            loadhalf(hf)
            yield
            bk = P.bank()
            for kl in range(8):
                sq = A.sqr[kl % 2]
                if kl % 2 == 0:
                    P.act(sq[:, :], A.xb[:, kl, :], AF.Square)
                else:
                    P.tt("dve", sq[:, :], A.xb[:, kl, :], A.xb[:, kl, :], ALU.mult)
                P.mm(bk, bk.t[:, 0:nt], k.onesb[:, :], sq[:, :])
            if hf == 0:
                P.cp("act", A.ssp[:, :], bk.t[:, 0:nt])
            else:
                P.tt("dve", A.ssp[:, :], bk.t[:, 0:nt], A.ssp[:, :], ALU.add)
            yield
        rsqrt_act(P, A.rstd[:, :], A.ssp[:, :], 1.0 / D, 1e-6, A.tmpn[:, :])
        for hf in range(2):
            loadhalf(hf)
            yield
            for kl in range(8):
                kc = 8 * hf + kl
                P.stt(A.hTk[kc][:, :], A.xb[:, kl, :], pcol(k, "n1g", kc), A.rstd[:, :], ALU.mult, ALU.mult)
                if kl % 4 == 3:
                    yield
        bk = P.bank()
        proj_a(k, bk, A.W, 1536, 96, A.hTk, nt)
        shift(k, A, bk.t, 96, nt, pcol(k, "mu_wd", 0, 96), omucol(k, "mu_wd", 0, 96), 12, A.scr[0:96, :])
        P.act(A.th_wd[0:96, :], A.scr[0:96, :], AF.Tanh)
        yield
        bk = P.bank()
        proj_a(k, bk, A.W, 1632, 96, A.hTk, nt)
        shift(k, A, bk.t, 96, nt, pcol(k, "mu_ad", 0, 96), omucol(k, "mu_ad", 0, 96), 13, A.ad_s[0:96, :])
        yield
        if bi >= BI_R:
            for j in range(2):
                bk = P.bank()
                proj_a(k, bk, A.W, 1728 + 128 * j, 128, A.hTk, nt)
                shift(k, A, bk.t, 128, nt, pcol(k, "mu_gd", j), omucol(k, "mu_gd", j), 14 + j, A.scr[:, :])
                P.act(A.sg_gd[bi % 2][:, j, :], A.scr[:, :], AF.Sigmoid)
                yield

    def prep(bi, lc):
        cc = 4 * hp + lc
        needR = bi >= BI_R
        outp = bi >= BI_O
        T = A.T[(bi * 4 + lc) % 3]
        bk = P.bank()
        proj_a(k, bk, A.W, lc * 128, 128, A.hTk, nt)
        shift(k, A, bk.t, 128, nt, pcol(k, "mu_k", cc), omucol(k, "mu_k", cc), lc, T.ks[:, :])
        yield
        bk = P.bank()
        proj_a(k, bk, A.W, 512 + lc * 128, 128, A.hTk, nt)
        shift(k, A, bk.t, 128, nt, pcol(k, "mu_v", cc), omucol(k, "mu_v", cc), 4 + lc, T.vs[:, :])
        yield
        if needR:
            bk = P.bank()
            proj_a(k, bk, A.W, 1024 + lc * 128, 128, A.hTk, nt)
            shift(k, A, bk.t, 128, nt, pcol(k, "mu_r", cc), omucol(k, "mu_r", cc), 8 + lc, T.rs[:, :])
            yield
        bw = P.bank()
        P.mm(bw, bw.t[:, 0:nt], A.w2b[0:96, lc * 128:(lc + 1) * 128], A.th_wd[0:96, :])
        P.act(T.sg[:, :], bw.t[:, 0:nt], AF.Sigmoid, bias=pcol(k, "w0", cc))
        ba = P.bank()
        P.mm(ba, ba.t[:, 0:nt], A.a2b[0:96, lc * 128:(lc + 1) * 128], A.ad_s[0:96, :])
        P.act(T.aa[:, :], ba.t[:, 0:nt], AF.Sigmoid, bias=pcol(k, "a0", cc))
        yield
        rm = cview(k, "rmask", nt)
        cs_ap, rm_ap, sg_ap = T.cs[:, :].ap, rm.ap, T.sg[:, :].ap
        P.op("dve", lambda e: e.tensor_tensor_scan(out=cs_ap, data0=rm_ap, data1=sg_ap, initial=0.0,
                                                   op0=ALU.mult, op1=ALU.add),
             reads=[rm, T.sg[:, :]], writes=[T.cs[:, :]])
        P.tt("pool", T.cm[:, :], T.cs[:, :], T.sg[:, :], ALU.subtract)
        P.act(T.En[:, :], T.cs[:, :], AF.Exp, scale=CDEC)
        P.act(T.Em[:, :], T.cm[:, :], AF.Exp, scale=-CDEC)
        for c in range(2):
            P.act(T.EpL[:, c:c + 1], T.cs[:, c * 128 + 127:c * 128 + 128], AF.Exp, scale=-CDEC)
        if outp:
            P.act(T.Ep[:, :], T.cs[:, :], AF.Exp, scale=-CDEC)
        yield
        P.act(T.sq[:, :], T.ks[:, :], AF.Square, scale=pcol(k, "kk", cc))
        bs = P.bank()
        P.mm(bs, bs.t[:, 0:nt], k.bonesb[:, :], T.sq[:, :])
        P.ts("dve", T.ssc[:, :], bs.t[:, 0:nt], 1e-24, ALU.max)
        P.act(T.ln[:, :], T.ssc[:, :], AF.Ln, scale=float(2.0 ** 40))
        P.act(T.rn[:, :], T.ln[:, :], AF.Exp, scale=-0.5)
        yield
        P.stt(T.kkn[:, :], T.ks[:, :], pcol(k, "kk", cc), T.rn[:, :], ALU.mult, ALU.mult)
        P.stt(T.At[:, :], T.kkn[:, :], -float(2.0 ** 20), T.Em[:, :], ALU.mult, ALU.mult)
        P.stt(T.kba[:, :], T.kkn[:, :], float(2.0 ** 20), T.aa[:, :], ALU.mult, ALU.mult)
        yield
        P.ts("dve", T.f[:, :], T.aa[:, :], -1.0, ALU.add, pcol(k, "ka", cc), ALU.mult)
        P.stt(T.kp[:, :], T.f[:, :], 1.0, T.ks[:, :], ALU.add, ALU.mult)
        P.tt("pool", T.Bt[:, :], T.kba[:, :], T.En[:, :], ALU.mult)
        P.tt("pool", T.Kt[:, :], T.kp[:, :], T.En[:, :], ALU.mult)
        P.cp("pool", T.Vb[:, :], T.vs[:, :])
        yield
        for c in range(2):
            cs_ = slice(c * 128, (c + 1) * 128)
            P.stt(T.Bh[:, cs_], T.kba[:, cs_], T.EpL[:, c:c + 1], T.En[:, cs_], ALU.mult, ALU.mult)
            P.stt(T.Kh[:, cs_], T.kp[:, cs_], T.EpL[:, c:c + 1], T.En[:, cs_], ALU.mult, ALU.mult)
            P.ts("dve", T.dgW[:, c * 64:(c + 1) * 64], cview(k, "i2", 64), T.EpL[:, c:c + 1], ALU.mult)
        yield
        if outp:
            P.tt("pool", T.Rt[:, :], T.rs[:, :], T.Ep[:, :], ALU.mult)
            P.stt(T.rkb[:, :], T.rs[:, :], pcol(k, "rk", cc), T.kp[:, :], ALU.mult, ALU.mult)
            bb = P.bank()
            P.mm(bb, bb.t[:, 0:nt], k.bonesb[:, :], T.rkb[:, :])
            P.tt("dve", T.bonus[:, :], bb.t[:, 0:nt], T.vs[:, :], ALU.mult)
            yield

    def hquad(bi, lc, h, slot):
        T = A.T[(bi * 4 + lc) % 3]
        Q = A.Q[slot]
        outp = bi >= BI_O
        m1 = cview(k, "m1", 256)
        hs = slice(64 * h, 64 * h + 64)
        hd = 2 * lc + h

        def cs_(ci):
            return slice(ci * 128, (ci + 1) * 128)
        m13 = cview(k, "m13", 512)
        m22 = cview(k, "m22", 512)
        g = P.bank()
        for ci in range(2):
            P.mm(g, g.t[:, cs_(ci)], T.Bt[hs, cs_(ci)], T.At[hs, cs_(ci)])
        for ci in range(2):
            P.mm(g, g.t[:, 256 + ci * 128:256 + (ci + 1) * 128], T.At[hs, cs_(ci)], T.Bt[hs, cs_(ci)])
        P.tt("dve", Q.XX[0][:, :], g.t[:, :], m13, ALU.mult)
        yield
        g = P.bank()
        for ci in range(2):
            P.mm(g, g.t[:, cs_(ci)], T.Kt[hs, cs_(ci)], T.At[hs, cs_(ci)])
        P.tt("dve", Q.AAK[:, :], g.t[:, 0:256], m1, ALU.mult)
        if outp:
            g = P.bank()
            for ci in range(2):
                P.mm(g, g.t[:, cs_(ci)], T.Bt[hs, cs_(ci)], T.Rt[hs, cs_(ci)])
            for ci in range(2):
                P.mm(g, g.t[:, 256 + ci * 128:256 + (ci + 1) * 128], T.Kt[hs, cs_(ci)], T.Rt[hs, cs_(ci)])
            P.tt("dve", Q.ARBK[:, :], g.t[:, :], m22, ALU.mult)
        yield
        if h == 0:
            for ci in range(2):
                g = P.bank()
                P.mm(g, g.t[:, 0:128], T.Vb[:, cs_(ci)], k.identb[:, :])
                P.mm(g, g.t[:, 128:256], T.Bh[:, cs_(ci)], k.identb[:, :])
                P.mm(g, g.t[:, 256:384], T.Kh[:, cs_(ci)], k.identb[:, :])
                evac(T.TK[ci][:, :], g.t[:, 0:384])
        yield
        acc = Q.acc
        acc.fresh = [True, True]
        for ci in range(2):
            P.mm(acc, acc.t[:, ci * 128:ci * 128 + 64], T.At[:, cs_(ci)], k.identb[:, hs])
            P.mm(acc, acc.t[:, ci * 128 + 64:ci * 128 + 128], Q.AAK[:, cs_(ci)], T.TK[ci][:, 64 * h:64 * h + 64])
        evac(Q.Z[0][:, :], acc.t[:, 0:256])
        yield
        zi = 0
        for i in range(7):
            XX = Q.XX[i % 2]
            for ci in range(2):
                P.mm(acc, acc.t[:, cs_(ci)], XX[:, cs_(ci)], Q.Z[zi][:, cs_(ci)])
            zi ^= 1
            evac(Q.Z[zi][:, :], acc.t[:, 0:256])
            if i < 6:
                g = P.bank()
                for ci in range(2):
                    P.mm(g, g.t[:, cs_(ci)], XX[:, 256 + ci * 128:256 + (ci + 1) * 128], XX[:, cs_(ci)])
                if i < 5:
                    for ci in range(2):
                        P.mm(g, g.t[:, 256 + ci * 128:256 + (ci + 1) * 128], XX[:, cs_(ci)],
                             XX[:, 256 + ci * 128:256 + (ci + 1) * 128])
                    evac(Q.XX[(i + 1) % 2][:, :], g.t[:, :])
                else:
                    evac(Q.XX[(i + 1) % 2][:, 0:256], g.t[:, 0:256])
            yield
        Z = Q.Z[zi]
        g = P.bank()
        ic = CSI["ident"]
        for ci in range(2):
            P.mm(g, g.t[0:64, ci * 128:ci * 128 + 64], Z[:, ci * 128:ci * 128 + 64],
                 T.TK[ci][:, 128 + 64 * h:128 + 64 * h + 64], half=0)
            P.mm(g, g.t[0:64, ci * 128:ci * 128 + 64], k.cs[:, ic + 64 * h:ic + 64 * h + 64],
                 T.dgW[:, ci * 64:(ci + 1) * 64], half=0)
            P.mm(g, g.t[0:64, ci * 128 + 64:ci * 128 + 128], T.TK[ci][:, 128 + 64 * h:128 + 64 * h + 64],
                 Z[:, ci * 128 + 64:ci * 128 + 128], half=0)
            P.mm(g, g.t[0:64, ci * 128 + 64:ci * 128 + 128], T.TK[ci][:, 256 + 64 * h:256 + 64 * h + 64],
                 T.TK[ci][:, 64 * h:64 * h + 64], half=0)
        evac(Q.PG[0:64, :], g.t[0:64, 0:256])
        yield
        if outp:
            gq = P.bank()
            for ci in range(2):
                P.mm(gq, gq.t[0:64, cs_(ci)], Z[:, ci * 128:ci * 128 + 64], Q.ARBK[:, cs_(ci)], half=0)
                P.mm(gq, gq.t[0:64, cs_(ci)], k.identb[:, hs], T.Rt[:, cs_(ci)], half=0)
            evac(Q.QT[0:64, :], gq.t[0:64, 0:256])
            yield
            gy = P.bank()
        for ci in range(2):
            if outp:
                P.cp("pool", A.Hbf[hd][0:64, :], A.H[hd][0:64, :])
                P.mm(gy, gy.t[hs, cs_(ci)], Z[:, ci * 128 + 64:ci * 128 + 128], Q.ARBK[:, cs_(ci)], half=h)
                P.mm(gy, gy.t[hs, cs_(ci)], T.TK[ci][:, 64 * h:64 * h + 64], Q.ARBK[:, 256 + ci * 128:256 + (ci + 1) * 128], half=h)
                P.mm(gy, gy.t[hs, cs_(ci)], A.Hbf[hd][:, :], Q.QT[:, cs_(ci)], half=h)
            gs = P.bank()
            P.mm(gs, gs.t[0:64, 0:64], Q.PG[:, ci * 128:ci * 128 + 64], A.H[hd][:, :], half=0)
            P.tt("dve", A.H[hd][0:64, :], gs.t[0:64, 0:64], Q.PG[0:64, ci * 128 + 64:ci * 128 + 128], ALU.add)
        if outp:
            evac(T.yT[hs, :], gy.t[hs, 0:256])
        yield

    def post(bi, lc):
        cc = 4 * hp + lc
        T = A.T[(bi * 4 + lc) % 3]
        bonesf = cview(k, "bones", 128)
        bm = P.bank()
        P.mm(bm, bm.t[:, 0:nt], bonesf, T.yT[:, :])
        P.stt(T.dd[:, :], bm.t[:, 0:nt], -1.0 / 64, T.yT[:, :], ALU.mult, ALU.add)
        P.act(T.sqd[:, :], T.dd[:, :], AF.Square)
        yield
        bv = P.bank()
        P.mm(bv, bv.t[:, 0:nt], bonesf, T.sqd[:, :])
        rsqrt_act(P, T.rs2[:, :], bv.t[:, 0:nt], 1.0 / 64, 64e-5, T.ln[:, :])
        P.tt("dve", T.yn[:, :], T.dd[:, :], T.rs2[:, :], ALU.mult)
        P.ts("dve", T.t1[:, :], T.yn[:, :], pcol(k, "gnw", cc), ALU.mult, pcol(k, "gnb", cc), ALU.add)
        P.tt("pool", T.t2[:, :], T.t1[:, :], T.bonus[:, :], ALU.add)
        yield
        bg = P.bank()
        for j in range(2):
            P.mm(bg, bg.t[:, 0:nt], A.g2b[:, j, lc * 128:(lc + 1) * 128], A.sg_gd[bi % 2][:, j, :])
        if bi == BI_O:
            P.tt("dve", k.o_rw[:, cc, 0:128], T.t2[:, 128:256], bg.t[:, 128:256], ALU.mult)
        else:
            e0 = nt * bi - E0
            P.tt("dve", k.o_rw[:, cc, e0:e0 + nt], T.t2[:, :], bg.t[:, 0:nt], ALU.mult)
        yield

    ksteps = int(os.environ.get("KSTEPS", "1000000000"))
    stepc = [0]

    warm_bank = P.banks[7]
    nwarm = int(os.environ.get("KWARM", "0"))

    def keep_warm():
        o, l, r = warm_bank.t[:, 0:512].ap, k.identb[:, :].ap, A.W[:, 0, 0:512].ap
        for _ in range(nwarm):
            P.op("pe", lambda e: e.matmul(o, lhsT=l, rhs=r, start=True, stop=True, skip_group_check=True),
                 reads=[], writes=[])

    def run_all(gens):
        gens = list(gens)
        while gens:
            if len(gens) > 1:
                keep_warm()
            nxt = []
            for g_ in gens:
                if stepc[0] >= ksteps:
                    return
                stepc[0] += 1
                try:
                    next(g_)
                    nxt.append(g_)
                except StopIteration:
                    pass
            gens = nxt

    nblk = int(os.environ.get("KNBLK", str(NBA))) if k.dbg else NBA
    work = [(bi, lc) for bi in range(nblk) for lc in range(4)]

    def prep_full(w):
        bi, lc = w
        if lc == 0:
            yield from block_head(bi)
        yield from prep(bi, lc)

    def advance(g_):
        if stepc[0] >= ksteps:
            return False
        stepc[0] += 1
        try:
            next(g_)
            return True
        except StopIteration:
            return False

    def item_gen(wi):
        bi, lc = work[wi]
        sl = 2 * (wi % 2)
        hq = [hquad(bi, lc, 0, sl), hquad(bi, lc, 1, sl + 1)]
        while hq:
            hq = [g_ for g_ in hq if advance(g_)]
            yield
        if bi >= BI_O:
            yield from post(bi, lc)

    nw = len(work)
    prep_done = 0

    def run_prep(wi):
        for _ in prep_full(work[wi]):
            yield

    for wi in range(min(2, nw)):
        run_all([prep_full(work[wi])])
    prep_next = min(2, nw)
    active = []
    next_item = 0
    done = 0
    prep_gen = None
    prep_wi = None
    while done < nw:
        while len(active) < 2 and next_item < nw and next_item < prep_next and (prep_wi is None or prep_wi != next_item):
            active.append((next_item, item_gen(next_item)))
            next_item += 1
        if prep_gen is None and prep_next < nw and (prep_next - 3 < 0 or done > prep_next - 3):
            prep_gen = prep_full(work[prep_next])
            prep_wi = prep_next
        still = []
        for wi, g_ in active:
            if advance(g_):
                still.append((wi, g_))
            else:
                done += 1
        active = still
        if prep_gen is not None:
            if not advance(prep_gen):
                prep_gen = None
                prep_wi = None
                prep_next += 1
        if not active and prep_gen is None and next_item >= nw:
            break
        if stepc[0] >= ksteps:
            break
    P.bank = _orig_bank
    if k.dbg in ("A0", "A"):
        for i in range(8):
            dbg_out(k, f"H{hp}_{i}", A.H[i][0:64, :], [64, 64])
        if nblk > BI_O + 1:
            dbg_out(k, f"orw{hp}", k.o_rw[:, 4 * hp:4 * hp + 4, :], [128, 4, NE], BF16)
        dbg_out(k, f"ks{hp}", A.T[1].ks[:, :], [128, nt])


def rms_block_256(k, xb, sqr, rstd, tmpn, hT_out, t0, gname):
    P = k.P
    nt = 256
    src = k.xT.ap().rearrange("(k p) t -> p k t", p=128)[:, :, t0:t0 + nt]
    P.dma("sp", xb[:, :, :], View(k.d["xT"], src))
    bk = P.bank()
    for kc in range(NK):
        sq = sqr[kc % 2]
        P.act(sq[:, :], xb[:, kc, :], AF.Square)
        P.mm(bk, bk.t[:, 0:nt], k.onesb[:, :], sq[:, :])
    rsqrt_act(P, rstd[:, :], bk.t[:, 0:nt], 1.0 / D, 1e-6, tmpn[:, :])
    for kc in range(NK):
        P.stt(hT_out(kc), xb[:, kc, :], pcol(k, gname, kc), rstd[:, :], ALU.mult, ALU.mult)


def phase_b1(k, ph):
    P = k.P
    S = lambda shape, dt, name: P.sb(shape, dt, name, stack=ph)
    nt = 256
    bufA = S([128, NK, 512], BF16, "wbufA")
    bufB = S([128, NK, 512], BF16, "wbufB")
    xb = S([128, NK, nt], F32, "xbB")
    sqr = [S([128, nt], BF16, f"sqrB{i}") for i in range(2)]
    rstd = S([128, nt], F32, "rstdB")
    tmpn = S([128, nt], F32, "tmpnB")
    sqq = [S([128, nt], BF16, f"sqq{i}") for i in range(2)]
    rq = [S([128, nt], F32, f"rq{i}") for i in range(2)]
    lq = [S([128, nt], F32, f"lq{i}") for i in range(2)]
    load_w(k, bufA[:, :, 0:256], k.w_in, 0, NK, C_AK, 256)
    for bi in range(5):
        rms_block_256(k, xb, sqr, rstd, tmpn, lambda kc: k.hE[:, kc, bi * nt:(bi + 1) * nt], E1 + nt * bi, "n1g")
    for h in range(4):
        for dup in range(2):
            P.cp("pool" if dup else "act", bufB[:, :, h * 128 + 64 * dup:h * 128 + 64 * dup + 64],
                 bufA[:, :, h * 64:(h + 1) * 64])
    ii = [0]

    def qknorm(bank, out_view, gname, c0, c1):
        i2 = ii[0] % 2
        ii[0] += 1
        P.act(sqq[i2][:, :], bank.t[:, 0:nt], AF.Square)
        bs = P.bank()
        P.mm(bs, bs.t[:, 0:nt], k.bonesb[:, :], sqq[i2][:, :])
        rsqrt_act(P, rq[i2][:, :], bs.t[:, 0:nt], 1.0 / 64, 1e-6, lq[i2][:, :])
        P.stt(out_view, bank.t[:, c0:c1], pcol(k, gname), rq[i2][:, c0:c1], ALU.mult, ALU.mult)

    for bi in range(5):
        for h in range(4):
            bk = P.bank()
            for kc in range(NK):
                P.mm(bk, bk.t[:, 0:nt], bufB[:, kc, h * 128:(h + 1) * 128], k.hE[:, kc, bi * nt:(bi + 1) * nt])
            qknorm(bk, k.khat[:, h, bi * nt:(bi + 1) * nt], "kg", 0, nt)
    load_w(k, bufA[:, :, 0:256], k.w_in, 0, NK, C_AV, 256)
    for blk in range(10):
        bv = P.bank()
        for kc in range(NK):
            P.mm(bv, bv.t[:, 0:256], k.hE[:, kc, blk * 128:(blk + 1) * 128], bufA[:, kc, 0:256])
        P.cp("act" if blk % 2 else "dve", k.vtok[:, blk, :], bv.t[:, 0:256])
    for half in range(2):
        buf = bufB if half == 0 else bufA
        load_w(k, buf[:, :, :], k.w_in, 0, NK, C_AQ + 512 * half, 512)
        for bi in range(5):
            for q4 in range(4):
                qc = 4 * half + q4
                bq = P.bank()
                for kc in range(NK):
                    P.mm(bq, bq.t[:, 0:nt], buf[:, kc, q4 * 128:(q4 + 1) * 128], k.hE[:, kc, bi * nt:(bi + 1) * nt])
                if bi == 0:
                    qknorm(bq, k.qhat[:, qc, 0:128], "qg", 128, 256)
                else:
                    e0 = nt * bi - 128
                    qknorm(bq, k.qhat[:, qc, e0:e0 + nt], "qg", 0, nt)


def phase_b2(k, ph):
    P = k.P
    S = lambda shape, dt, name: P.sb(shape, dt, name, stack=ph)
    ab = S([128, 4096], F32, "abias")
    P.dma("sp", ab[:, :], k.d["abias"][:, :])
    sk = S([128, 8], F32, "sk")
    P.act(sk[:, :], k.ppt[:, PPI["sk"]:PPI["sk"] + 8], AF.Exp)
    oneslo = S([128, 128], BF16, "oneslo")
    oneshi = S([128, 128], BF16, "oneshi")
    rowlo = S([128, 1], F32, "rowlo")
    rowhi = S([128, 1], F32, "rowhi")
    for t_, lo in ((oneslo, True), (oneshi, False)):
        P.memset("pool", t_[:, :], 0.0)
        P.memset("pool", t_[:, 0:64] if lo else t_[:, 64:128], 1.0)
    P.memset("pool", rowlo[:, :], 0.0)
    P.memset("pool", rowhi[:, :], 0.0)
    P.memset("pool", rowlo[0:64, :], 1.0)
    P.memset("pool", rowhi[64:128, :], 1.0)
    NR = 3
    klo = [S([128, 128], BF16, f"klo{i}") for i in range(NR)]
    khi = [S([128, 128], BF16, f"khi{i}") for i in range(NR)]
    vlo = [S([128, 128], BF16, f"vlo{i}") for i in range(NR)]
    vhi = [S([128, 128], BF16, f"vhi{i}") for i in range(NR)]
    for i in range(NR):
        P.memset("pool", vlo[i][:, :], 0.0)
        P.memset("pool", vhi[i][:, :], 0.0)
    sbuf = [S([128, 512], F32, f"sb{i}") for i in range(2)]
    pb = [S([128, 512], BF16, f"pb{i}") for i in range(4)]
    dn = [S([128, 256], F32, f"dn{i}") for i in range(2)]
    it = 0
    vi = 0

    def variants(h, kb, slot):
        kcols = slice(kb * 128, (kb + 1) * 128)
        P.ts("pool", klo[slot][:, :], k.khat[:, h, kcols], rowlo[:, 0:1], ALU.mult)
        P.ts("pool", khi[slot][:, :], k.khat[:, h, kcols], rowhi[:, 0:1], ALU.mult)
        P.cp("pool", vlo[slot][:, 0:64], k.vtok[:, kb, h * 64:(h + 1) * 64])
        P.cp("pool", vhi[slot][:, 64:128], k.vtok[:, kb, h * 64:(h + 1) * 64])

    for h in range(4):
        variants(h, 0, vi % NR)
        prev_slot = vi % NR
        vi += 1
        for n in range(9):
            qcols = slice(n * 128, (n + 1) * 128)
            cur_slot = vi % NR
            vi += 1
            variants(h, n + 1, cur_slot)
            ps = []
            for which, slot in ((0, prev_slot), (1, cur_slot)):
                bs = P.bank()
                for g in range(4):
                    P.mm(bs, bs.t[:, g * 128:(g + 1) * 128], (klo if g % 2 == 0 else khi)[slot][:, :],
                         k.qhat[:, 2 * h + g // 2, qcols])
                s_ = sbuf[it % 2]
                p_ = pb[it % 4]
                it += 1
                o = (h * 2 + which) * 512
                P.stt(s_[:, :], bs.t[:, :], 0.125, ab[:, o:o + 512], ALU.mult, ALU.add)
                P.act(p_[:, :], s_[:, :], AF.Exp)
                if n == 1 and which == 0:
                    P.ts("pool", p_[:, :], p_[:, :], pcol(k, "prevmask"), ALU.mult)
                ps.append((p_, slot))
            bo = P.bank()
            bd = P.bank()
            for g2 in range(2):
                for (p_, slot) in ps:
                    for par in range(2):
                        g = 2 * g2 + par
                        P.mm(bo, bo.t[:, g2 * 128:(g2 + 1) * 128], (vlo if par == 0 else vhi)[slot][:, :],
                             p_[:, g * 128:(g + 1) * 128])
                        P.mm(bd, bd.t[:, g2 * 128:(g2 + 1) * 128], (oneslo if par == 0 else oneshi)[:, :],
                             p_[:, g * 128:(g + 1) * 128])
            d_ = dn[(n * 4 + h) % 2]
            for g2 in range(2):
                P.ts("dve", d_[:, g2 * 128:(g2 + 1) * 128], bd.t[:, g2 * 128:(g2 + 1) * 128],
                     sk[:, 2 * h + g2:2 * h + g2 + 1], ALU.add)
            d_ap = d_[:, :].ap
            P.op("dve", lambda e, d_ap=d_ap: e.reciprocal(out=d_ap, in_=d_ap), reads=[d_[:, :]], writes=[d_[:, :]])
            for g2 in range(2):
                P.tt("dve", k.o_att[:, 2 * h + g2, qcols], bo.t[:, g2 * 128:(g2 + 1) * 128],
                     d_[:, g2 * 128:(g2 + 1) * 128], ALU.mult)
            prev_slot = cur_slot
    if k.dbg == "B":
        dbg_out(k, "oatt", k.o_att[:, :, :], [128, 8, NE], BF16)
        dbg_out(k, "qhat", k.qhat[:, :, :], [128, 8, NE], BF16)
        dbg_out(k, "hE", k.hE[:, :, :], [128, NK, NE1], BF16)


def phase_c1(k, ph):
    P = k.P
    S = lambda shape, dt, name: P.sb(shape, dt, name, stack=ph)
    wb = [S([128, NK, 256], BF16, f"wb{i}") for i in range(2)]
    wgr = [S([128, NK, 256], BF16, f"wgr{i}") for i in range(2)]
    wga = [S([128, NK, 256], BF16, f"wga{i}") for i in range(2)]
    grw = [S([128, NT3], F32, f"grw{i}") for i in range(2)]
    gat = [S([128, NT3], F32, f"gat{i}") for i in range(2)]
    t1 = [S([128, NT3], F32, f"t1c{i}") for i in range(2)]
    t2 = [S([128, NT3], F32, f"t2c{i}") for i in range(2)]

    def load(g):
        i = g % 2
        load_w(k, wb[i][:, :, :], k.w_branch, 0, NK, g * 256, 256)
        load_w(k, wgr[i][:, :, :], k.w_in, 0, NK, C_GRW + g * 256, 256)
        load_w(k, wga[i][:, :, :], k.w_in, 0, NK, C_GAT + g * 256, 256)

    load(0)
    it = 0
    for g in range(8):
        if g + 1 < 8:
            load(g + 1)
        i = g % 2
        for dl in range(2):
            dm = 2 * g + dl
            wc = slice(dl * 128, (dl + 1) * 128)
            for tt in range(3):
                c0 = HC0 + NT3 * tt
                ec = slice(c0, c0 + NT3)
                hc = slice(c0 + 128, c0 + 128 + NT3)
                mc = slice(NT3 * tt, NT3 * (tt + 1))
                b1 = P.bank()
                for kc in range(NK):
                    P.mm(b1, b1.t[:, 0:NT3], wgr[i][:, kc, wc], k.hE[:, kc, hc])
                b2 = P.bank()
                for kc in range(NK):
                    P.mm(b2, b2.t[:, 0:NT3], wga[i][:, kc, wc], k.hE[:, kc, hc])
                b3 = P.bank()
                for cc in range(8):
                    P.mm(b3, b3.t[:, 0:NT3], wb[i][:, cc, wc], k.o_rw[:, cc, ec])
                b4 = P.bank()
                for cc in range(8):
                    P.mm(b4, b4.t[:, 0:NT3], wb[i][:, 8 + cc, wc], k.o_att[:, cc, ec])
                j = it % 2
                it += 1
                P.act(grw[j][:, :], b1.t[:, 0:NT3], AF.Sigmoid)
                P.act(gat[j][:, :], b2.t[:, 0:NT3], AF.Sigmoid)
                P.tt("dve", t1[j][:, :], b3.t[:, 0:NT3], grw[j][:, :], ALU.mult)
                P.tt("dve", t2[j][:, :], b4.t[:, 0:NT3], gat[j][:, :], ALU.mult)
                P.tt("pool", k.mT[:, dm, mc], t1[j][:, :], t2[j][:, :], ALU.add)
    if k.dbg == "C1":
        dbg_out(k, "mT", k.mT[:, :, :], [128, NK, 3 * NT3], BF16)


def phase_c2(k, ph):
    P = k.P
    S = lambda shape, dt, name: P.sb(shape, dt, name, stack=ph)
    wo = [S([128, NK, 256], BF16, f"wo{i}") for i in range(2)]
    xr = [S([128, 3 * NT3], F32, f"xr{i}") for i in range(2)]
    load_w(k, wo[0][:, :, :], k.w_out, 0, NK, 0, 256)
    xsrc = k.xT.ap().rearrange("(k p) t -> p k t", p=128)
    for g in range(8):
        if g + 1 < 8:
            load_w(k, wo[(g + 1) % 2][:, :, :], k.w_out, 0, NK, (g + 1) * 256, 256)
        for dl in range(2):
            dm2 = 2 * g + dl
            xx = xr[dm2 % 2]
            P.dma("sp", xx[:, :], View(k.d["xT"], xsrc[:, dm2, E0 + HC0:SEQ]))
            for tt in range(3):
                mc = slice(NT3 * tt, NT3 * (tt + 1))
                bk = P.bank()
                for dm in range(NK):
                    P.mm(bk, bk.t[:, 0:NT3], wo[g % 2][:, dm, dl * 128:(dl + 1) * 128], k.mT[:, dm, mc])
                P.tt("dve", k.x1T[:, dm2, mc], bk.t[:, 0:NT3], xx[:, mc], ALU.add)
    if k.dbg == "C2":
        dbg_out(k, "x1T", k.x1T[:, :, :], [128, NK, 3 * NT3])


def phase_d(k, ph):
    P = k.P
    S = lambda shape, dt, name: P.sb(shape, dt, name, stack=ph)
    NC = 3 * NT3
    h2T = S([128, NK, NC], BF16, "h2T")
    sqr = [S([128, NT3], BF16, f"sqrD{i}") for i in range(2)]
    rstd = S([128, NC], F32, "rstdD")
    tmpn = S([128, NT3], F32, "tmpnD")
    for tt in range(3):
        mc = slice(NT3 * tt, NT3 * (tt + 1))
        bk = P.bank()
        for kc in range(NK):
            sq = sqr[kc % 2]
            P.act(sq[:, :], k.x1T[:, kc, mc], AF.Square)
            P.mm(bk, bk.t[:, 0:NT3], k.onesb[:, :], sq[:, :])
        rsqrt_act(P, rstd[:, mc], bk.t[:, 0:NT3], 1.0 / D, 1e-6, tmpn[:, :])
    for kc in range(NK):
        P.stt(h2T[:, kc, :], k.x1T[:, kc, :], pcol(k, "n2g", kc), rstd[:, :], ALU.mult, ALU.mult)
    wv = [S([128, NK, 256], BF16, f"wv{i}") for i in range(2)]
    wg = [S([128, NK, 256], BF16, f"wg{i}") for i in range(2)]
    wd = [S([128, 2, D], BF16, f"wd{i}") for i in range(2)]
    aT = [S([128, 2, 1024], BF16, f"aT{i}") for i in range(2)]
    u = [S([128, NC], F32, f"u{i}") for i in range(2)]
    cv = S([128, 1024], F32, "cv")
    cg = S([128, 1024], F32, "cg")
    sg = S([128, 1024], F32, "sgD")
    NFG = NFC // 2
    wdsrc = k.w_down.ap().rearrange("(f p) d -> p f d", p=128)

    def load(fg):
        i = fg % 2
        load_w(k, wv[i][:, :, :], k.w_up, 0, NK, fg * 256, 256)
        load_w(k, wg[i][:, :, :], k.w_up, 0, NK, DFF + fg * 256, 256)
        P.dma("pool", wd[i][:, :, :], View(k.d["w_down"], wdsrc[:, 2 * fg:2 * fg + 2, :]))

    nfg = int(os.environ.get("KNFG", str(NFG))) if k.dbg else NFG
    ev = [0]

    def up(fg):
        i = fg % 2
        for fc in range(2):
            wc = slice(fc * 128, (fc + 1) * 128)
            for kind, W, ctile in ((0, wv[i], cv), (1, wg[i], cg)):
                ch = (0 if kind == 0 else NFC) + 2 * fg + fc
                uu = u[kind]
                for tt in range(3):
                    mc = slice(NT3 * tt, NT3 * (tt + 1))
                    bk = P.bank()
                    for kc in range(NK):
                        P.mm(bk, bk.t[:, 0:NT3], W[:, kc, wc], h2T[:, kc, mc])
                    ev[0] += 1
                    P.cp("act" if ev[0] % 3 else "dve", uu[:, mc], bk.t[:, 0:NT3])
                P.ts("pool", uu[:, 0:2], uu[:, 0:2], pcol(k, "convmask"), ALU.mult)
                P.act(ctile[:, :], uu[:, 2:NC], AF.Identity, bias=pcol(k, "cb", ch), scale=pcol(k, "cw2", ch))
                P.stt(ctile[:, :], uu[:, 1:NC - 1], pcol(k, "cw1", ch), ctile[:, :], ALU.mult, ALU.add)
                P.stt(ctile[:, :], uu[:, 0:NC - 2], pcol(k, "cw0", ch), ctile[:, :], ALU.mult, ALU.add)
            P.act(sg[:, :], cg[:, :], AF.Silu)
            P.tt("pool", aT[i][:, fc, :], sg[:, :], cv[:, :], ALU.mult)

    def down(fg):
        i = fg % 2
        for dm in range(NK):
            for t2 in range(2):
                bk = P.bank()
                for fc in range(2):
                    P.mm(bk, bk.t[:, 0:512], wd[i][:, fc, dm * 128:(dm + 1) * 128], aT[i][:, fc, t2 * 512:(t2 + 1) * 512])
                oc = slice(2 + t2 * 512, 2 + (t2 + 1) * 512)
                P.tt("dve", k.x1T[:, dm, oc], bk.t[:, 0:512], k.x1T[:, dm, oc], ALU.add)

    load(0)
    if nfg > 1:
        load(1)
    up(0)
    for fg in range(nfg):
        if fg + 1 < nfg:
            up(fg + 1)
        down(fg)
        if fg + 2 < nfg:
            load(fg + 2)
    dst = k.outT.ap().rearrange("(k p) t -> p k t", p=128)
    for half in range(2):
        P.dma("sp", View(k.d["outT"], dst[:, 8 * half:8 * half + 8, :]), k.x1T[:, 8 * half:8 * half + 8, 2:NC])
    k.final_views.append(k.d["outT"][:, :])


def finish(k):
    P = k.P
    if k.final_views:
        P.wait_all("sp", k.final_views)
    P.emit()


def make_inputs(inp):
    cs, ab = host_consts()
    x = np.asarray(inp["x"], np.float32)
    shared = {
        "w_in": np.ascontiguousarray(np.asarray(inp["w_in"][0], np.float32)),
        "w_branch": np.ascontiguousarray(np.asarray(inp["w_branch"][0], np.float32)),
        "w_out": np.ascontiguousarray(np.asarray(inp["w_out"][0], np.float32)),
        "w_up": np.ascontiguousarray(np.asarray(inp["w_up"][0], np.float32)),
        "w_down": np.ascontiguousarray(np.asarray(inp["w_down"][0], np.float32)),
        "rw_w2": np.ascontiguousarray(np.asarray(inp["rw_w2"][0], np.float32)),
        "rw_a2": np.ascontiguousarray(np.asarray(inp["rw_a2"][0], np.float32)),
        "rw_g2": np.ascontiguousarray(np.asarray(inp["rw_g2"][0], np.float32)),
        "cst": cs, "abias": ab,
    }
    if DBG in ("A0", "A", "IO"):
        for n in ("w_branch", "w_out", "w_up", "w_down"):
            shared[n] = np.zeros((128, 128), np.float32)
    maps = []
    for c in range(8):
        b, j = c // 4, c % 4
        n = 1024 * (j + 1)
        xT = np.zeros((D, SEQ), np.float32)
        xT[:, SEQ - n:] = x[b, :n].T
        m = dict(shared)
        m["xT"] = xT
        m["pp"] = host_params(inp, j)
        maps.append(m)
    return maps


def kernel(**inp):
    nc, k = build_program(DBG)
    maps = make_inputs(inp)
    res = run_bass_kernel_spmd(nc, maps, core_ids=list(range(8)))
    out = np.zeros((2, SEQ, D), np.float32)
    for c in range(8):
        b, j = c // 4, c % 4
        out[b, 1024 * j:1024 * (j + 1), :] = np.asarray(res.results[c]["outT"]).T
    kernel.last = (res, k)
    return out
```

```python
import os
import numpy as np
from contextlib import ExitStack
import concourse.bass as bass
import concourse.mybir as mybir
from concourse.bass_utils import run_bass_kernel_spmd

F32 = mybir.dt.float32
BF16 = mybir.dt.bfloat16
ALU = mybir.AluOpType
AF = mybir.ActivationFunctionType

ENGS = ["sp", "act", "pool", "pe", "dve"]


class Tile:
    def __init__(self, handle, name):
        self.h = handle
        self.name = name
        self.w = None
        self.rd = {}

    def __getitem__(self, idx):
        return View(self, self.h[idx])


class View:
    def __init__(self, tile, ap):
        self.tile = tile
        self.ap = ap


class DSem:
    def __init__(self, handle, idx):
        self.h = handle
        self.key = ("d", idx)
        self.count = 0


class Bank:
    def __init__(self, tile):
        self.t = tile
        self.fresh = [True, True]


class Prog:
    def __init__(self, nc, stack):
        self.nc = nc
        self.stack = stack
        self.rec = {e: [] for e in ENGS}
        self.clock = {e: {} for e in ENGS}
        self.nseq = {e: 0 for e in ENGS}
        self.esem = {e: stack.enter_context(nc.semaphore("es_" + e)) for e in ENGS}
        self.sembase = {e: 0 for e in ENGS}
        self.dsems = []
        self.snap = {}
        self.ntile = 0
        self.banks = []
        self.bank_rr = 0
        self.nblocks = 0
        self.tot = {e: [0, 0] for e in ENGS}

    def sb(self, shape, dt, name=None, stack=None, side=None):
        self.ntile += 1
        name = (name or "t") + f"_{self.ntile}"
        h = (stack or self.stack).enter_context(self.nc.sbuf_tensor(name, list(shape), dt, side=side))
        return Tile(h, name)

    def dram(self, handle, name):
        return Tile(handle, name)

    def dsem(self):
        h = self.stack.enter_context(self.nc.semaphore(f"ds{len(self.dsems)}"))
        d = DSem(h, len(self.dsems))
        self.dsems.append(d)
        return d

    def init_banks(self):
        for i in range(8):
            h = self.stack.enter_context(self.nc.psum_tensor(f"bank{i}", [128, 512], F32))
            self.banks.append(Bank(Tile(h, f"bank{i}")))

    def bank(self):
        b = self.banks[self.bank_rr % 8]
        self.bank_rr += 1
        b.fresh = [True, True]
        return b

    def _collect(self, eng, reads, writes, same_engine_ok=False):
        need = {}

        def add(tok):
            if tok is None:
                return
            k, v = tok
            if same_engine_ok and k == eng:
                return
            if need.get(k, 0) < v:
                need[k] = v

        for r in reads:
            add(r.tile.w)
        for w in writes:
            add(w.tile.w)
            for k, v in w.tile.rd.items():
                add((k, v))
        ck = self.clock[eng]
        waits = []
        for k, v in need.items():
            if ck.get(k, 0) >= v:
                continue
            waits.append((k, v))
            ck[k] = v
            sn = self.snap.get((k, v))
            if sn:
                for k2, v2 in sn.items():
                    if ck.get(k2, 0) < v2:
                        ck[k2] = v2
        return waits

    def op(self, eng, fn, reads=(), writes=()):
        waits = self._collect(eng, reads, writes, same_engine_ok=(eng in ("pe", "act", "dve")))
        self.nseq[eng] += 1
        s = self.nseq[eng]
        tok = (eng, s)
        self.snap[tok] = dict(self.clock[eng])
        for r in reads:
            t = r.tile
            if t.rd.get(eng, 0) < s:
                t.rd[eng] = s
        for w in writes:
            w.tile.w = tok
            w.tile.rd = {}
        self.rec[eng].append(("op", fn, waits, s))
        return tok

    def dma(self, eng, out, in_, dsem=None, **kw):
        if dsem is None:
            if not hasattr(out.tile, "dsem"):
                out.tile.dsem = self.dsem()
            dsem = out.tile.dsem
        waits = self._collect(eng, [in_], [out])
        dsem.count += 16
        tok = (dsem.key, dsem.count)
        self.snap[tok] = dict(self.clock[eng])
        t = in_.tile
        if t.rd.get(dsem.key, 0) < dsem.count:
            t.rd[dsem.key] = dsem.count
        out.tile.w = tok
        out.tile.rd = {}
        self.rec[eng].append(("dma", (out.ap, in_.ap, dsem, kw), waits, None))
        return tok

    def wait_all(self, eng, views):
        waits = self._collect(eng, views, [])
        self.rec[eng].append(("wait", None, waits, None))

    def emit(self):
        waited = {e: set() for e in ENGS}
        for e in ENGS:
            for kind, payload, waits, s in self.rec[e]:
                for k, v in waits:
                    if isinstance(k, str):
                        waited[k].add(v)
        semval = {}
        for e in ENGS:
            for i, s in enumerate(sorted(waited[e])):
                semval[(e, s)] = self.sembase[e] + i + 1
            self.sembase[e] += len(waited[e])
        dsem_by_key = {d.key: d for d in self.dsems}
        tot = self.tot

        def run(e, eobj):
            for kind, payload, waits, s in self.rec[e]:
                for k, v in waits:
                    if isinstance(k, str):
                        eobj.wait_ge(self.esem[k], semval[(k, v)])
                    else:
                        eobj.wait_ge(dsem_by_key[k].h, v)
                    tot[e][1] += 1
                if kind == "op":
                    ins = payload(eobj)
                    if (e, s) in semval:
                        ins.then_inc(self.esem[e], 1)
                    tot[e][0] += 1
                elif kind == "dma":
                    oap, iap, dsem, kw = payload
                    eobj.dma_start(out=oap, in_=iap, **kw).then_inc(dsem.h, 16)
                    tot[e][0] += 1

        self.nblocks += 1
        with self.nc.Block() as block:
            @block.sync
            def _(eng):
                run("sp", eng)

            @block.scalar
            def _(eng):
                run("act", eng)

            @block.gpsimd
            def _(eng):
                run("pool", eng)

            @block.tensor
            def _(eng):
                run("pe", eng)

            @block.vector
            def _(eng):
                run("dve", eng)
        self.rec = {e: [] for e in ENGS}
        for e in ENGS:
            for k in ENGS:
                self.clock[e][k] = self.nseq[k]

    def mm(self, bank, out, lhsT, rhs, half=None):
        if half is None:
            start = bank.fresh[0] or bank.fresh[1]
            bank.fresh = [False, False]
        else:
            start = bank.fresh[half]
            bank.fresh[half] = False
        o, l, r = out.ap, lhsT.ap, rhs.ap
        self.op("pe", lambda e: e.matmul(o, lhsT=l, rhs=r, start=start, stop=True, skip_group_check=True),
                reads=[lhsT, rhs], writes=[out])

    def act(self, out, in_, func, bias=None, scale=None):
        reads = [in_]
        kw = {}
        if bias is not None:
            if isinstance(bias, View):
                reads.append(bias)
                kw["bias"] = bias.ap
            else:
                kw["bias"] = float(bias)
        if scale is not None:
            if isinstance(scale, View):
                reads.append(scale)
                kw["scale"] = scale.ap
            else:
                kw["scale"] = float(scale)
        o, i = out.ap, in_.ap
        self.op("act", lambda e: e.activation(out=o, in_=i, func=func, **kw), reads=reads, writes=[out])

    def tt(self, eng, out, a, b, op):
        o, x, y = out.ap, a.ap, b.ap
        self.op(eng, lambda e: e.tensor_tensor(out=o, in0=x, in1=y, op=op), reads=[a, b], writes=[out])

    def ts(self, eng, out, a, s1, op0, s2=None, op1=None):
        reads = [a]
        v1 = s1
        if isinstance(s1, View):
            reads.append(s1)
            v1 = s1.ap
        v2 = s2
        if isinstance(s2, View):
            reads.append(s2)
            v2 = s2.ap
        o, x = out.ap, a.ap
        if op1 is None:
            self.op(eng, lambda e: e.tensor_scalar(out=o, in0=x, scalar1=v1, scalar2=None, op0=op0),
                    reads=reads, writes=[out])
        else:
            self.op(eng, lambda e: e.tensor_scalar(out=o, in0=x, scalar1=v1, scalar2=v2, op0=op0, op1=op1),
                    reads=reads, writes=[out])

    def stt(self, out, a, s, b, op0, op1):
        reads = [a, b]
        sv = s
        if isinstance(s, View):
            reads.append(s)
            sv = s.ap
        o, x, y = out.ap, a.ap, b.ap
        self.op("dve", lambda e: e.scalar_tensor_tensor(out=o, in0=x, scalar=sv, in1=y, op0=op0, op1=op1),
                reads=reads, writes=[out])

    def cp(self, eng, out, a):
        o, x = out.ap, a.ap
        if eng == "act":
            self.op("act", lambda e: e.activation(out=o, in_=x, func=AF.Copy), reads=[a], writes=[out])
        else:
            self.op(eng, lambda e: e.tensor_copy(out=o, in_=x), reads=[a], writes=[out])

    def memset(self, eng, out, val):
        o = out.ap
        self.op(eng, lambda e: e.memset(o, val), reads=[], writes=[out])


D = 2048
SEQ = 4096
NK = 16
RW = 1024
DFF = 5632
NFC = DFF // 128
CDEC = float(np.exp(-0.5))
E0 = 2944
NE = 1152
E1 = 2816
NE1 = 1280
HC0 = 126
NT3 = 342
C_R, C_K, C_V, C_WD, C_AD, C_GD = 0, 1024, 2048, 3072, 3168, 3264
C_AQ, C_AK, C_AV = 3520, 4544, 4800
C_GRW, C_GAT = 5056, 7104

PPI = {}
_o = 0
for _n, _c in [("n1g", 16), ("n2g", 16), ("mu_r", 8), ("mu_k", 8), ("mu_v", 8), ("mu_wd", 1), ("mu_ad", 1),
               ("mu_gd", 2), ("w0", 8), ("a0", 8), ("kk", 8), ("ka", 8), ("rk", 8), ("gnw", 8), ("gnb", 8),
               ("qg", 1), ("kg", 1), ("cw0", 88), ("cw1", 88), ("cw2", 88), ("cb", 88), ("sk", 8),
               ("prevmask", 1), ("convmask", 1)]:
    PPI[_n] = _o
    _o += _c
NPP = _o

CSI = {}
_o = 0
for _n, _c in [("ident", 128), ("i2", 64), ("m1", 256), ("bones", 128), ("rmask", 256), ("m13", 512), ("m22", 512),
               ("ones", 128)]:
    CSI[_n] = _o
    _o += _c
NCS = _o


def _chunked(v, ncol):
    return np.ascontiguousarray(np.asarray(v, np.float32).reshape(ncol, 128).T)


def host_consts():
    cs = np.zeros((128, NCS), np.float32)
    p = np.arange(128)[:, None]
    f = np.arange(128)[None, :]
    cs[:, CSI["ident"]:CSI["ident"] + 128] = (p == f)
    cs[:, CSI["i2"]:CSI["i2"] + 64] = ((p % 64) == np.arange(64)[None, :])
    m1 = (p < f).astype(np.float32)
    m2 = (p <= f).astype(np.float32)
    m3 = (f < p).astype(np.float32)
    cs[:, CSI["m1"]:CSI["m1"] + 256] = np.tile(m1, (1, 2))
    cs[:, CSI["m13"]:CSI["m13"] + 512] = np.concatenate([np.tile(m1, (1, 2)), np.tile(m3, (1, 2))], axis=1)
    cs[:, CSI["m22"]:CSI["m22"] + 512] = np.tile(m2, (1, 4))
    cs[:, CSI["bones"]:CSI["bones"] + 128] = ((p // 64) == (f // 64))
    rm = np.ones((128, 256), np.float32)
    rm[:, 0::128] = 0.0
    cs[:, CSI["rmask"]:CSI["rmask"] + 256] = rm
    cs[:, CSI["ones"]:CSI["ones"] + 128] = 1.0
    slopes = np.exp2(-8.0 * np.arange(1, 17, dtype=np.float32) / 16.0).astype(np.float32)
    ab = np.zeros((128, 4, 2, 4, 128), np.float32)
    k = np.arange(128)[:, None]
    q = np.arange(128)[None, :]
    for h in range(4):
        for g in range(4):
            sl = slopes[4 * h + g]
            dprev = (128 + q - k).astype(np.float32)
            dcur = (q - k).astype(np.float32)
            ab[:, h, 0, g, :] = np.where(dprev < 128, -sl * dprev, -30000.0)
            ab[:, h, 1, g, :] = np.where(dcur >= 0, -sl * dcur, -30000.0)
    return cs, ab.reshape(128, 4096)


def host_params(inp, j):
    pp = np.zeros((128, NPP), np.float32)

    def put(name, arr):
        arr = np.asarray(arr, np.float32)
        pp[:arr.shape[0], PPI[name]:PPI[name] + arr.shape[1]] = arr

    put("n1g", _chunked(inp["norm1_g"][0], 16))
    put("n2g", _chunked(inp["norm2_g"][0], 16))
    mu = np.asarray(inp["rw_mu"][0], np.float32)
    put("mu_r", _chunked(mu[0:1024], 8))
    put("mu_k", _chunked(mu[1024:2048], 8))
    put("mu_v", _chunked(mu[2048:3072], 8))
    put("mu_wd", mu[3072:3168].reshape(96, 1))
    put("mu_ad", mu[3168:3264].reshape(96, 1))
    put("mu_gd", _chunked(mu[3264:3520], 2))
    put("w0", _chunked(inp["rw_w0"][0], 8))
    put("a0", _chunked(inp["rw_a0"][0], 8))
    put("kk", _chunked(inp["rw_k_k"][0], 8))
    put("ka", _chunked(inp["rw_k_a"][0], 8))
    put("rk", _chunked(np.asarray(inp["rw_r_k"][0]).reshape(-1), 8))
    put("gnw", _chunked(inp["rw_gn_w"][0], 8))
    put("gnb", _chunked(inp["rw_gn_b"][0], 8))
    put("qg", np.tile(np.asarray(inp["q_norm_g"][0], np.float32), 2).reshape(128, 1))
    put("kg", np.tile(np.asarray(inp["k_norm_g"][0], np.float32), 2).reshape(128, 1))
    cw = np.asarray(inp["conv_w"][0], np.float32)
    put("cw0", _chunked(cw[0], 88))
    put("cw1", _chunked(cw[1], 88))
    put("cw2", _chunked(cw[2], 88))
    put("cb", _chunked(inp["conv_b"][0], 88))
    sk = np.asarray(inp["attn_sinks"][0], np.float32)
    put("sk", np.stack([sk[2 * c + (np.arange(128) // 64)] for c in range(8)], axis=1))
    put("prevmask", np.full((128, 1), 0.0 if j == 0 else 1.0, np.float32))
    put("convmask", np.full((128, 1), 0.0 if j == 0 else 1.0, np.float32))
    return pp


DBG = os.environ.get("KDBG", "")


class K:
    pass


def build_program(dbg=""):
    nc = bass.Bass("TRN2", target_bir_lowering=False)
    k = K()
    k.nc = nc
    dt_in = lambda name, shape: nc.dram_tensor(name, list(shape), F32, kind="ExternalInput")
    k.xT = dt_in("xT", [D, SEQ])
    k.w_in = dt_in("w_in", [D, 9152])
    small = dbg in ("A0", "A", "IO")
    k.w_branch = dt_in("w_branch", [D, D] if not small else [128, 128])
    k.w_out = dt_in("w_out", [D, D] if not small else [128, 128])
    k.w_up = dt_in("w_up", [D, 2 * DFF] if not small else [128, 128])
    k.w_down = dt_in("w_down", [DFF, D] if not small else [128, 128])
    k.w2 = dt_in("rw_w2", [96, RW])
    k.a2 = dt_in("rw_a2", [96, RW])
    k.g2 = dt_in("rw_g2", [256, RW])
    k.pp = dt_in("pp", [128, NPP])
    k.cst = dt_in("cst", [128, NCS])
    k.abias = dt_in("abias", [128, 4096])
    k.outT = nc.dram_tensor("outT", [D, 1024], F32, kind="ExternalOutput")
    k.dbg = dbg
    k.dbg_outs = {}
    with ExitStack() as st:
        P = Prog(nc, st)
        k.P = P
        P.init_banks()
        k.d = {n: P.dram(getattr(k, n), n) for n in
               ["xT", "w_in", "w_branch", "w_out", "w_up", "w_down", "w2", "a2", "g2", "pp", "cst", "abias", "outT"]}
        setup_persistent(k, st)
        R = lambda stack, shape, dt, name: P.sb(shape, dt, name, stack=stack, side="right")
        r_orw = ExitStack()
        k.o_rw = R(r_orw, [128, 8, NE], BF16, "o_rw")
        if dbg == "IO":
            dbg_out(k, "pp", k.ppt[:, :], [128, NPP])
        if dbg not in ("IO", "B"):
            for hp in range(2):
                with ExitStack() as ph:
                    phase_a(k, ph, hp)
                    P.emit()
                if dbg == "A0":
                    break
        if dbg not in ("A0", "A", "IO"):
            r_b = ExitStack()
            k.hE = R(r_b, [128, NK, NE1], BF16, "hE")
            k.o_att = R(r_b, [128, 8, NE], BF16, "o_att")
            r_b2 = ExitStack()
            k.qhat = R(r_b2, [128, 8, NE], BF16, "qhat")
            k.khat = R(r_b2, [128, 4, NE1], BF16, "khat")
            k.vtok = R(r_b2, [128, 10, 256], BF16, "vtok")
            with ExitStack() as ph:
                phase_b1(k, ph)
                P.emit()
            with ExitStack() as ph:
                phase_b2(k, ph)
                P.emit()
            r_b2.close()
            if dbg != "B":
                l_m = ExitStack()
                k.mT = P.sb([128, NK, 3 * NT3], BF16, "mT", stack=l_m)
                with ExitStack() as ph:
                    phase_c1(k, ph)
                    P.emit()
                r_b.close()
                r_orw.close()
                r_orw = None
                r_x1 = ExitStack()
                k.x1T = R(r_x1, [128, NK, 3 * NT3], F32, "x1T")
                with ExitStack() as ph:
                    phase_c2(k, ph)
                    P.emit()
                l_m.close()
                with ExitStack() as ph:
                    phase_d(k, ph)
                    P.emit()
                r_x1.close()
            else:
                r_b.close()
        if r_orw is not None:
            r_orw.close()
        finish(k)
        k.stats = P.tot
    return nc, k


def dbg_out(k, name, view, shape, dt=F32):
    P = k.P
    t = k.nc.dram_tensor("dbg_" + name, list(shape), dt, kind="ExternalOutput")
    k.dbg_outs[name] = t
    td = P.dram(t, "dbg_" + name)
    P.dma("sp", View(td, t.ap()), view)
    k.final_views.append(View(td, t.ap()))


def setup_persistent(k, st):
    P = k.P
    k.final_views = []
    k.ppt = P.sb([128, NPP], F32, "pp")
    k.cs = P.sb([128, NCS], F32, "cs")
    P.dma("sp", k.ppt[:, :], k.d["pp"][:, :])
    P.dma("sp", k.cs[:, :], k.d["cst"][:, :])
    k.identb = P.sb([128, 128], BF16, "identb")
    k.onesb = P.sb([128, 128], BF16, "onesb")
    k.bonesb = P.sb([128, 128], BF16, "bonesb")
    P.cp("dve", k.identb[:, :], k.cs[:, CSI["ident"]:CSI["ident"] + 128])
    P.cp("dve", k.onesb[:, :], k.cs[:, CSI["ones"]:CSI["ones"] + 128])
    P.cp("dve", k.bonesb[:, :], k.cs[:, CSI["bones"]:CSI["bones"] + 128])
    k.negb = P.sb([128, 16], F32, "negb")
    P.ts("dve", k.negb[:, :], k.ppt[:, PPI["w0"]:PPI["w0"] + 16], -1.0, ALU.mult)
    k.omu = P.sb([128, 28], F32, "omu")
    P.ts("dve", k.omu[:, :], k.ppt[:, PPI["mu_r"]:PPI["mu_r"] + 28], -1.0, ALU.mult, 1.0, ALU.add)


def pcol(k, name, c=0, rows=128):
    o = PPI[name] + c
    return k.ppt[0:rows, o:o + 1]


def omucol(k, name, c=0, rows=128):
    o = PPI[name] - PPI["mu_r"] + c
    return k.omu[0:rows, o:o + 1]


def cview(k, name, n):
    return k.cs[:, CSI[name]:CSI[name] + n]


def rsqrt_act(P, out, in_, scale, bias, tmp):
    P.act(tmp, in_, AF.Ln, bias=bias, scale=scale)
    P.act(out, tmp, AF.Exp, scale=-0.5)


def rmsnorm_block(k, ph_tiles, xb, hT, nt, gname):
    P = k.P
    sqb, rstd, tmp = ph_tiles
    P.act(sqb[:, :, 0:nt], xb[:, :, 0:nt], AF.Square)
    bk = P.bank()
    for kc in range(NK):
        P.mm(bk, bk.t[:, 0:nt], k.onesb[:, :], sqb[:, kc, 0:nt])
    rsqrt_act(P, rstd[:, 0:nt], bk.t[:, 0:nt], 1.0 / D, 1e-6, tmp[:, 0:nt])
    for kc in range(NK):
        P.stt(hT[:, kc, 0:nt], xb[:, kc, 0:nt], pcol(k, gname, kc), rstd[:, 0:nt], ALU.mult, ALU.mult)


def load_w(k, dst_view, src_handle, r0, nk, c0, ncol):
    P = k.P
    src = src_handle.ap().rearrange("(k p) c -> p k c", p=128)[:, r0:r0 + nk, c0:c0 + ncol]
    name = [n for n, t in k.d.items() if t.h is src_handle][0]
    P.dma("pool", dst_view, View(k.d[name], src))


class PA:
    pass


NTA = 256
NBA = SEQ // NTA
BI_R = 10
BI_O = 11


def proj(k, bank, W, col0, ncols, hT, nt):
    P = k.P
    for kc in range(NK):
        P.mm(bank, bank.t[0:ncols, 0:nt], W[:, kc, col0:col0 + ncols], hT[:, kc, 0:nt])


def shift(k, A, zps, rows, nt, mu, omu, carcol, out):
    P = k.P
    tmp = A.shtmp[A.shi % 3]
    A.shi += 1
    P.act(tmp[0:rows, 1:nt + 1], zps[0:rows, 0:nt], AF.Identity, scale=mu)
    P.cp("pool", tmp[0:rows, 0:1], A.car[0:rows, carcol:carcol + 1])
    P.cp("pool", A.car[0:rows, carcol:carcol + 1], tmp[0:rows, nt:nt + 1])
    P.stt(out, zps[0:rows, 0:nt], omu, tmp[0:rows, 0:nt], ALU.mult, ALU.add)


def phase_a(k, ph, hp):
    P = k.P
    A = PA()
    A.hp = hp
    nt = NTA
    S = lambda shape, dt, name: P.sb(shape, dt, name, stack=ph)
    A.W = S([128, NK, 1984], BF16, "WA")
    ch0 = 512 * hp
    load_w(k, A.W[:, :, 0:512], k.w_in, 0, NK, C_K + ch0, 512)
    load_w(k, A.W[:, :, 512:1024], k.w_in, 0, NK, C_V + ch0, 512)
    load_w(k, A.W[:, :, 1536:1984], k.w_in, 0, NK, C_WD, 448)
    load_w(k, A.W[:, :, 1024:1536], k.w_in, 0, NK, C_R + ch0, 512)
    A.w2b = S([128, 512], BF16, "w2b")
    A.a2b = S([128, 512], BF16, "a2b")
    A.g2b = S([128, 2, 512], BF16, "g2b")
    P.dma("pool", A.w2b[0:96, :], k.d["w2"][:, ch0:ch0 + 512])
    P.dma("pool", A.a2b[0:96, :], k.d["a2"][:, ch0:ch0 + 512])
    P.dma("pool", A.g2b[:, :, :], View(k.d["g2"], k.g2.ap().rearrange("(k p) c -> p k c", p=128)[:, :, ch0:ch0 + 512]))
    A.car = S([128, 16], F32, "car")
    P.memset("pool", A.car[:, :], 0.0)
    A.H = [S([128, 64], F32, f"H{i}") for i in range(8)]
    A.Hbf = [S([128, 64], BF16, f"Hbf{i}") for i in range(8)]
    for i in range(8):
        P.memset("pool", A.H[i][:, :], 0.0)
        P.memset("pool", A.Hbf[i][:, :], 0.0)
    A.xb = S([128, NK, nt], F32, "xb")
    A.sqr = [S([128, nt], BF16, f"sqr{i}") for i in range(2)]
    A.hT = S([128, NK, nt], BF16, "hT")
    A.rstd = S([128, nt], F32, "rstd")
    A.tmpn = S([128, nt], F32, "tmpn")
    A.shtmp = [S([128, nt + 1], F32, f"shtmp{i}") for i in range(3)]
    A.shi = 0
    A.scr = S([128, nt], F32, "scr")
    A.th_wd = S([128, nt], BF16, "th_wd")
    A.ad_s = S([128, nt], BF16, "ad_s")
    A.sg_gd = [S([128, 2, nt], BF16, f"sg_gd{i}") for i in range(2)]
    alias32 = {"ks": 0, "vs": 1, "rs": 2, "sg": 3, "cm": 3, "Em": 3, "aa": 4, "f": 4, "kp": 4, "cs": 5, "En": 6,
               "Ep": 7, "ssc": 8, "ln": 8, "rn": 8, "kkn": 9, "kba": 10, "bonus": 11, "yT": 12,
               "dd": 0, "sqd": 1, "rs2": 2, "yn": 3, "t1": 5, "t2": 6}
    alias16 = {"sq": 0, "rkb": 0, "At": 1, "Bt": 2, "Kt": 3, "Bh": 4, "Kh": 5, "Vb": 6, "Rt": 7}
    A.T = []
    for par in range(3):
        T = PA()
        b32 = [S([128, nt], F32, f"b32_{par}_{i}") for i in range(13)]
        b16 = [S([128, nt], BF16, f"b16_{par}_{i}") for i in range(8)]
        for n, i in alias32.items():
            setattr(T, n, b32[i])
        for n, i in alias16.items():
            setattr(T, n, b16[i])
        T.TK = [S([128, 384], BF16, f"TK{par}_{i}") for i in range(2)]
        T.EpL = S([128, 2], F32, f"EpL{par}")
        T.dgW = S([128, 128], F32, f"dgW{par}")
        A.T.append(T)
    A.Q = []
    for slot in range(2):
        Q = PA()
        Q.XX = [S([128, 512], BF16, f"XX{slot}_{i}") for i in range(2)]
        Q.Z = [S([128, 256], BF16, f"Z{slot}_{i}") for i in range(2)]
        Q.AAK = S([128, 256], BF16, f"AAK{slot}")
        Q.ARBK = S([128, 512], BF16, f"ARBK{slot}")
        Q.PG = S([128, 256], F32, f"PG{slot}")
        Q.QT = S([128, 256], BF16, f"QT{slot}")
        P.memset("pool", Q.PG[:, :], 0.0)
        P.memset("pool", Q.QT[:, :], 0.0)
        Q.acc = P.banks[slot]
        A.Q.append(Q)
    P.bank_rr = 0
    _orig_bank = P.bank

    def ring_bank():
        b = P.banks[2 + (P.bank_rr % 6)]
        P.bank_rr += 1
        b.fresh = [True, True]
        return b
    P.bank = ring_bank

    evi = [0]

    def evac(out, in_):
        evi[0] += 1
        P.cp("act" if evi[0] % 2 else "dve", out, in_)

    def recip(out, in_):
        o, i = out.ap, in_.ap
        P.op("dve", lambda e: e.reciprocal(out=o, in_=i), reads=[in_], writes=[out])

    def block_head(bi):
        t0 = nt * bi
        src = k.xT.ap().rearrange("(k p) t -> p k t", p=128)[:, :, t0:t0 + nt]
        P.dma("sp", A.xb[:, :, :], View(k.d["xT"], src))
        bk = P.bank()
        for kc in range(NK):
            sq = A.sqr[kc % 2]
            P.act(sq[:, :], A.xb[:, kc, :], AF.Square)
            P.mm(bk, bk.t[:, 0:nt], k.onesb[:, :], sq[:, :])
        rsqrt_act(P, A.rstd[:, :], bk.t[:, 0:nt], 1.0 / D, 1e-6, A.tmpn[:, :])
        for kc in range(NK):
            P.stt(A.hT[:, kc, :], A.xb[:, kc, :], pcol(k, "n1g", kc), A.rstd[:, :], ALU.mult, ALU.mult)
        bk = P.bank()
        proj(k, bk, A.W, 1536, 96, A.hT, nt)
        shift(k, A, bk.t, 96, nt, pcol(k, "mu_wd", 0, 96), omucol(k, "mu_wd", 0, 96), 12, A.scr[0:96, :])
        P.act(A.th_wd[0:96, :], A.scr[0:96, :], AF.Tanh)
        bk = P.bank()
        proj(k, bk, A.W, 1632, 96, A.hT, nt)
        shift(k, A, bk.t, 96, nt, pcol(k, "mu_ad", 0, 96), omucol(k, "mu_ad", 0, 96), 13, A.ad_s[0:96, :])
        if bi >= BI_R:
            for j in range(2):
                bk = P.bank()
                proj(k, bk, A.W, 1728 + 128 * j, 128, A.hT, nt)
                shift(k, A, bk.t, 128, nt, pcol(k, "mu_gd", j), omucol(k, "mu_gd", j), 14 + j, A.scr[:, :])
                P.act(A.sg_gd[bi % 2][:, j, :], A.scr[:, :], AF.Sigmoid)

    def prep(bi, lc):
        cc = 4 * hp + lc
        needR = bi >= BI_R
        outp = bi >= BI_O
        T = A.T[(bi * 4 + lc) % 3]
        bk = P.bank()
        proj(k, bk, A.W, lc * 128, 128, A.hT, nt)
        shift(k, A, bk.t, 128, nt, pcol(k, "mu_k", cc), omucol(k, "mu_k", cc), lc, T.ks[:, :])
        yield
        bk = P.bank()
        proj(k, bk, A.W, 512 + lc * 128, 128, A.hT, nt)
        shift(k, A, bk.t, 128, nt, pcol(k, "mu_v", cc), omucol(k, "mu_v", cc), 4 + lc, T.vs[:, :])
        yield
        if needR:
            bk = P.bank()
            proj(k, bk, A.W, 1024 + lc * 128, 128, A.hT, nt)
            shift(k, A, bk.t, 128, nt, pcol(k, "mu_r", cc), omucol(k, "mu_r", cc), 8 + lc, T.rs[:, :])
            yield
        bw = P.bank()
        P.mm(bw, bw.t[:, 0:nt], A.w2b[0:96, lc * 128:(lc + 1) * 128], A.th_wd[0:96, :])
        P.act(T.sg[:, :], bw.t[:, 0:nt], AF.Sigmoid, bias=pcol(k, "w0", cc))
        ba = P.bank()
        P.mm(ba, ba.t[:, 0:nt], A.a2b[0:96, lc * 128:(lc + 1) * 128], A.ad_s[0:96, :])
        P.act(T.aa[:, :], ba.t[:, 0:nt], AF.Sigmoid, bias=pcol(k, "a0", cc))
        yield
        rm = cview(k, "rmask", nt)
        cs_ap, rm_ap, sg_ap = T.cs[:, :].ap, rm.ap, T.sg[:, :].ap
        P.op("dve", lambda e: e.tensor_tensor_scan(out=cs_ap, data0=rm_ap, data1=sg_ap, initial=0.0,
                                                   op0=ALU.mult, op1=ALU.add),
             reads=[rm, T.sg[:, :]], writes=[T.cs[:, :]])
        P.tt("pool", T.cm[:, :], T.cs[:, :], T.sg[:, :], ALU.subtract)
        P.act(T.En[:, :], T.cs[:, :], AF.Exp, scale=CDEC)
        P.act(T.Em[:, :], T.cm[:, :], AF.Exp, scale=-CDEC)
        for c in range(2):
            P.act(T.EpL[:, c:c + 1], T.cs[:, c * 128 + 127:c * 128 + 128], AF.Exp, scale=-CDEC)
        if outp:
            P.act(T.Ep[:, :], T.cs[:, :], AF.Exp, scale=-CDEC)
        yield
        P.act(T.sq[:, :], T.ks[:, :], AF.Square, scale=pcol(k, "kk", cc))
        bs = P.bank()
        P.mm(bs, bs.t[:, 0:nt], k.bonesb[:, :], T.sq[:, :])
        P.ts("dve", T.ssc[:, :], bs.t[:, 0:nt], 1e-24, ALU.max)
        P.act(T.ln[:, :], T.ssc[:, :], AF.Ln, scale=float(2.0 ** 40))
        P.act(T.rn[:, :], T.ln[:, :], AF.Exp, scale=-0.5)
        yield
        P.stt(T.kkn[:, :], T.ks[:, :], pcol(k, "kk", cc), T.rn[:, :], ALU.mult, ALU.mult)
        P.stt(T.At[:, :], T.kkn[:, :], -float(2.0 ** 20), T.Em[:, :], ALU.mult, ALU.mult)
        P.stt(T.kba[:, :], T.kkn[:, :], float(2.0 ** 20), T.aa[:, :], ALU.mult, ALU.mult)
        yield
        P.ts("dve", T.f[:, :], T.aa[:, :], -1.0, ALU.add, pcol(k, "ka", cc), ALU.mult)
        P.stt(T.kp[:, :], T.f[:, :], 1.0, T.ks[:, :], ALU.add, ALU.mult)
        P.tt("pool", T.Bt[:, :], T.kba[:, :], T.En[:, :], ALU.mult)
        P.tt("pool", T.Kt[:, :], T.kp[:, :], T.En[:, :], ALU.mult)
        P.cp("pool", T.Vb[:, :], T.vs[:, :])
        yield
        for c in range(2):
            cs_ = slice(c * 128, (c + 1) * 128)
            P.stt(T.Bh[:, cs_], T.kba[:, cs_], T.EpL[:, c:c + 1], T.En[:, cs_], ALU.mult, ALU.mult)
            P.stt(T.Kh[:, cs_], T.kp[:, cs_], T.EpL[:, c:c + 1], T.En[:, cs_], ALU.mult, ALU.mult)
            P.ts("dve", T.dgW[:, c * 64:(c + 1) * 64], cview(k, "i2", 64), T.EpL[:, c:c + 1], ALU.mult)
        yield
        if outp:
            P.tt("pool", T.Rt[:, :], T.rs[:, :], T.Ep[:, :], ALU.mult)
            P.stt(T.rkb[:, :], T.rs[:, :], pcol(k, "rk", cc), T.kp[:, :], ALU.mult, ALU.mult)
            bb = P.bank()
            P.mm(bb, bb.t[:, 0:nt], k.bonesb[:, :], T.rkb[:, :])
            P.tt("dve", T.bonus[:, :], bb.t[:, 0:nt], T.vs[:, :], ALU.mult)
            yield

    def hquad(bi, lc, h, slot):
        T = A.T[(bi * 4 + lc) % 3]
        Q = A.Q[slot]
        outp = bi >= BI_O
        m1 = cview(k, "m1", 256)
        hs = slice(64 * h, 64 * h + 64)
        hd = 2 * lc + h

        def cs_(ci):
            return slice(ci * 128, (ci + 1) * 128)
        m13 = cview(k, "m13", 512)
        m22 = cview(k, "m22", 512)
        g = P.bank()
        for ci in range(2):
            P.mm(g, g.t[:, cs_(ci)], T.Bt[hs, cs_(ci)], T.At[hs, cs_(ci)])
        for ci in range(2):
            P.mm(g, g.t[:, 256 + ci * 128:256 + (ci + 1) * 128], T.At[hs, cs_(ci)], T.Bt[hs, cs_(ci)])
        P.tt("dve", Q.XX[0][:, :], g.t[:, :], m13, ALU.mult)
        yield
        g = P.bank()
        for ci in range(2):
            P.mm(g, g.t[:, cs_(ci)], T.Kt[hs, cs_(ci)], T.At[hs, cs_(ci)])
        P.tt("dve", Q.AAK[:, :], g.t[:, 0:256], m1, ALU.mult)
        if outp:
            g = P.bank()
            for ci in range(2):
                P.mm(g, g.t[:, cs_(ci)], T.Bt[hs, cs_(ci)], T.Rt[hs, cs_(ci)])
            for ci in range(2):
                P.mm(g, g.t[:, 256 + ci * 128:256 + (ci + 1) * 128], T.Kt[hs, cs_(ci)], T.Rt[hs, cs_(ci)])
            P.tt("dve", Q.ARBK[:, :], g.t[:, :], m22, ALU.mult)
        yield
        if h == 0:
            for ci in range(2):
                g = P.bank()
                P.mm(g, g.t[:, 0:128], T.Vb[:, cs_(ci)], k.identb[:, :])
                P.mm(g, g.t[:, 128:256], T.Bh[:, cs_(ci)], k.identb[:, :])
                P.mm(g, g.t[:, 256:384], T.Kh[:, cs_(ci)], k.identb[:, :])
                evac(T.TK[ci][:, :], g.t[:, 0:384])
        yield
        acc = Q.acc
        acc.fresh = [True, True]
        for ci in range(2):
            P.mm(acc, acc.t[:, ci * 128:ci * 128 + 64], T.At[:, cs_(ci)], k.identb[:, hs])
            P.mm(acc, acc.t[:, ci * 128 + 64:ci * 128 + 128], Q.AAK[:, cs_(ci)], T.TK[ci][:, 64 * h:64 * h + 64])
        evac(Q.Z[0][:, :], acc.t[:, 0:256])
        yield
        zi = 0
        for i in range(7):
            XX = Q.XX[i % 2]
            for ci in range(2):
                P.mm(acc, acc.t[:, cs_(ci)], XX[:, cs_(ci)], Q.Z[zi][:, cs_(ci)])
            zi ^= 1
            evac(Q.Z[zi][:, :], acc.t[:, 0:256])
            if i < 6:
                g = P.bank()
                for ci in range(2):
                    P.mm(g, g.t[:, cs_(ci)], XX[:, 256 + ci * 128:256 + (ci + 1) * 128], XX[:, cs_(ci)])
                if i < 5:
                    for ci in range(2):
                        P.mm(g, g.t[:, 256 + ci * 128:256 + (ci + 1) * 128], XX[:, cs_(ci)],
                             XX[:, 256 + ci * 128:256 + (ci + 1) * 128])
                    evac(Q.XX[(i + 1) % 2][:, :], g.t[:, :])
                else:
                    evac(Q.XX[(i + 1) % 2][:, 0:256], g.t[:, 0:256])
            yield
        Z = Q.Z[zi]
        g = P.bank()
        ic = CSI["ident"]
        for ci in range(2):
            P.mm(g, g.t[0:64, ci * 128:ci * 128 + 64], Z[:, ci * 128:ci * 128 + 64],
                 T.TK[ci][:, 128 + 64 * h:128 + 64 * h + 64], half=0)
            P.mm(g, g.t[0:64, ci * 128:ci * 128 + 64], k.cs[:, ic + 64 * h:ic + 64 * h + 64],
                 T.dgW[:, ci * 64:(ci + 1) * 64], half=0)
            P.mm(g, g.t[0:64, ci * 128 + 64:ci * 128 + 128], T.TK[ci][:, 128 + 64 * h:128 + 64 * h + 64],
                 Z[:, ci * 128 + 64:ci * 128 + 128], half=0)
            P.mm(g, g.t[0:64, ci * 128 + 64:ci * 128 + 128], T.TK[ci][:, 256 + 64 * h:256 + 64 * h + 64],
                 T.TK[ci][:, 64 * h:64 * h + 64], half=0)
        evac(Q.PG[0:64, :], g.t[0:64, 0:256])
        yield
        if outp:
            gq = P.bank()
            for ci in range(2):
                P.mm(gq, gq.t[0:64, cs_(ci)], Z[:, ci * 128:ci * 128 + 64], Q.ARBK[:, cs_(ci)], half=0)
                P.mm(gq, gq.t[0:64, cs_(ci)], k.identb[:, hs], T.Rt[:, cs_(ci)], half=0)
            evac(Q.QT[0:64, :], gq.t[0:64, 0:256])
            yield
            gy = P.bank()
        for ci in range(2):
            if outp:
                P.cp("pool", A.Hbf[hd][0:64, :], A.H[hd][0:64, :])
                P.mm(gy, gy.t[hs, cs_(ci)], Z[:, ci * 128 + 64:ci * 128 + 128], Q.ARBK[:, cs_(ci)], half=h)
                P.mm(gy, gy.t[hs, cs_(ci)], T.TK[ci][:, 64 * h:64 * h + 64], Q.ARBK[:, 256 + ci * 128:256 + (ci + 1) * 128], half=h)
                P.mm(gy, gy.t[hs, cs_(ci)], A.Hbf[hd][:, :], Q.QT[:, cs_(ci)], half=h)
            gs = P.bank()
            P.mm(gs, gs.t[0:64, 0:64], Q.PG[:, ci * 128:ci * 128 + 64], A.H[hd][:, :], half=0)
            P.tt("dve", A.H[hd][0:64, :], gs.t[0:64, 0:64], Q.PG[0:64, ci * 128 + 64:ci * 128 + 128], ALU.add)
        if outp:
            evac(T.yT[hs, :], gy.t[hs, 0:256])
        yield

    def post(bi, lc):
        cc = 4 * hp + lc
        T = A.T[(bi * 4 + lc) % 3]
        bonesf = cview(k, "bones", 128)
        bm = P.bank()
        P.mm(bm, bm.t[:, 0:nt], bonesf, T.yT[:, :])
        P.stt(T.dd[:, :], bm.t[:, 0:nt], -1.0 / 64, T.yT[:, :], ALU.mult, ALU.add)
        P.act(T.sqd[:, :], T.dd[:, :], AF.Square)
        yield
        bv = P.bank()
        P.mm(bv, bv.t[:, 0:nt], bonesf, T.sqd[:, :])
        rsqrt_act(P, T.rs2[:, :], bv.t[:, 0:nt], 1.0 / 64, 64e-5, T.ln[:, :])
        P.tt("dve", T.yn[:, :], T.dd[:, :], T.rs2[:, :], ALU.mult)
        P.ts("dve", T.t1[:, :], T.yn[:, :], pcol(k, "gnw", cc), ALU.mult, pcol(k, "gnb", cc), ALU.add)
        P.tt("pool", T.t2[:, :], T.t1[:, :], T.bonus[:, :], ALU.add)
        yield
        bg = P.bank()
        for j in range(2):
            P.mm(bg, bg.t[:, 0:nt], A.g2b[:, j, lc * 128:(lc + 1) * 128], A.sg_gd[bi % 2][:, j, :])
        if bi == BI_O:
            P.tt("dve", k.o_rw[:, cc, 0:128], T.t2[:, 128:256], bg.t[:, 128:256], ALU.mult)
        else:
            e0 = nt * bi - E0
            P.tt("dve", k.o_rw[:, cc, e0:e0 + nt], T.t2[:, :], bg.t[:, 0:nt], ALU.mult)
        yield

    ksteps = int(os.environ.get("KSTEPS", "1000000000"))
    stepc = [0]

    def run_all(gens):
        gens = list(gens)
        while gens:
            nxt = []
            for g_ in gens:
                if stepc[0] >= ksteps:
                    return
                stepc[0] += 1
                try:
                    next(g_)
                    nxt.append(g_)
                except StopIteration:
                    pass
            gens = nxt

    nblk = int(os.environ.get("KNBLK", str(NBA))) if k.dbg else NBA
    work = [(bi, lc) for bi in range(nblk) for lc in range(4)]

    def prep_full(w):
        bi, lc = w
        if lc == 0:
            block_head(bi)
            yield
        yield from prep(bi, lc)

    run_all([prep_full(work[0])])
    if len(work) > 1:
        run_all([prep_full(work[1])])
    for wi, (bi, lc) in enumerate(work):
        gens = [hquad(bi, lc, 0, 0), hquad(bi, lc, 1, 1)]
        if wi + 2 < len(work):
            gens.append(prep_full(work[wi + 2]))
        run_all(gens)
        if bi >= BI_O:
            run_all([post(bi, lc)])
    P.bank = _orig_bank
    if k.dbg in ("A0", "A"):
        for i in range(8):
            dbg_out(k, f"H{hp}_{i}", A.H[i][0:64, :], [64, 64])
        if nblk > BI_O + 1:
            dbg_out(k, f"orw{hp}", k.o_rw[:, 4 * hp:4 * hp + 4, :], [128, 4, NE], BF16)
        dbg_out(k, f"hT{hp}", A.hT[:, :, :], [128, NK, nt], BF16)
        dbg_out(k, f"ks{hp}", A.T[1].ks[:, :], [128, nt])


def rms_block_256(k, xb, sqr, rstd, tmpn, hT_out, t0, gname):
    P = k.P
    nt = 256
    src = k.xT.ap().rearrange("(k p) t -> p k t", p=128)[:, :, t0:t0 + nt]
    P.dma("sp", xb[:, :, :], View(k.d["xT"], src))
    bk = P.bank()
    for kc in range(NK):
        sq = sqr[kc % 2]
        P.act(sq[:, :], xb[:, kc, :], AF.Square)
        P.mm(bk, bk.t[:, 0:nt], k.onesb[:, :], sq[:, :])
    rsqrt_act(P, rstd[:, :], bk.t[:, 0:nt], 1.0 / D, 1e-6, tmpn[:, :])
    for kc in range(NK):
        P.stt(hT_out(kc), xb[:, kc, :], pcol(k, gname, kc), rstd[:, :], ALU.mult, ALU.mult)


def phase_b1(k, ph):
    P = k.P
    S = lambda shape, dt, name: P.sb(shape, dt, name, stack=ph)
    nt = 256
    bufA = S([128, NK, 512], BF16, "wbufA")
    bufB = S([128, NK, 512], BF16, "wbufB")
    xb = S([128, NK, nt], F32, "xbB")
    sqr = [S([128, nt], BF16, f"sqrB{i}") for i in range(2)]
    rstd = S([128, nt], F32, "rstdB")
    tmpn = S([128, nt], F32, "tmpnB")
    sqq = [S([128, nt], BF16, f"sqq{i}") for i in range(2)]
    rq = [S([128, nt], F32, f"rq{i}") for i in range(2)]
    lq = [S([128, nt], F32, f"lq{i}") for i in range(2)]
    load_w(k, bufA[:, :, 0:256], k.w_in, 0, NK, C_AK, 256)
    for bi in range(5):
        rms_block_256(k, xb, sqr, rstd, tmpn, lambda kc: k.hE[:, kc, bi * nt:(bi + 1) * nt], E1 + nt * bi, "n1g")
    for h in range(4):
        for dup in range(2):
            P.cp("pool" if dup else "act", bufB[:, :, h * 128 + 64 * dup:h * 128 + 64 * dup + 64],
                 bufA[:, :, h * 64:(h + 1) * 64])
    ii = [0]

    def qknorm(bank, out_view, gname, c0, c1):
        i2 = ii[0] % 2
        ii[0] += 1
        P.act(sqq[i2][:, :], bank.t[:, 0:nt], AF.Square)
        bs = P.bank()
        P.mm(bs, bs.t[:, 0:nt], k.bonesb[:, :], sqq[i2][:, :])
        rsqrt_act(P, rq[i2][:, :], bs.t[:, 0:nt], 1.0 / 64, 1e-6, lq[i2][:, :])
        P.stt(out_view, bank.t[:, c0:c1], pcol(k, gname), rq[i2][:, c0:c1], ALU.mult, ALU.mult)

    for bi in range(5):
        for h in range(4):
            bk = P.bank()
            for kc in range(NK):
                P.mm(bk, bk.t[:, 0:nt], bufB[:, kc, h * 128:(h + 1) * 128], k.hE[:, kc, bi * nt:(bi + 1) * nt])
            qknorm(bk, k.khat[:, h, bi * nt:(bi + 1) * nt], "kg", 0, nt)
    load_w(k, bufA[:, :, 0:256], k.w_in, 0, NK, C_AV, 256)
    for blk in range(10):
        bv = P.bank()
        for kc in range(NK):
            P.mm(bv, bv.t[:, 0:256], k.hE[:, kc, blk * 128:(blk + 1) * 128], bufA[:, kc, 0:256])
        P.cp("act" if blk % 2 else "dve", k.vtok[:, blk, :], bv.t[:, 0:256])
    for half in range(2):
        buf = bufB if half == 0 else bufA
        load_w(k, buf[:, :, :], k.w_in, 0, NK, C_AQ + 512 * half, 512)
        for bi in range(5):
            for q4 in range(4):
                qc = 4 * half + q4
                bq = P.bank()
                for kc in range(NK):
                    P.mm(bq, bq.t[:, 0:nt], buf[:, kc, q4 * 128:(q4 + 1) * 128], k.hE[:, kc, bi * nt:(bi + 1) * nt])
                if bi == 0:
                    qknorm(bq, k.qhat[:, qc, 0:128], "qg", 128, 256)
                else:
                    e0 = nt * bi - 128
                    qknorm(bq, k.qhat[:, qc, e0:e0 + nt], "qg", 0, nt)


def phase_b2(k, ph):
    P = k.P
    S = lambda shape, dt, name: P.sb(shape, dt, name, stack=ph)
    ab = S([128, 4096], F32, "abias")
    P.dma("sp", ab[:, :], k.d["abias"][:, :])
    sk = S([128, 8], F32, "sk")
    P.act(sk[:, :], k.ppt[:, PPI["sk"]:PPI["sk"] + 8], AF.Exp)
    oneslo = S([128, 128], BF16, "oneslo")
    oneshi = S([128, 128], BF16, "oneshi")
    rowlo = S([128, 1], F32, "rowlo")
    rowhi = S([128, 1], F32, "rowhi")
    for t_, lo in ((oneslo, True), (oneshi, False)):
        P.memset("pool", t_[:, :], 0.0)
        P.memset("pool", t_[:, 0:64] if lo else t_[:, 64:128], 1.0)
    P.memset("pool", rowlo[:, :], 0.0)
    P.memset("pool", rowhi[:, :], 0.0)
    P.memset("pool", rowlo[0:64, :], 1.0)
    P.memset("pool", rowhi[64:128, :], 1.0)
    NR = 3
    klo = [S([128, 128], BF16, f"klo{i}") for i in range(NR)]
    khi = [S([128, 128], BF16, f"khi{i}") for i in range(NR)]
    vlo = [S([128, 128], BF16, f"vlo{i}") for i in range(NR)]
    vhi = [S([128, 128], BF16, f"vhi{i}") for i in range(NR)]
    for i in range(NR):
        P.memset("pool", vlo[i][:, :], 0.0)
        P.memset("pool", vhi[i][:, :], 0.0)
    sbuf = [S([128, 512], F32, f"sb{i}") for i in range(2)]
    pb = [S([128, 512], BF16, f"pb{i}") for i in range(4)]
    dn = [S([128, 256], F32, f"dn{i}") for i in range(2)]
    it = 0
    vi = 0

    def variants(h, kb, slot):
        kcols = slice(kb * 128, (kb + 1) * 128)
        P.ts("pool", klo[slot][:, :], k.khat[:, h, kcols], rowlo[:, 0:1], ALU.mult)
        P.ts("pool", khi[slot][:, :], k.khat[:, h, kcols], rowhi[:, 0:1], ALU.mult)
        P.cp("pool", vlo[slot][:, 0:64], k.vtok[:, kb, h * 64:(h + 1) * 64])
        P.cp("pool", vhi[slot][:, 64:128], k.vtok[:, kb, h * 64:(h + 1) * 64])

    for h in range(4):
        variants(h, 0, vi % NR)
        prev_slot = vi % NR
        vi += 1
        for n in range(9):
            qcols = slice(n * 128, (n + 1) * 128)
            cur_slot = vi % NR
            vi += 1
            variants(h, n + 1, cur_slot)
            ps = []
            for which, slot in ((0, prev_slot), (1, cur_slot)):
                bs = P.bank()
                for g in range(4):
                    P.mm(bs, bs.t[:, g * 128:(g + 1) * 128], (klo if g % 2 == 0 else khi)[slot][:, :],
                         k.qhat[:, 2 * h + g // 2, qcols])
                s_ = sbuf[it % 2]
                p_ = pb[it % 4]
                it += 1
                o = (h * 2 + which) * 512
                P.stt(s_[:, :], bs.t[:, :], 0.125, ab[:, o:o + 512], ALU.mult, ALU.add)
                P.act(p_[:, :], s_[:, :], AF.Exp)
                if n == 1 and which == 0:
                    P.ts("pool", p_[:, :], p_[:, :], pcol(k, "prevmask"), ALU.mult)
                ps.append((p_, slot))
            bo = P.bank()
            bd = P.bank()
            for g2 in range(2):
                for (p_, slot) in ps:
                    for par in range(2):
                        g = 2 * g2 + par
                        P.mm(bo, bo.t[:, g2 * 128:(g2 + 1) * 128], (vlo if par == 0 else vhi)[slot][:, :],
                             p_[:, g * 128:(g + 1) * 128])
                        P.mm(bd, bd.t[:, g2 * 128:(g2 + 1) * 128], (oneslo if par == 0 else oneshi)[:, :],
                             p_[:, g * 128:(g + 1) * 128])
            d_ = dn[(n * 4 + h) % 2]
            for g2 in range(2):
                P.ts("dve", d_[:, g2 * 128:(g2 + 1) * 128], bd.t[:, g2 * 128:(g2 + 1) * 128],
                     sk[:, 2 * h + g2:2 * h + g2 + 1], ALU.add)
            d_ap = d_[:, :].ap
            P.op("dve", lambda e, d_ap=d_ap: e.reciprocal(out=d_ap, in_=d_ap), reads=[d_[:, :]], writes=[d_[:, :]])
            for g2 in range(2):
                P.tt("dve", k.o_att[:, 2 * h + g2, qcols], bo.t[:, g2 * 128:(g2 + 1) * 128],
                     d_[:, g2 * 128:(g2 + 1) * 128], ALU.mult)
            prev_slot = cur_slot
    if k.dbg == "B":
        dbg_out(k, "oatt", k.o_att[:, :, :], [128, 8, NE], BF16)
        dbg_out(k, "qhat", k.qhat[:, :, :], [128, 8, NE], BF16)
        dbg_out(k, "hE", k.hE[:, :, :], [128, NK, NE1], BF16)


def phase_c1(k, ph):
    P = k.P
    S = lambda shape, dt, name: P.sb(shape, dt, name, stack=ph)
    wb = [S([128, NK, 256], BF16, f"wb{i}") for i in range(2)]
    wgr = [S([128, NK, 256], BF16, f"wgr{i}") for i in range(2)]
    wga = [S([128, NK, 256], BF16, f"wga{i}") for i in range(2)]
    grw = [S([128, NT3], F32, f"grw{i}") for i in range(2)]
    gat = [S([128, NT3], F32, f"gat{i}") for i in range(2)]
    t1 = [S([128, NT3], F32, f"t1c{i}") for i in range(2)]
    t2 = [S([128, NT3], F32, f"t2c{i}") for i in range(2)]

    def load(g):
        i = g % 2
        load_w(k, wb[i][:, :, :], k.w_branch, 0, NK, g * 256, 256)
        load_w(k, wgr[i][:, :, :], k.w_in, 0, NK, C_GRW + g * 256, 256)
        load_w(k, wga[i][:, :, :], k.w_in, 0, NK, C_GAT + g * 256, 256)

    load(0)
    it = 0
    for g in range(8):
        if g + 1 < 8:
            load(g + 1)
        i = g % 2
        for dl in range(2):
            dm = 2 * g + dl
            wc = slice(dl * 128, (dl + 1) * 128)
            for tt in range(3):
                c0 = HC0 + NT3 * tt
                ec = slice(c0, c0 + NT3)
                hc = slice(c0 + 128, c0 + 128 + NT3)
                mc = slice(NT3 * tt, NT3 * (tt + 1))
                b1 = P.bank()
                for kc in range(NK):
                    P.mm(b1, b1.t[:, 0:NT3], wgr[i][:, kc, wc], k.hE[:, kc, hc])
                b2 = P.bank()
                for kc in range(NK):
                    P.mm(b2, b2.t[:, 0:NT3], wga[i][:, kc, wc], k.hE[:, kc, hc])
                b3 = P.bank()
                for cc in range(8):
                    P.mm(b3, b3.t[:, 0:NT3], wb[i][:, cc, wc], k.o_rw[:, cc, ec])
                b4 = P.bank()
                for cc in range(8):
                    P.mm(b4, b4.t[:, 0:NT3], wb[i][:, 8 + cc, wc], k.o_att[:, cc, ec])
                j = it % 2
                it += 1
                P.act(grw[j][:, :], b1.t[:, 0:NT3], AF.Sigmoid)
                P.act(gat[j][:, :], b2.t[:, 0:NT3], AF.Sigmoid)
                P.tt("dve", t1[j][:, :], b3.t[:, 0:NT3], grw[j][:, :], ALU.mult)
                P.tt("dve", t2[j][:, :], b4.t[:, 0:NT3], gat[j][:, :], ALU.mult)
                P.tt("pool", k.mT[:, dm, mc], t1[j][:, :], t2[j][:, :], ALU.add)
    if k.dbg == "C1":
        dbg_out(k, "mT", k.mT[:, :, :], [128, NK, 3 * NT3], BF16)


def phase_c2(k, ph):
    P = k.P
    S = lambda shape, dt, name: P.sb(shape, dt, name, stack=ph)
    wo = [S([128, NK, 256], BF16, f"wo{i}") for i in range(2)]
    xr = [S([128, 3 * NT3], F32, f"xr{i}") for i in range(2)]
    load_w(k, wo[0][:, :, :], k.w_out, 0, NK, 0, 256)
    xsrc = k.xT.ap().rearrange("(k p) t -> p k t", p=128)
    for g in range(8):
        if g + 1 < 8:
            load_w(k, wo[(g + 1) % 2][:, :, :], k.w_out, 0, NK, (g + 1) * 256, 256)
        for dl in range(2):
            dm2 = 2 * g + dl
            xx = xr[dm2 % 2]
            P.dma("sp", xx[:, :], View(k.d["xT"], xsrc[:, dm2, E0 + HC0:SEQ]))
            for tt in range(3):
                mc = slice(NT3 * tt, NT3 * (tt + 1))
                bk = P.bank()
                for dm in range(NK):
                    P.mm(bk, bk.t[:, 0:NT3], wo[g % 2][:, dm, dl * 128:(dl + 1) * 128], k.mT[:, dm, mc])
                P.tt("dve", k.x1T[:, dm2, mc], bk.t[:, 0:NT3], xx[:, mc], ALU.add)
    if k.dbg == "C2":
        dbg_out(k, "x1T", k.x1T[:, :, :], [128, NK, 3 * NT3])


def phase_d(k, ph):
    P = k.P
    S = lambda shape, dt, name: P.sb(shape, dt, name, stack=ph)
    NC = 3 * NT3
    h2T = S([128, NK, NC], BF16, "h2T")
    sqr = [S([128, NT3], BF16, f"sqrD{i}") for i in range(2)]
    rstd = S([128, NC], F32, "rstdD")
    tmpn = S([128, NT3], F32, "tmpnD")
    for tt in range(3):
        mc = slice(NT3 * tt, NT3 * (tt + 1))
        bk = P.bank()
        for kc in range(NK):
            sq = sqr[kc % 2]
            P.act(sq[:, :], k.x1T[:, kc, mc], AF.Square)
            P.mm(bk, bk.t[:, 0:NT3], k.onesb[:, :], sq[:, :])
        rsqrt_act(P, rstd[:, mc], bk.t[:, 0:NT3], 1.0 / D, 1e-6, tmpn[:, :])
    for kc in range(NK):
        P.stt(h2T[:, kc, :], k.x1T[:, kc, :], pcol(k, "n2g", kc), rstd[:, :], ALU.mult, ALU.mult)
    wv = [S([128, NK, 256], BF16, f"wv{i}") for i in range(2)]
    wg = [S([128, NK, 256], BF16, f"wg{i}") for i in range(2)]
    wd = [S([128, 2, D], BF16, f"wd{i}") for i in range(2)]
    aT = [S([128, 2, 1024], BF16, f"aT{i}") for i in range(2)]
    u = [S([128, NC], F32, f"u{i}") for i in range(2)]
    cv = S([128, 1024], F32, "cv")
    cg = S([128, 1024], F32, "cg")
    sg = S([128, 1024], F32, "sgD")
    NFG = NFC // 2
    wdsrc = k.w_down.ap().rearrange("(f p) d -> p f d", p=128)

    def load(fg):
        i = fg % 2
        load_w(k, wv[i][:, :, :], k.w_up, 0, NK, fg * 256, 256)
        load_w(k, wg[i][:, :, :], k.w_up, 0, NK, DFF + fg * 256, 256)
        P.dma("pool", wd[i][:, :, :], View(k.d["w_down"], wdsrc[:, 2 * fg:2 * fg + 2, :]))

    nfg = int(os.environ.get("KNFG", str(NFG))) if k.dbg else NFG
    ev = [0]

    def up(fg):
        i = fg % 2
        for fc in range(2):
            wc = slice(fc * 128, (fc + 1) * 128)
            for kind, W, ctile in ((0, wv[i], cv), (1, wg[i], cg)):
                ch = (0 if kind == 0 else NFC) + 2 * fg + fc
                uu = u[kind]
                for tt in range(3):
                    mc = slice(NT3 * tt, NT3 * (tt + 1))
                    bk = P.bank()
                    for kc in range(NK):
                        P.mm(bk, bk.t[:, 0:NT3], W[:, kc, wc], h2T[:, kc, mc])
                    ev[0] += 1
                    P.cp("act" if ev[0] % 3 else "dve", uu[:, mc], bk.t[:, 0:NT3])
                P.ts("pool", uu[:, 0:2], uu[:, 0:2], pcol(k, "convmask"), ALU.mult)
                P.act(ctile[:, :], uu[:, 2:NC], AF.Identity, bias=pcol(k, "cb", ch), scale=pcol(k, "cw2", ch))
                P.stt(ctile[:, :], uu[:, 1:NC - 1], pcol(k, "cw1", ch), ctile[:, :], ALU.mult, ALU.add)
                P.stt(ctile[:, :], uu[:, 0:NC - 2], pcol(k, "cw0", ch), ctile[:, :], ALU.mult, ALU.add)
            P.act(sg[:, :], cg[:, :], AF.Silu)
            P.tt("pool", aT[i][:, fc, :], sg[:, :], cv[:, :], ALU.mult)

    def down(fg):
        i = fg % 2
        for dm in range(NK):
            for t2 in range(2):
                bk = P.bank()
                for fc in range(2):
                    P.mm(bk, bk.t[:, 0:512], wd[i][:, fc, dm * 128:(dm + 1) * 128], aT[i][:, fc, t2 * 512:(t2 + 1) * 512])
                oc = slice(2 + t2 * 512, 2 + (t2 + 1) * 512)
                P.tt("dve", k.x1T[:, dm, oc], bk.t[:, 0:512], k.x1T[:, dm, oc], ALU.add)

    load(0)
    if nfg > 1:
        load(1)
    up(0)
    for fg in range(nfg):
        if fg + 1 < nfg:
            up(fg + 1)
        down(fg)
        if fg + 2 < nfg:
            load(fg + 2)
    dst = k.outT.ap().rearrange("(k p) t -> p k t", p=128)
    for half in range(2):
        P.dma("sp", View(k.d["outT"], dst[:, 8 * half:8 * half + 8, :]), k.x1T[:, 8 * half:8 * half + 8, 2:NC])
    k.final_views.append(k.d["outT"][:, :])


def finish(k):
    P = k.P
    if k.final_views:
        P.wait_all("sp", k.final_views)
    P.emit()


def make_inputs(inp):
    cs, ab = host_consts()
    x = np.asarray(inp["x"], np.float32)
    shared = {
        "w_in": np.ascontiguousarray(np.asarray(inp["w_in"][0], np.float32)),
        "w_branch": np.ascontiguousarray(np.asarray(inp["w_branch"][0], np.float32)),
        "w_out": np.ascontiguousarray(np.asarray(inp["w_out"][0], np.float32)),
        "w_up": np.ascontiguousarray(np.asarray(inp["w_up"][0], np.float32)),
        "w_down": np.ascontiguousarray(np.asarray(inp["w_down"][0], np.float32)),
        "rw_w2": np.ascontiguousarray(np.asarray(inp["rw_w2"][0], np.float32)),
        "rw_a2": np.ascontiguousarray(np.asarray(inp["rw_a2"][0], np.float32)),
        "rw_g2": np.ascontiguousarray(np.asarray(inp["rw_g2"][0], np.float32)),
        "cst": cs, "abias": ab,
    }
    if DBG in ("A0", "A", "IO"):
        for n in ("w_branch", "w_out", "w_up", "w_down"):
            shared[n] = np.zeros((128, 128), np.float32)
    maps = []
    for c in range(8):
        b, j = c // 4, c % 4
        n = 1024 * (j + 1)
        xT = np.zeros((D, SEQ), np.float32)
        xT[:, SEQ - n:] = x[b, :n].T
        m = dict(shared)
        m["xT"] = xT
        m["pp"] = host_params(inp, j)
        maps.append(m)
    return maps


def kernel(**inp):
    nc, k = build_program(DBG)
    maps = make_inputs(inp)
    res = run_bass_kernel_spmd(nc, maps, core_ids=list(range(8)))
    out = np.zeros((2, SEQ, D), np.float32)
    for c in range(8):
        b, j = c // 4, c % 4
        out[b, 1024 * j:1024 * (j + 1), :] = np.asarray(res.results[c]["outT"]).T
    kernel.last = (res, k)
    return out
```
